# Optimizing a Trainium2 kernel written in Bass

```python
import jax, jax.numpy as jnp
from jax import lax
import numpy as np

D_MODEL = 1024
BATCH = 2
SEQ = 16384
DEPTH = 2

HG_HEADS = 8
HG_KEY_DIM = 128
HG_VAL_DIM = D_MODEL // HG_HEADS
HG_KEY = HG_HEADS * HG_KEY_DIM
HG_VAL = HG_HEADS * HG_VAL_DIM
HG_CHUNK = 64
SG_GROUPS = 8
SG_GROUP_DIM = 64
SG_WIDTH = SG_GROUPS * SG_GROUP_DIM
SG_CHUNK = 128
FFN_HIDDEN = ((8 * D_MODEL // 3 + 255) // 256) * 256
PLE_DIM = 256
EPS = 1e-6
IN_SPLITS = (HG_KEY, HG_KEY, HG_KEY, HG_VAL, HG_VAL, SG_WIDTH, SG_WIDTH, D_MODEL, D_MODEL)
N_IN = HG_KEY * 3 + HG_VAL * 2 + SG_WIDTH * 2 + D_MODEL * 2

kernel_name = "hgrn2_gmlp_gated_hybrid_encoder"


def rms_norm(x, g):
    xf = x.astype(jnp.float32)
    y = xf * lax.rsqrt(jnp.mean(xf * xf, axis=-1, keepdims=True) + EPS)
    return (y * g.astype(jnp.float32)).astype(x.dtype)


def layer_norm(x, g, b):
    xf = x.astype(jnp.float32)
    mu = jnp.mean(xf, axis=-1, keepdims=True)
    xc = xf - mu
    y = xc * lax.rsqrt(jnp.mean(xc * xc, axis=-1, keepdims=True) + EPS)
    return (y * g.astype(jnp.float32) + b.astype(jnp.float32)).astype(x.dtype)


def layer_lower_bounds(gamma):
    sm = jax.nn.softmax(gamma.astype(jnp.float32), axis=0)
    return jnp.cumsum(sm, axis=0) - sm[0:1]


def _to_chunks(t):
    b, s, h, d = t.shape
    return t.reshape(b, s // HG_CHUNK, HG_CHUNK, h, d).transpose(1, 0, 3, 2, 4)


def hgrn2_direction(q, k, v, logf):
    bsz, s, h, dk = q.shape
    dv = v.shape[-1]
    mask = jnp.tril(jnp.ones((HG_CHUNK, HG_CHUNK), dtype=jnp.float32))

    def step(state, inp):
        qc, kc, vc, gc = inp
        b = jnp.cumsum(gc, axis=2)
        o_inter = jnp.einsum('bhtk,bhkv->bhtv', qc * jnp.exp(b), state)
        diff = b[:, :, :, None, :] - b[:, :, None, :, :]
        decay = jnp.exp(jnp.minimum(diff, 0.0)) * mask[:, :, None]
        scores = jnp.einsum('bhtk,bhsk,bhtsk->bhts', qc, kc, decay)
        o_intra = jnp.einsum('bhts,bhsv->bhtv', scores, vc)
        b_last = b[:, :, -1:, :]
        new_state = (jnp.exp(b_last[:, :, 0, :])[..., None] * state
                     + jnp.einsum('bhsk,bhsv->bhkv', kc * jnp.exp(b_last - b), vc))
        return new_state, o_inter + o_intra

    init = jnp.zeros((bsz, h, dk, dv), jnp.float32)
    _, o = lax.scan(step, init, (_to_chunks(q), _to_chunks(k), _to_chunks(v), _to_chunks(logf)))
    return o.transpose(1, 0, 3, 2, 4).reshape(bsz, s, h, dv)


def hgrn2_mixer(zq, zf_fwd, zf_bwd, zi, zg, lb_f, lb_b, norm_g):
    bsz, s, _ = zq.shape
    f32 = jnp.float32
    tiny = jnp.finfo(f32).tiny
    q = jax.nn.silu(zq.astype(f32)).reshape(bsz, s, HG_HEADS, HG_KEY_DIM)
    v = zi.astype(f32).reshape(bsz, s, HG_HEADS, HG_VAL_DIM)

    def gates(zf, lb):
        zf = zf.astype(f32)
        f = lb + (1.0 - lb) * jax.nn.sigmoid(zf)
        logf = jnp.log(jnp.maximum(f, tiny))
        k = (1.0 - lb) * jax.nn.sigmoid(-zf)
        return (k.reshape(bsz, s, HG_HEADS, HG_KEY_DIM), logf.reshape(bsz, s, HG_HEADS, HG_KEY_DIM))

    k_f, logf_f = gates(zf_fwd, lb_f)
    k_b, logf_b = gates(zf_bwd, lb_b)
    o_fwd = hgrn2_direction(q, k_f, v, logf_f)
    o_bwd = hgrn2_direction(q[:, ::-1], k_b[:, ::-1], v[:, ::-1], logf_b[:, ::-1])[:, ::-1]
    o = o_fwd + o_bwd
    o = rms_norm(o, norm_g.reshape(HG_HEADS, HG_VAL_DIM)).reshape(bsz, s, HG_VAL)
    return (o * jax.nn.silu(zg.astype(f32))).astype(zq.dtype)


def spatial_gating(zu, zv, w_s, b_s, ln_g, ln_b):
    bsz, s, _ = zu.shape
    u = jax.nn.gelu(zu, approximate=False)
    v = layer_norm(jax.nn.gelu(zv, approximate=False), ln_g, ln_b)
    vr = v.reshape(bsz, s // SG_CHUNK, SG_CHUNK, SG_GROUPS, SG_GROUP_DIM)
    sg = jnp.einsum('gts,bcsge->bctge', w_s, vr) + b_s.T[None, None, :, :, None]
    return u * sg.reshape(bsz, s, SG_WIDTH)


def setup_inputs(seed: int = 0) -> dict:
    key = jax.random.key(seed)
    ks = jax.random.split(key, 24)
    f32 = jnp.float32

    def nrm(k, shape, scale):
        return jax.random.normal(k, shape, f32) * scale

    def gain(k, shape):
        return 1.0 + 0.05 * jax.random.normal(k, shape, f32)

    return {
        "x": nrm(ks[0], (BATCH, SEQ, D_MODEL), 1.0),
        "p": nrm(ks[1], (DEPTH, BATCH, SEQ, PLE_DIM), 1.0),
        "norm_mix_pre": gain(ks[2], (DEPTH, D_MODEL)),
        "w_in": nrm(ks[3], (DEPTH, D_MODEL, N_IN), D_MODEL ** -0.5),
        "lb_gamma_fwd": nrm(ks[4], (DEPTH, HG_KEY), 0.1),
        "lb_gamma_bwd": nrm(ks[5], (DEPTH, HG_KEY), 0.1),
        "hg_norm": gain(ks[6], (DEPTH, HG_VAL)),
        "sg_w": nrm(ks[7], (DEPTH, SG_GROUPS, SG_CHUNK, SG_CHUNK), SG_CHUNK ** -0.5),
        "sg_b": gain(ks[8], (DEPTH, SG_GROUPS, SG_CHUNK)),
        "sg_ln_g": gain(ks[9], (DEPTH, SG_WIDTH)),
        "sg_ln_b": nrm(ks[10], (DEPTH, SG_WIDTH), 0.02),
        "w_a": nrm(ks[11], (DEPTH, HG_VAL, D_MODEL), HG_VAL ** -0.5),
        "w_b": nrm(ks[12], (DEPTH, SG_WIDTH, D_MODEL), SG_WIDTH ** -0.5),
        "w_out": nrm(ks[13], (DEPTH, D_MODEL, D_MODEL), D_MODEL ** -0.5),
        "norm_mix_post": gain(ks[14], (DEPTH, D_MODEL)),
        "norm_ffn_pre": gain(ks[15], (DEPTH, D_MODEL)),
        "w_gate": nrm(ks[16], (DEPTH, D_MODEL, FFN_HIDDEN), D_MODEL ** -0.5),
        "w_up": nrm(ks[17], (DEPTH, D_MODEL, FFN_HIDDEN), D_MODEL ** -0.5),
        "w_down": nrm(ks[18], (DEPTH, FFN_HIDDEN, D_MODEL), FFN_HIDDEN ** -0.5),
        "norm_ffn_post": gain(ks[19], (DEPTH, D_MODEL)),
        "w_ple": nrm(ks[20], (DEPTH, PLE_DIM, D_MODEL), PLE_DIM ** -0.5),
        "w_ple_gate": nrm(ks[21], (DEPTH, D_MODEL, D_MODEL), D_MODEL ** -0.5),
    }


def reference(x, p, norm_mix_pre, w_in, lb_gamma_fwd, lb_gamma_bwd, hg_norm, sg_w, sg_b,
              sg_ln_g, sg_ln_b, w_a, w_b, w_out, norm_mix_post, norm_ffn_pre, w_gate, w_up,
              w_down, norm_ffn_post, w_ple, w_ple_gate):
    lb_fwd_all = layer_lower_bounds(lb_gamma_fwd)
    lb_bwd_all = layer_lower_bounds(lb_gamma_bwd)
    offsets = [int(o) for o in np.cumsum(IN_SPLITS)[:-1]]
    for l in range(DEPTH):
        h = rms_norm(x, norm_mix_pre[l])
        z = jnp.einsum('bsd,dn->bsn', h, w_in[l])
        zq, zf_f, zf_b, zi, zg, zu, zv, ga, gb = jnp.split(z, offsets, axis=-1)
        a_out = hgrn2_mixer(zq, zf_f, zf_b, zi, zg, lb_fwd_all[l], lb_bwd_all[l], hg_norm[l])
        b_out = spatial_gating(zu, zv, sg_w[l], sg_b[l], sg_ln_g[l], sg_ln_b[l])
        merged = (jax.nn.sigmoid(ga) * jnp.einsum('bsv,vd->bsd', a_out, w_a[l])
                  + jax.nn.sigmoid(gb) * jnp.einsum('bsw,wd->bsd', b_out, w_b[l]))
        mix = jnp.einsum('bsd,de->bse', merged, w_out[l])
        x = x + rms_norm(mix, norm_mix_post[l])
        h2 = rms_norm(x, norm_ffn_pre[l])
        ff = jnp.einsum('bsf,fd->bsd',
                        jax.nn.silu(jnp.einsum('bsd,df->bsf', h2, w_gate[l]))
                        * jnp.einsum('bsd,df->bsf', h2, w_up[l]), w_down[l])
        x = x + rms_norm(ff, norm_ffn_post[l])
        x = x + (jnp.einsum('bse,ed->bsd', p[l], w_ple[l])
                 * jax.nn.sigmoid(jnp.einsum('bsd,de->bse', x, w_ple_gate[l])))
    return x
```

```python
import contextlib
import numpy as np
import concourse.bass as bass
import concourse.mybir as mybir
from concourse.bass_utils import run_bass_kernel_spmd

F32 = mybir.dt.float32
BF16 = mybir.dt.bfloat16
AF = mybir.ActivationFunctionType
ALU = mybir.AluOpType
AX = mybir.AxisListType

ENGINES = ("pe", "act", "dve", "pool", "sp")
N_DSEM = 40

D = 1024
NIN = 8192
FH = 2816
FC = FH // 128
EPS = 1e-6
OQ, OFF, OFB, OI, OG, OU, OV, OGA, OGB = 0, 1024, 2048, 3072, 4096, 5120, 5632, 6144, 7168


class Buf:
    __slots__ = ("name", "w", "r")

    def __init__(self, name):
        self.name = name
        self.w = None
        self.r = []


class Op:
    __slots__ = ("eng", "fn", "deps", "sig", "sigval", "dma", "key", "flushed", "sem")

    def __init__(self, eng, fn, dma, key):
        self.eng = eng
        self.fn = fn
        self.deps = []
        self.sig = False
        self.sigval = 0
        self.dma = dma
        self.key = key
        self.flushed = False


def _flat(xs):
    out = []
    for x in xs:
        if isinstance(x, (list, tuple)):
            out.extend(_flat(x))
        elif x is not None:
            out.append(x)
    return out


class Prog:
    def __init__(self, nc, st):
        self.nc = nc
        self.ops = {e: [] for e in ENGINES}
        self.nbuf = 0
        self.sweep = 0
        self.esem = [{e: st.enter_context(nc.semaphore(f"s{k}_{e}")) for e in ENGINES} for k in range(2)]
        self.dsem = [[st.enter_context(nc.semaphore(f"d{k}_{i}")) for i in range(N_DSEM)] for k in range(2)]
        self.ninst = 0

    def buf(self, name=None):
        self.nbuf += 1
        return Buf(name or f"b{self.nbuf}")

    def bufs(self, n, name="b"):
        return [self.buf(f"{name}{i}") for i in range(n)]

    def op(self, eng, fn, reads=(), writes=(), dma=False):
        reads = _flat(reads)
        writes = _flat(writes)
        key = None
        if dma:
            key = writes[0]
        o = Op(eng, fn, dma, key)
        deps = []
        for b in reads:
            if b.w is not None:
                deps.append(b.w)
        for b in writes:
            if b.w is not None:
                deps.append(b.w)
            deps.extend(b.r)
        for b in reads:
            b.r.append(o)
        for b in writes:
            b.w = o
            b.r = []
        seen = set()
        for d in deps:
            if d is o or id(d) in seen or d.flushed:
                continue
            seen.add(id(d))
            if d.eng == "pe" and eng == "pe" and not d.dma:
                continue
            o.deps.append(d)
            d.sig = True
        self.ops[eng].append(o)
        return o

    def pe(self, fn, reads=(), writes=()):
        return self.op("pe", fn, reads, writes)

    def act(self, fn, reads=(), writes=()):
        return self.op("act", fn, reads, writes)

    def dve(self, fn, reads=(), writes=()):
        return self.op("dve", fn, reads, writes)

    def pool(self, fn, reads=(), writes=()):
        return self.op("pool", fn, reads, writes)

    def dma(self, fn, reads=(), writes=()):
        return self.op("sp", fn, reads, writes, dma=True)

    def flush(self):
        nc = self.nc
        k = self.sweep % 2
        esem = self.esem[k]
        dpool = self.dsem[k]
        other_e = self.esem[1 - k]
        other_d = self.dsem[1 - k]
        dma_keys = {}
        nsem = [0]
        allsems = []
        for e in ENGINES:
            cnt = 0
            for o in self.ops[e]:
                if o.dma:
                    kk = id(o.key)
                    if kk not in dma_keys or dma_keys[kk][1] + 16 > 224:
                        assert nsem[0] < N_DSEM, "too many DMA semaphores in one sweep"
                        dma_keys[kk] = [dpool[nsem[0]], 0]
                        allsems.append(dma_keys[kk])
                        nsem[0] += 1
                    dma_keys[kk][1] += 16
                    o.sigval = dma_keys[kk][1]
                    o.sem = dma_keys[kk][0]
                elif o.sig:
                    cnt += 1
                    o.sigval = cnt
        ops = self.ops
        first = self.sweep == 0

        def body(ename):
            def f(eng):
                waited = {}

                def wait_for(d):
                    s = d.sem if d.dma else esem[d.eng]
                    if waited.get(id(s), 0) >= d.sigval:
                        return
                    waited[id(s)] = d.sigval
                    eng.wait_ge(s, d.sigval)

                if ename == "sp" and not first:
                    for s in list(other_e.values()) + list(other_d):
                        eng.sem_clear(s)
                for o in ops[ename]:
                    for d in o.deps:
                        wait_for(d)
                    ins = o.fn(eng)
                    self.ninst += 1
                    if o.dma:
                        ins.then_inc(o.sem, 16)
                    elif o.sig:
                        ins.then_inc(esem[ename], 1)
                if ename == "sp":
                    for s, tot in allsems:
                        if tot > 0:
                            eng.wait_ge(s, tot)
            return f

        with nc.allow_low_precision(reason="bf16 matmul operands, fp32 accumulation"), nc.Block() as block:
            block.tensor(body("pe"))
            block.scalar(body("act"))
            block.vector(body("dve"))
            block.gpsimd(body("pool"))
            block.sync(body("sp"))
        for e in ENGINES:
            for o in self.ops[e]:
                o.flushed = True
                o.fn = None
        self.ops = {e: [] for e in ENGINES}
        self.sweep += 1


def MM(out, lhsT, rhs, start=True, stop=True):
    return lambda e: e.matmul(out, lhsT=lhsT, rhs=rhs, start=start, stop=stop)


def TR(out, in_, ident):
    return lambda e: e.transpose(out, in_, ident)


def ACT(out, in_, func, bias=None, scale=None, accum=None):
    kw = {}
    if bias is not None:
        kw["bias"] = bias
    if scale is not None:
        kw["scale"] = scale
    if accum is not None:
        kw["accum_out"] = accum
    return lambda e: e.activation(out, in_, func, **kw)


def CP(out, in_):
    return lambda e: e.tensor_copy(out, in_)


def ACP(out, in_):
    return lambda e: e.copy(out, in_)


def TT(out, a, b, op):
    return lambda e: e.tensor_tensor(out=out, in0=a, in1=b, op=op)


def TS(out, a, s1, s2, op0, op1):
    return lambda e: e.tensor_scalar(out=out, in0=a, scalar1=s1, scalar2=s2, op0=op0, op1=op1)


def STT(out, a, s, b, op0, op1):
    return lambda e: e.scalar_tensor_tensor(out=out, in0=a, scalar=s, in1=b, op0=op0, op1=op1)


def RECIP(out, in_):
    return lambda e: e.reciprocal(out, in_)


def MSET(ap, v):
    return lambda e: e.memset(ap, v)


def DMA(out, in_, slow=False):
    if slow:
        return lambda e: e.dma_start(out=out, in_=in_, allow_slow_non_contiguous=True)
    return lambda e: e.dma_start(out=out, in_=in_)


_UID = {"n": 0}


def uniq(name):
    _UID["n"] += 1
    return f"{name}_{_UID['n']}"


class Cfg:
    def __init__(self, NT, out_lo, out_n, debug=False, stop_after=None):
        self.stop_after = stop_after
        self.NT = NT
        self.out_lo = out_lo
        self.out_n = out_n
        self.debug = debug


def build(cfg):
    NT = cfg.NT
    NS = NT // 512
    nc = bass.Bass("TRN2", target_bir_lowering=False)

    def din(name, shape):
        return nc.dram_tensor(name, shape, F32, kind="ExternalInput").ap()

    x_in = din("x", [NT, D])
    p_in = din("p", [2, NT, 256])
    W = {}
    for nm, shp in [("norm_mix_pre", [2, D]), ("w_in", [2, D, NIN]), ("lb_gamma_fwd", [2, D]),
                    ("lb_gamma_bwd", [2, D]), ("hg_norm", [2, D]), ("sg_w", [2, 8, 128, 128]),
                    ("sg_b", [2, 8, 128]), ("sg_ln_g", [2, 512]), ("sg_ln_b", [2, 512]),
                    ("w_a", [2, D, D]), ("w_b", [2, 512, D]), ("w_out", [2, D, D]),
                    ("norm_mix_post", [2, D]), ("norm_ffn_pre", [2, D]), ("w_gate", [2, D, FH]),
                    ("w_up", [2, D, FH]), ("w_down", [2, FH, D]), ("norm_ffn_post", [2, D]),
                    ("w_ple", [2, 256, D]), ("w_ple_gate", [2, D, D])]:
        W[nm] = din(nm, shp)
    out = nc.dram_tensor("out", [cfg.out_n, D], F32, kind="ExternalOutput").ap()

    okind = "ExternalOutput" if cfg.debug else "Internal"

    def dscr(name, shape, dt):
        return nc.dram_tensor(name, shape, dt, kind=okind).ap()

    hT_st = dscr("hT_st", [NS, 128, 8 * 512], BF16)
    qT_st = dscr("qT_st", [NS, 128, 8 * 512], BF16)
    v_st = dscr("v_st", [NS, 128, 4 * 1024], BF16)
    of_st = dscr("of_st", [NS, 128, 4 * 1024], F32)
    mAT_st = dscr("mAT_st", [NS, 128, 8 * 512], BF16)
    xmid = dscr("xmid", [NT, D], F32)
    xa = dscr("xa", [NT, D], F32)
    x1 = dscr("x1", [NT, D], F32)

    with contextlib.ExitStack() as gst:
        p = Prog(nc, gst)

        def GT(name, shape, dt):
            return gst.enter_context(nc.sbuf_tensor(uniq(name), shape, dt))

        identf = GT("identf", [128, 128], F32)
        identb = GT("identb", [128, 128], BF16)
        scanm = GT("scanm", [128, 512], F32)
        maskf = GT("maskf", [128, 8, 64], F32)
        maskb = GT("maskb", [128, 8, 64], F32)
        ones1 = GT("ones1", [128, 1], F32)
        mhalf = GT("mhalf", [128, 8], F32)
        lbt = GT("lbt", [128, 2, 2, 8], F32)
        omlt = GT("omlt", [128, 2, 2, 8], F32)
        nomlt = GT("nomlt", [128, 2, 2, 8], F32)
        gam = GT("gam", [128, 2, 2, 8], F32)
        cB = p.buf("consts")

        p.pool(MSET(identf[:], 0.0), writes=[cB])
        p.pool(lambda e: e.affine_select(out=identf[:], in_=identf[:], pattern=[[-1, 128]],
                                         compare_op=ALU.not_equal, fill=1.0, base=0,
                                         channel_multiplier=1), reads=[cB], writes=[cB])
        p.dve(CP(identb[:], identf[:]), reads=[cB], writes=[cB])
        p.pool(MSET(scanm[:], 1.0), writes=[cB])
        p.pool(MSET(scanm[:].rearrange("p (c t) -> p c t", t=64)[:, :, 0:1], 0.0), writes=[cB])
        p.pool(MSET(ones1[:], 1.0), writes=[cB])
        p.pool(MSET(mhalf[:], -0.5), writes=[cB])
        p.pool(MSET(maskf[:], 1.0), writes=[cB])
        p.pool(MSET(maskb[:], 1.0), writes=[cB])
        for lo in (0, 64):
            p.pool(lambda e, lo=lo: e.affine_select(out=maskf[lo:lo + 64], in_=maskf[lo:lo + 64],
                                                    pattern=[[0, 8], [1, 64]], compare_op=ALU.is_ge,
                                                    fill=0.0, base=0, channel_multiplier=-1),
                   reads=[cB], writes=[cB])
            p.pool(lambda e, lo=lo: e.affine_select(out=maskb[lo:lo + 64], in_=maskb[lo:lo + 64],
                                                    pattern=[[0, 8], [-1, 64]], compare_op=ALU.is_ge,
                                                    fill=0.0, base=0, channel_multiplier=1),
                   reads=[cB], writes=[cB])
        gB = p.buf("gam")
        for di, nm in enumerate(("lb_gamma_fwd", "lb_gamma_bwd")):
            for l in range(2):
                p.dma(DMA(gam[:, di, l, :], W[nm][l].rearrange("(h p) -> p h", p=128), slow=True), writes=[gB])
        p.act(ACT(gam[:], gam[:], AF.Exp), reads=[gB], writes=[gB])
        for di in range(2):
            p.dve(TT(omlt[:, di, 0, :], gam[:, di, 0, :], gam[:, di, 1, :], ALU.add), reads=[gB], writes=[cB])
            p.dve(RECIP(omlt[:, di, 0, :], omlt[:, di, 0, :]), reads=[cB], writes=[cB])
            p.dve(TT(lbt[:, di, 1, :], gam[:, di, 1, :], omlt[:, di, 0, :], ALU.mult), reads=[gB, cB], writes=[cB])
            p.dve(MSET(lbt[:, di, 0, :], 0.0), reads=[cB], writes=[cB])
        p.dve(TS(omlt[:], lbt[:], -1.0, 1.0, ALU.mult, ALU.add), reads=[cB], writes=[cB])
        p.dve(TS(nomlt[:], omlt[:], -1.0, None, ALU.mult, ALU.bypass), reads=[cB], writes=[cB])
        p.flush()

        wl_state = {"i": 0}

        def load_rowscale(st, name, dram_vec, kc_n):
            t = st.enter_context(nc.sbuf_tensor(uniq(name), [128, kc_n], F32))
            b = p.buf(name)
            p.dma(DMA(t[:], dram_vec.rearrange("(c p) -> p c", p=128), slow=True), writes=[b])
            return t, b

        def load_bcast(st, name, dram_vec, n):
            t = st.enter_context(nc.sbuf_tensor(uniq(name), [128, n], F32))
            b = p.buf(name)
            p.dma(DMA(t[:], dram_vec.partition_broadcast(128)), writes=[b])
            return t, b

        def make_stage(st):
            stg = [st.enter_context(nc.sbuf_tensor(uniq(f"wstg{i}"), [128, 2048], F32)) for i in range(4)]
            return stg, p.bufs(4, "wstg")

        def load_w(stage, wd, kc_n, c0, ncols, dst, dc0, dstB, scale=None, scaleB=None):
            stg, sB = stage
            for kc in range(kc_n):
                for c in range(0, ncols, 2048):
                    n = min(2048, ncols - c)
                    i = wl_state["i"]
                    wl_state["i"] += 1
                    s = i % 4
                    p.dma(DMA(stg[s][:, :n], wd[kc * 128:(kc + 1) * 128, c0 + c:c0 + c + n]), writes=[sB[s]])
                    o_ap = dst[:, kc, dc0 + c:dc0 + c + n]
                    eng = ("dve", "act", "dve", "act", "pool")[i % 5]
                    if scale is None:
                        if eng == "act":
                            p.act(ACP(o_ap, stg[s][:, :n]), reads=[sB[s]], writes=[dstB])
                        else:
                            p.op(eng, CP(o_ap, stg[s][:, :n]), reads=[sB[s]], writes=[dstB])
                    else:
                        sc = scale[:, kc:kc + 1]
                        if eng == "act":
                            p.act(ACT(o_ap, stg[s][:, :n], AF.Copy, scale=sc), reads=[sB[s], scaleB], writes=[dstB])
                        else:
                            p.op(eng, TS(o_ap, stg[s][:, :n], sc, 0.0, ALU.mult, ALU.add),
                                 reads=[sB[s], scaleB], writes=[dstB])

        def proj_fm(ps, w, c0, inT, kc_n, tok0, ntok, rd, wr):
            for kc in range(kc_n):
                p.pe(MM(ps, w[:, kc, c0:c0 + 128], inT[:, kc, tok0:tok0 + ntok], kc == 0, kc == kc_n - 1),
                     reads=rd, writes=wr)

        def proj_tm(ps, inT, tok0, w, c0, ncols, kc_n, rd, wr):
            for kc in range(kc_n):
                p.pe(MM(ps, inT[:, kc, tok0:tok0 + 128], w[:, kc, c0:c0 + ncols], kc == 0, kc == kc_n - 1),
                     reads=rd, writes=wr)

        def sigmoid_from_psum(ps, psB, tmp, tmpB, out_ap, outB):
            p.act(ACT(tmp, ps, AF.Exp, scale=-1.0), reads=[psB], writes=[tmpB])
            p.act(ACT(tmp, tmp, AF.Ln, bias=1.0), reads=[tmpB], writes=[tmpB])
            p.act(ACT(out_ap, tmp, AF.Exp, scale=-1.0), reads=[tmpB], writes=[outB])

        def rms_rstd(ss, ssB, ncol, n, rstd, rstdB):
            p.dve(TS(ss, ss, 1.0 / n, EPS, ALU.mult, ALU.add), reads=[ssB], writes=[ssB])
            p.pool(TT(rstd, ss, mhalf[:, 0:ncol], ALU.pow), reads=[ssB, cB], writes=[rstdB])

        def norm_transpose(st_tiles, xt, xtB, ntile, hT, hTB, psT, psTB):
            junk, junkB, ss, ssB, rstd, rstdB, xs, xsB = st_tiles
            for t in range(ntile):
                p.act(ACT(junk[:], xt[:, t, :], AF.Square, accum=ss[:, t:t + 1]), reads=[xtB], writes=[junkB, ssB])
            rms_rstd(ss[:, 0:ntile], ssB, ntile, D, rstd[:, 0:ntile], rstdB)
            for t in range(ntile):
                if t % 2 == 0:
                    p.act(ACT(xs[:, t, :], xt[:, t, :], AF.Copy, scale=rstd[:, t:t + 1]), reads=[xtB, rstdB], writes=[xsB[t]])
                else:
                    p.dve(TS(xs[:, t, :], xt[:, t, :], rstd[:, t:t + 1], 0.0, ALU.mult, ALU.add),
                          reads=[xtB, rstdB], writes=[xsB[t]])
            for kc in range(8):
                s = kc % 2
                for t in range(ntile):
                    p.pe(TR(psT[s][:, t * 128:(t + 1) * 128], xs[:, t, kc * 128:(kc + 1) * 128], identb[:]),
                         reads=[xsB[t], cB], writes=[psTB[s]])
                if kc % 2 == 0:
                    p.dve(CP(hT[:, kc, 0:ntile * 128], psT[s][:, 0:ntile * 128]), reads=[psTB[s]], writes=[hTB])
                else:
                    p.act(ACP(hT[:, kc, 0:ntile * 128], psT[s][:, 0:ntile * 128]), reads=[psTB[s]], writes=[hTB])

        def post_norm_residual(psX, psXB, sq, sqB, ss, ssB, rstd, rstdB, gbc, gbcB, xres, xresB, yout, youtB, tmp, tmpB):
            for hf in range(2):
                p.act(ACT(sq[:, 0:512], psX[hf], AF.Square, accum=ss[:, hf:hf + 1]), reads=[psXB[hf]], writes=[sqB, ssB])
            p.dve(TT(ss[:, 0:1], ss[:, 0:1], ss[:, 1:2], ALU.add), reads=[ssB], writes=[ssB])
            rms_rstd(ss[:, 0:1], ssB, 1, D, rstd[:, 0:1], rstdB)
            for hf in range(2):
                sl = slice(hf * 512, (hf + 1) * 512)
                p.dve(STT(tmp[:, sl], psX[hf], rstd[:, 0:1], gbc[:, sl], ALU.mult, ALU.mult),
                      reads=[psXB[hf], rstdB, gbcB], writes=[tmpB])
                p.pool(TT(yout[:, sl], tmp[:, sl], xres[:, sl], ALU.add), reads=[tmpB, xresB], writes=[youtB])

        def sweep_hgrn(l, rev):
            x_src = x_in if l == 0 else x1
            di = 1 if rev else 0
            with contextlib.ExitStack() as st:
                def T(name, shape, dt):
                    return st.enter_context(nc.sbuf_tensor(uniq(name), shape, dt))

                def PS(name, shape, dt):
                    return st.enter_context(nc.psum_tensor(uniq(name), shape, dt))

                wl = W["w_in"][l]
                gpre, gpreB = load_rowscale(st, "gpre", W["norm_mix_pre"][l], 8)
                wr = T("wr", [128, 8, 3072], BF16)
                wrB = p.buf("wr")
                if rev:
                    gh, ghB = load_rowscale(st, "gh", W["hg_norm"][l], 8)
                    wa = T("wa", [128, 8, 1024], BF16)
                    waB = p.buf("wa")
                with contextlib.ExitStack() as wst:
                    stage = make_stage(wst)
                    if not rev:
                        load_w(stage, wl, 8, OQ, 1024, wr, 0, wrB, gpre, gpreB)
                        load_w(stage, wl, 8, OFF, 1024, wr, 1024, wrB, gpre, gpreB)
                        load_w(stage, wl, 8, OI, 1024, wr, 2048, wrB, gpre, gpreB)
                    else:
                        load_w(stage, wl, 8, OFB, 1024, wr, 0, wrB, gpre, gpreB)
                        load_w(stage, wl, 8, OG, 1024, wr, 1024, wrB, gpre, gpreB)
                        load_w(stage, wl, 8, OGA, 1024, wr, 2048, wrB, gpre, gpreB)
                        load_w(stage, W["w_a"][l], 8, 0, 1024, wa, 0, waB, gh, ghB)
                    p.flush()
                if not rev:
                    cQ, cF, cI = 0, 1024, 2048
                else:
                    cF, cG, cGA = 0, 1024, 2048

                hT = T("hT", [128, 8, 512], BF16); hTB = p.buf("hT")
                vtm = [T(f"vtm{i}", [128, 4, 1024], BF16) for i in range(2)]; vtmB = [p.bufs(4, f"vtm{i}_") for i in range(2)]
                qtT = [T(f"qtT{i}", [128, 8, 512], BF16) for i in range(2)]; qtTB = [p.bufs(8, f"qtT{i}_") for i in range(2)]
                ktT = [T(f"ktT{i}", [128, 8, 512], BF16) for i in range(2)]; ktTB = [p.bufs(8, f"ktT{i}_") for i in range(2)]
                dsv = [T(f"dsv{i}", [128, 8, 8], F32) for i in range(2)]; dsvB = [p.bufs(8, f"dsv{i}_") for i in range(2)]
                tE = [T(f"tE{i}", [128, 512], F32) for i in range(2)]; tEB = p.bufs(2, "tE")
                tS = [T(f"tS{i}", [128, 512], F32) for i in range(2)]; tSB = p.bufs(2, "tS")
                tEb = T("tEb", [128, 512], F32); tEbB = p.buf("tEb")
                tL = T("tL", [128, 512], F32); tLB = p.buf("tL")
                tK = T("tK", [128, 512], F32); tKB = p.buf("tK")
                tB = T("tB", [128, 512], F32); tBB = p.buf("tB")
                tN = T("tN", [128, 512], F32); tNB = p.buf("tN")
                ktm = T("ktm", [128, 8, 128], BF16); ktmB = p.bufs(2, "ktm")
                PT = T("PT", [128, 8, 64], BF16); PTB = p.buf("PT")
                Sp = T("Sp", [128, 8, 128], F32); SpB = p.bufs(8, "Sp")
                Sbf = T("Sbf", [128, 8, 128], BF16); SbfB = p.bufs(8, "Sbf")

                psP = [PS(f"psP{i}", [128, 512], F32) for i in range(2)]; psPB = p.bufs(2, "psP")
                psS = PS("psS", [128, 8, 64], F32); psSB = p.buf("psS")
                psO = PS("psO", [128, 8, 128], F32); psOB = p.buf("psO")
                psOf = psO[:].rearrange("p h v -> p (h v)")
                psM = PS("psM", [128, 4, 128], F32); psMB = [p.buf("psM")] * 4
                psKa = PS("psKa", [128, 1024], BF16); psKb = PS("psKb", [128, 1024], BF16)
                psKB = p.bufs(2, "psK")
                psKv = [psKa[:, 0:512].rearrange("p (h k) -> p h k", k=128), psKb[:, 0:512].rearrange("p (h k) -> p h k", k=128)]
                pp = {"i": 0}

                def next_ps():
                    i = pp["i"] % 2
                    pp["i"] += 1
                    return psP[i][:], psPB[i]

                if not rev:
                    xt = [T(f"xt{i}", [128, 4, D], F32) for i in range(2)]; xtB = p.bufs(2, "xt")
                    junk = T("junk", [128, D], BF16); junkB = p.buf("junk")
                    ss = T("ss", [128, 4], F32); ssB = p.buf("ss")
                    rstd = T("rstd", [128, 4], F32); rstdB = p.buf("rstd")
                    xs = T("xs", [128, 4, D], BF16); xsB = p.bufs(4, "xs")
                    osb = [T(f"osb{i}", [128, 1024], F32) for i in range(2)]; osbB = p.bufs(2, "osb")
                    psT = [psKa[:, 0:512], psKb[:, 0:512]]; psTB = [psKB[0], psKB[1]]
                    hTstB, qTstB, vstB, ofstB = p.buf("hTst"), p.buf("qTst"), p.buf("vst"), p.buf("ofst")
                else:
                    oft = [T(f"oft{i}", [128, 1024], F32) for i in range(2)]; oftB = p.bufs(2, "oft")
                    gT = T("gT", [128, 8, 512], BF16); gTB = p.bufs(8, "gT")
                    sga = T("sga", [128, 8, 512], BF16); sgaB = p.bufs(8, "sga")
                    AT = T("AT", [128, 8, 512], BF16); ATB = p.bufs(4, "AT")
                    mATr = [T(f"mATr{i}", [128, 512], BF16) for i in range(2)]; mATrB = p.bufs(2, "mATr")
                    osum = T("osum", [128, 1024], F32); osumB = p.buf("osum")
                    sq = T("sq", [128, 1024], F32); sqB = p.buf("sq")
                    ss8 = T("ss8", [128, 8], F32); ss8B = p.buf("ss8")
                    rs8 = T("rs8", [128, 8], F32); rs8B = p.buf("rs8")
                    on = [T(f"on{i}", [128, 8, 128], BF16) for i in range(2)]; onB = p.bufs(2, "on")
                    psOT = psKa[:].rearrange("p (h k) -> p h k", k=128); psOTB = psKB[0]
                    mATstB = p.buf("mATst")

                for h in range(8):
                    p.pool(MSET(Sp[:, h, :], 0.0), writes=[SpB[h]])
                    p.pool(MSET(Sbf[:, h, :], 0.0), writes=[SbfB[h]])
                mask = maskb if rev else maskf

                order = list(range(NS))
                if rev:
                    order = order[::-1]

                def xload(it):
                    j = order[it]
                    p.dma(DMA(xt[it % 2][:], x_src[j * 512:(j + 1) * 512, :].rearrange("(t p) d -> p t d", p=128)),
                          writes=[xtB[it % 2]])

                def gate_head(b, h):
                    s = h % 2
                    ps, psB_ = next_ps()
                    proj_fm(ps, wr, cF + h * 128, hT, 8, 0, 512, [wrB, hTB], [psB_])
                    sigmoid_from_psum(ps, psB_, tE[s][:], tEB[s], tS[s][:], tSB[s])
                    lb_c = lbt[:, di, l, h:h + 1]
                    oml_c = omlt[:, di, l, h:h + 1]
                    noml_c = nomlt[:, di, l, h:h + 1]
                    p.act(ACT(tL[:], tS[s][:], AF.Ln, bias=lb_c, scale=oml_c), reads=[tSB[s], cB], writes=[tLB])
                    p.dve(TS(tK[:], tS[s][:], noml_c, oml_c, ALU.mult, ALU.add), reads=[tSB[s], cB], writes=[tKB])
                    p.dve(lambda e: e.tensor_tensor_scan(out=tB[:], data0=scanm[:], data1=tL[:], initial=0.0,
                                                         op0=ALU.mult, op1=ALU.add),
                          reads=[tLB, cB], writes=[tBB])
                    if rev:
                        p.pool(TT(tL[:], tL[:], tB[:], ALU.subtract), reads=[tLB, tBB], writes=[tLB])
                        tot = tB[:].rearrange("p (c t) -> p c t", t=64)[:, :, 63:64].to_broadcast([128, 8, 64])
                        p.pool(TT(tL[:].rearrange("p (c t) -> p c t", t=64), tL[:].rearrange("p (c t) -> p c t", t=64),
                                  tot, ALU.add), reads=[tLB, tBB], writes=[tLB])
                        bsrc, bsrcB = tL, tLB
                    else:
                        bsrc, bsrcB = tB, tBB
                    p.act(ACT(tEb[:], bsrc[:], AF.Exp), reads=[bsrcB], writes=[tEbB])
                    p.act(ACT(tN[:], bsrc[:], AF.Exp, scale=-1.0), reads=[bsrcB], writes=[tNB])
                    dc_ = 0 if rev else 63
                    p.pool(CP(dsv[b][:, h, :], tEb[:].rearrange("p (c t) -> p c t", t=64)[:, :, dc_]),
                           reads=[tEbB], writes=[dsvB[b][h]])
                    p.pool(TT(qtT[b][:, h, :], qtT[b][:, h, :], tEb[:], ALU.mult), reads=[qtTB[b][h], tEbB], writes=[qtTB[b][h]])
                    p.dve(TT(ktT[b][:, h, :], tK[:], tN[:], ALU.mult), reads=[tKB, tNB], writes=[ktTB[b][h]])

                def front(it):
                    b = it % 2
                    j = order[it]
                    if not rev:
                        if it + 1 < NS:
                            xload(it + 1)
                        norm_transpose((junk, junkB, ss, ssB, rstd, rstdB, xs, xsB), xt[b], xtB[b], 4, hT, hTB, psT, psTB)
                        p.dma(DMA(hT_st[j], hT[:].rearrange("p k t -> p (k t)")), reads=[hTB], writes=[hTstB])
                        yield
                        for h in range(8):
                            ps, psB_ = next_ps()
                            proj_fm(ps, wr, cQ + h * 128, hT, 8, 0, 512, [wrB, hTB], [psB_])
                            s = h % 2
                            sigmoid_from_psum(ps, psB_, tE[s][:], tEB[s], tS[s][:], tSB[s])
                            p.dve(TT(qtT[b][:, h, :], ps, tS[s][:], ALU.mult), reads=[psB_, tSB[s]], writes=[qtTB[b][h]])
                            yield
                        p.dma(DMA(qT_st[j], qtT[b][:].rearrange("p k t -> p (k t)")), reads=qtTB[b], writes=[qTstB])
                        for t in range(4):
                            for hf in range(2):
                                ps, psB_ = next_ps()
                                proj_tm(ps, hT, t * 128, wr, cI + hf * 512, 512, 8, [wrB, hTB], [psB_])
                                if hf == 0:
                                    p.act(ACP(vtm[b][:, t, 0:512], ps), reads=[psB_], writes=[vtmB[b][t]])
                                else:
                                    p.dve(CP(vtm[b][:, t, 512:1024], ps), reads=[psB_], writes=[vtmB[b][t]])
                            yield
                        p.dma(DMA(v_st[j], vtm[b][:].rearrange("p t d -> p (t d)")), reads=vtmB[b], writes=[vstB])
                        for h in range(8):
                            gate_head(b, h)
                            yield
                    else:
                        p.dma(DMA(hT[:].rearrange("p k t -> p (k t)"), hT_st[j]), writes=[hTB])
                        p.dma(DMA(qtT[b][:].rearrange("p k t -> p (k t)"), qT_st[j]), writes=qtTB[b])
                        p.dma(DMA(vtm[b][:].rearrange("p t d -> p (t d)"), v_st[j]), writes=vtmB[b])
                        yield
                        for h in range(8):
                            gate_head(b, h)
                            yield

                def frontB(it):
                    if True:
                        for h in range(8):
                            ps, psB_ = next_ps()
                            proj_fm(ps, wr, cG + h * 128, hT, 8, 0, 512, [wrB, hTB], [psB_])
                            s = h % 2
                            sigmoid_from_psum(ps, psB_, tE[s][:], tEB[s], tS[s][:], tSB[s])
                            p.dve(TT(gT[:, h, :], ps, tS[s][:], ALU.mult), reads=[psB_, tSB[s]], writes=[gTB[h]])
                            yield
                        for h in range(8):
                            ps, psB_ = next_ps()
                            proj_fm(ps, wr, cGA + h * 128, hT, 8, 0, 512, [wrB, hTB], [psB_])
                            s = h % 2
                            sigmoid_from_psum(ps, psB_, tE[s][:], tEB[s], sga[:, h, :], sgaB[h])
                            yield

                def pump(gens, n):
                    for _ in range(n):
                        done = False
                        for g in gens:
                            try:
                                next(g)
                                done = True
                                break
                            except StopIteration:
                                continue
                        if not done:
                            return

                def drain(gen):
                    if gen is None:
                        return
                    for _ in gen:
                        pass

                state = {"dprev": [ones1[:, 0:1]] * 8, "dprevB": [cB] * 8, "oi": 0}

                def chain2(g1, g2):
                    if g1 is not None:
                        yield from g1
                    if g2 is not None:
                        yield from g2

                def emit_pending():
                    if state.get("pend") is None:
                        return
                    t_, o_ = state["pend"]
                    state["pend"] = None
                    tk_ = slice(t_ * 128, (t_ + 1) * 128)
                    for h in range(8):
                        p.pe(TR(psOT[:, h, :], on[o_][:, h, :], identb[:]), reads=[onB[o_], cB], writes=[psOTB])
                    p.act(ACP(AT[:, :, tk_], psOT[:]), reads=[psOTB], writes=[ATB[t_]])

                def back(it, genB, genA):
                    gen = [g for g in (genB, genA) if g is not None]
                    b = it % 2
                    j = order[it]
                    dprev, dprevB = state["dprev"], state["dprevB"]
                    tiles = [3, 2, 1, 0] if rev else [0, 1, 2, 3]
                    chunks = [1, 0] if rev else [0, 1]
                    for t in tiles:
                        tk = slice(t * 128, (t + 1) * 128)
                        if rev:
                            ofs = state["oi"] % 2
                            state["oi"] += 1
                            p.dma(DMA(oft[ofs][:], of_st[j][:, t * 1024:(t + 1) * 1024]), writes=[oftB[ofs]])
                        for half in range(2):
                            for h in range(half * 4, half * 4 + 4):
                                p.pe(TR(psKv[half][:, h % 4, :], ktT[b][:, h, tk], identb[:]), reads=[ktTB[b][h], cB], writes=[psKB[half]])
                            if half == 0:
                                p.act(ACP(ktm[:, 0:4, :], psKv[0]), reads=[psKB[0]], writes=[ktmB[0]])
                            else:
                                p.dve(CP(ktm[:, 4:8, :], psKv[1]), reads=[psKB[1]], writes=[ktmB[1]])
                        for c in range(2):
                            ck = slice(t * 128 + c * 64, t * 128 + c * 64 + 64)
                            for h in range(8):
                                p.pe(MM(psS[c * 64:(c + 1) * 64, h, :], ktT[b][:, h, ck], qtT[b][:, h, ck]),
                                     reads=[ktTB[b][h], qtTB[b][h]], writes=[psSB])
                        p.dve(TT(PT[:], psS[:], mask[:], ALU.mult), reads=[psSB, cB], writes=[PTB])
                        if rev:
                            emit_pending()
                        for c in chunks:
                            pr = slice(c * 64, (c + 1) * 64)
                            ck = slice(t * 128 + c * 64, t * 128 + c * 64 + 64)
                            cidx = t * 2 + c
                            for h in range(8):
                                vs = vtm[b][pr, t, h * 128:(h + 1) * 128]
                                p.pe(MM(psO[pr, h, :], qtT[b][:, h, ck], Sbf[:, h, :], True, False),
                                     reads=[qtTB[b][h], SbfB[h]], writes=[psOB])
                                p.pe(MM(psO[pr, h, :], PT[pr, h, :], vs, False, True),
                                     reads=[PTB, vtmB[b][t]], writes=[psOB])
                            for half in range(2):
                                for h in range(half * 4, half * 4 + 4):
                                    vs = vtm[b][pr, t, h * 128:(h + 1) * 128]
                                    p.pe(MM(psM[:, h % 4, :], ktm[pr, h, :], vs), reads=[ktmB[half], vtmB[b][t]], writes=[psMB[h % 4]])
                                for h in range(half * 4, half * 4 + 4):
                                    p.dve(STT(Sp[:, h, :], Sp[:, h, :], dprev[h], psM[:, h % 4, :], ALU.mult, ALU.add),
                                          reads=[SpB[h], dprevB[h], psMB[h % 4]], writes=[SpB[h]])
                                    dcur = dsv[b][:, h, cidx:cidx + 1]
                                    if h % 2 == 0:
                                        p.act(ACT(Sbf[:, h, :], Sp[:, h, :], AF.Copy, scale=dcur),
                                              reads=[SpB[h], dsvB[b][h]], writes=[SbfB[h]])
                                    else:
                                        p.pool(TS(Sbf[:, h, :], Sp[:, h, :], dcur, 0.0, ALU.mult, ALU.add),
                                               reads=[SpB[h], dsvB[b][h]], writes=[SbfB[h]])
                                    dprev[h] = dcur
                                    dprevB[h] = dsvB[b][h]
                            pump(gen, 3)
                        if not rev:
                            ob = t % 2
                            p.act(ACP(osb[ob][:, 0:512], psOf[:, 0:512]), reads=[psOB], writes=[osbB[ob]])
                            p.dve(CP(osb[ob][:, 512:1024], psOf[:, 512:1024]), reads=[psOB], writes=[osbB[ob]])
                            p.dma(DMA(of_st[j][:, t * 1024:(t + 1) * 1024], osb[ob][:]), reads=[osbB[ob]], writes=[ofstB])
                        else:
                            p.dve(TT(osum[:], psOf, oft[ofs][:], ALU.add), reads=[psOB, oftB[ofs]], writes=[osumB])
                            p.act(ACT(sq[:], osum[:], AF.Square), reads=[osumB], writes=[sqB])
                            p.dve(lambda e: e.tensor_reduce(out=ss8[:], in_=sq[:].rearrange("p (h v) -> p h v", v=128),
                                                            op=ALU.add, axis=AX.X), reads=[sqB], writes=[ss8B])
                            rms_rstd(ss8[:], ss8B, 8, 128, rs8[:], rs8B)
                            p.pool(TT(on[ofs][:], osum[:].rearrange("p (h v) -> p h v", v=128),
                                      rs8[:].unsqueeze(2).to_broadcast([128, 8, 128]), ALU.mult),
                                   reads=[osumB, rs8B], writes=[onB[ofs]])
                            state["pend"] = (t, ofs)
                    if rev:
                        emit_pending()
                    drain(genB)
                    for h in range(8):
                        p.pool(TS(Sp[:, h, :], Sp[:, h, :], dprev[h], 0.0, ALU.mult, ALU.add),
                               reads=[SpB[h], dprevB[h]], writes=[SpB[h]])
                        dprev[h] = ones1[:, 0:1]
                        dprevB[h] = cB
                    if rev:
                        for dc in range(8):
                            if dc % 2 == 0:
                                p.pool(TT(AT[:, dc, :], AT[:, dc, :], gT[:, dc, :], ALU.mult), reads=[ATB, gTB[dc]], writes=[ATB])
                            else:
                                p.dve(TT(AT[:, dc, :], AT[:, dc, :], gT[:, dc, :], ALU.mult), reads=[ATB, gTB[dc]], writes=[ATB])
                        for dc in range(8):
                            ps, psB_ = next_ps()
                            proj_fm(ps, wa, dc * 128, AT, 8, 0, 512, [waB, ATB], [psB_])
                            s = dc % 2
                            p.dve(TT(mATr[s][:], ps, sga[:, dc, :], ALU.mult), reads=[psB_, sgaB[dc]], writes=[mATrB[s]])
                            p.dma(DMA(mAT_st[j][:, dc * 512:(dc + 1) * 512], mATr[s][:]), reads=[mATrB[s]], writes=[mATstB])

                if not rev:
                    xload(0)
                drain(front(0))
                for it in range(NS):
                    genA = front(it + 1) if it + 1 < NS else None
                    genB = frontB(it) if rev else None
                    back(it, genB, genA)
                    drain(genA)
                p.flush()

        def sweep_c(l):
            x_src = x_in if l == 0 else x1
            with contextlib.ExitStack() as st:
                def T(name, shape, dt):
                    return st.enter_context(nc.sbuf_tensor(uniq(name), shape, dt))

                def PS(name, shape, dt):
                    return st.enter_context(nc.psum_tensor(uniq(name), shape, dt))

                gpre, gpreB = load_rowscale(st, "gpre", W["norm_mix_pre"][l], 8)
                wr = T("wr", [128, 8, 2048], BF16); wrB = p.buf("wr")
                wb = T("wb", [128, 4, 1024], BF16); wbB = p.buf("wb")
                wo = T("wo", [128, 8, 1024], BF16); woB = p.buf("wo")
                cU, cV, cGB = 0, 512, 1024
                with contextlib.ExitStack() as wst:
                    stage = make_stage(wst)
                    load_w(stage, W["w_in"][l], 8, OU, 512, wr, 0, wrB, gpre, gpreB)
                    load_w(stage, W["w_in"][l], 8, OV, 512, wr, 512, wrB, gpre, gpreB)
                    load_w(stage, W["w_in"][l], 8, OGB, 1024, wr, 1024, wrB, gpre, gpreB)
                    load_w(stage, W["w_b"][l], 4, 0, 1024, wb, 0, wbB)
                    load_w(stage, W["w_out"][l], 8, 0, 1024, wo, 0, woB)
                    p.flush()
                lng, lngB = load_bcast(st, "lng", W["sg_ln_g"][l], 512)
                lnb, lnbB = load_bcast(st, "lnb", W["sg_ln_b"][l], 512)
                gpo, gpoB = load_bcast(st, "gpo", W["norm_mix_post"][l], 1024)
                wsf = T("wsf", [128, 8, 128], F32); wsfB = p.buf("wsf")
                wsT = T("wsT", [128, 8, 128], BF16); wsTB = p.buf("wsT")
                bsb = T("bsb", [128, 4, 128], F32); bsbB = p.buf("bsb")
                psW = PS("psW", [128, 4, 128], F32); psWB = p.buf("psW")
                p.dma(DMA(wsf[:], W["sg_w"][l].rearrange("g t s -> t g s")), writes=[wsfB])
                for half in range(2):
                    for g in range(half * 4, half * 4 + 4):
                        p.pe(TR(psW[:, g % 4, :], wsf[:, g, :], identf[:]), reads=[wsfB, cB], writes=[psWB])
                    p.dve(CP(wsT[:, half * 4:half * 4 + 4, :], psW[:]), reads=[psWB], writes=[wsTB])
                for g in range(8):
                    p.dma(DMA(bsb[(g % 2) * 64:(g % 2) * 64 + 64, g // 2, :], W["sg_b"][l, g, :].partition_broadcast(64)),
                          writes=[bsbB])

                hT = T("hT", [128, 8, 512], BF16); hTB = p.buf("hT")
                mAT = T("mAT", [128, 8, 512], BF16); mATB = p.buf("mAT")
                xt = T("xt", [128, 4, D], F32); xtB = p.buf("xt")
                uT = T("uT", [128, 4, 512], BF16); uTB = p.bufs(4, "uT")
                gv = T("gv", [128, 512], F32); gvB = p.buf("gv")
                st6 = T("st6", [128, 6], F32); st6B = p.buf("st6")
                mv = T("mv", [128, 2], F32); mvB = p.buf("mv")
                rs = T("rs", [128, 1], F32); rsB = p.buf("rs")
                vh = T("vh", [128, 512], F32); vhB = p.buf("vh")
                vn = T("vn", [128, 512], BF16); vnB = p.buf("vn")
                tg = T("tg", [128, 4, 128], F32); tgB = p.buf("tg")
                BT = T("BT", [128, 4, 512], BF16); BTB = p.bufs(4, "BT")
                tE = [T(f"tE{i}", [128, 512], F32) for i in range(2)]; tEB = p.bufs(2, "tE")
                sgb = T("sgb", [128, 8, 512], BF16); sgbB = p.bufs(8, "sgb")
                tm = [T(f"tm{i}", [128, 512], F32) for i in range(2)]; tmB = p.bufs(2, "tm")
                mg = T("mg", [128, 8, 512], BF16); mgB = p.bufs(8, "mg")
                sq = T("sq", [128, 512], F32); sqB = p.buf("sq")
                ss = T("ss", [128, 2], F32); ssB = p.buf("ss")
                rstd = T("rstd", [128, 1], F32); rstdB = p.buf("rstd")
                tmp = T("tmp", [128, D], F32); tmpB = p.buf("tmp")
                yo = [T(f"yo{i}", [128, D], F32) for i in range(2)]; yoB = p.bufs(2, "yo")

                psP = [PS(f"psP{i}", [128, 512], F32) for i in range(3)]; psPB = p.bufs(3, "psP")
                psG = PS("psG", [128, 4, 128], F32); psGB = p.buf("psG")
                psX = [PS(f"psX{i}", [128, 512], F32) for i in range(2)]; psXB = p.bufs(2, "psX")
                pp = {"i": 0}

                def next_ps():
                    i = pp["i"] % 3
                    pp["i"] += 1
                    return psP[i][:], psPB[i]

                xmidstB = p.buf("xmidst")
                for j in range(NS):
                    p.dma(DMA(hT[:].rearrange("p k t -> p (k t)"), hT_st[j]), writes=[hTB])
                    p.dma(DMA(mAT[:].rearrange("p k t -> p (k t)"), mAT_st[j]), writes=[mATB])
                    p.dma(DMA(xt[:], x_src[j * 512:(j + 1) * 512, :].rearrange("(t p) d -> p t d", p=128)), writes=[xtB])
                    for c in range(4):
                        ps, psB_ = next_ps()
                        proj_fm(ps, wr, cU + c * 128, hT, 8, 0, 512, [wrB, hTB], [psB_])
                        p.act(ACT(uT[:, c, :], ps, AF.Gelu), reads=[psB_], writes=[uTB[c]])
                    for t in range(4):
                        tk = slice(t * 128, (t + 1) * 128)
                        ps, psB_ = next_ps()
                        proj_tm(ps, hT, t * 128, wr, cV, 512, 8, [wrB, hTB], [psB_])
                        p.act(ACT(gv[:], ps, AF.Gelu), reads=[psB_], writes=[gvB])
                        p.dve(lambda e: e.bn_stats(st6[:], gv[:]), reads=[gvB], writes=[st6B])
                        p.dve(lambda e: e.bn_aggr(mv[:], st6[:]), reads=[st6B], writes=[mvB])
                        p.dve(TS(rs[:], mv[:, 1:2], 1.0, EPS, ALU.mult, ALU.add), reads=[mvB], writes=[rsB])
                        p.pool(TT(rs[:], rs[:], mhalf[:, 0:1], ALU.pow), reads=[rsB, cB], writes=[rsB])
                        p.dve(TS(vh[:], gv[:], mv[:, 0:1], rs[:, 0:1], ALU.subtract, ALU.mult), reads=[gvB, mvB, rsB], writes=[vhB])
                        p.pool(TT(vh[:], vh[:], lng[:], ALU.mult), reads=[vhB, lngB], writes=[vhB])
                        p.pool(TT(vn[:], vh[:], lnb[:], ALU.add), reads=[vhB, lnbB], writes=[vnB])
                        for g in range(8):
                            p.pe(MM(psG[(g % 2) * 64:(g % 2) * 64 + 64, g // 2, :], vn[:, g * 64:(g + 1) * 64], wsT[:, g, :]),
                                 reads=[vnB, wsTB], writes=[psGB])
                        p.dve(TT(tg[:], psG[:], bsb[:], ALU.add), reads=[psGB, bsbB], writes=[tgB])
                        p.pool(TT(BT[:, :, tk], tg[:], uT[:, :, tk], ALU.mult), reads=[tgB, uTB], writes=[BTB[t]])
                    for dc in range(8):
                        ps, psB_ = next_ps()
                        proj_fm(ps, wr, cGB + dc * 128, hT, 8, 0, 512, [wrB, hTB], [psB_])
                        s = dc % 2
                        sigmoid_from_psum(ps, psB_, tE[s][:], tEB[s], sgb[:, dc, :], sgbB[dc])
                    for dc in range(8):
                        ps, psB_ = next_ps()
                        proj_fm(ps, wb, dc * 128, BT, 4, 0, 512, [wbB, BTB], [psB_])
                        s = dc % 2
                        p.dve(TT(tm[s][:], ps, sgb[:, dc, :], ALU.mult), reads=[psB_, sgbB[dc]], writes=[tmB[s]])
                        p.pool(TT(mg[:, dc, :], tm[s][:], mAT[:, dc, :], ALU.add), reads=[tmB[s], mATB], writes=[mgB[dc]])
                    for t in range(4):
                        for hf in range(2):
                            proj_tm(psX[hf][:], mg, t * 128, wo, hf * 512, 512, 8, [woB, mgB], [psXB[hf]])
                        ob = t % 2
                        post_norm_residual([psX[0][:], psX[1][:]], psXB, sq, sqB, ss, ssB, rstd, rstdB, gpo, gpoB,
                                           xt[:, t, :], xtB, yo[ob], yoB[ob], tmp, tmpB)
                        p.dma(DMA(xmid[j * 512 + t * 128:j * 512 + (t + 1) * 128, :], yo[ob][:]), reads=[yoB[ob]],
                              writes=[xmidstB])
                p.flush()

        def sweep_d(l):
            TD = 256
            with contextlib.ExitStack() as st:
                def T(name, shape, dt):
                    return st.enter_context(nc.sbuf_tensor(uniq(name), shape, dt))

                def PS(name, shape, dt):
                    return st.enter_context(nc.psum_tensor(uniq(name), shape, dt))

                gpre, gpreB = load_rowscale(st, "gpre", W["norm_ffn_pre"][l], 8)
                wg = T("wg", [128, 8, FH], BF16); wgB = p.buf("wg")
                wu = T("wu", [128, 8, FH], BF16); wuB = p.buf("wu")
                wd = T("wd", [128, FC, D], BF16); wdB = p.buf("wd")
                with contextlib.ExitStack() as wst:
                    stage = make_stage(wst)
                    load_w(stage, W["w_gate"][l], 8, 0, FH, wg, 0, wgB, gpre, gpreB)
                    load_w(stage, W["w_up"][l], 8, 0, FH, wu, 0, wuB, gpre, gpreB)
                    load_w(stage, W["w_down"][l], FC, 0, D, wd, 0, wdB)
                    p.flush()
                gpo, gpoB = load_bcast(st, "gpo", W["norm_ffn_post"][l], 1024)

                xt = [T(f"xt{i}", [128, 2, D], F32) for i in range(2)]; xtB = p.bufs(2, "xt")
                junk = T("junk", [128, D], BF16); junkB = p.buf("junk")
                ss = T("ss", [128, 4], F32); ssB = p.buf("ss")
                rstd = T("rstd", [128, 4], F32); rstdB = p.buf("rstd")
                xs = T("xs", [128, 2, D], BF16); xsB = p.bufs(2, "xs")
                hT = T("hT", [128, 8, TD], BF16); hTB = p.buf("hT")
                sl = [T(f"sl{i}", [128, TD], F32) for i in range(2)]; slB = p.bufs(2, "sl")
                hid = T("hid", [128, FC, TD], BF16); hidB = p.bufs(FC, "hid")
                sq = T("sq", [128, 512], F32); sqB = p.buf("sq")
                ss2 = T("ss2", [128, 2], F32); ss2B = p.buf("ss2")
                rstd2 = T("rstd2", [128, 1], F32); rstd2B = p.buf("rstd2")
                tmp = T("tmp", [128, D], F32); tmpB = p.buf("tmp")
                yo = [T(f"yo{i}", [128, D], F32) for i in range(2)]; yoB = p.bufs(2, "yo")

                psTt = PS("psTt", [128, 2, 512], BF16); psT = [psTt[:, 0, :], psTt[:, 1, :]]; psTB = [p.buf("psT")] * 2
                psA = [PS(f"psA{i}", [128, 512], F32)[:, 0:TD] for i in range(2)]; psAB = p.bufs(2, "psA")
                psU = [PS(f"psU{i}", [128, 512], F32)[:, 0:TD] for i in range(2)]; psUB = p.bufs(2, "psU")
                psX = [PS(f"psX{i}", [128, 512], F32) for i in range(2)]; psXB = p.bufs(2, "psX")

                if l == 1:
                    djs = list(range(cfg.out_lo // TD, (cfg.out_lo + cfg.out_n) // TD))
                else:
                    djs = list(range(NT // TD))
                xastB = p.buf("xast")
                p.dma(DMA(xt[0][:], xmid[djs[0] * TD:(djs[0] + 1) * TD, :].rearrange("(t p) d -> p t d", p=128)), writes=[xtB[0]])
                for dit, j in enumerate(djs):
                    cur = dit % 2
                    if dit + 1 < len(djs):
                        jn = djs[dit + 1]
                        p.dma(DMA(xt[1 - cur][:], xmid[jn * TD:(jn + 1) * TD, :].rearrange("(t p) d -> p t d", p=128)),
                              writes=[xtB[1 - cur]])
                    norm_transpose((junk, junkB, ss, ssB, rstd, rstdB, xs, xsB), xt[cur], xtB[cur], 2, hT, hTB, psT, psTB)
                    for fc in range(FC):
                        s = fc % 2
                        proj_fm(psA[s][:], wg, fc * 128, hT, 8, 0, TD, [wgB, hTB], [psAB[s]])
                        proj_fm(psU[s][:], wu, fc * 128, hT, 8, 0, TD, [wuB, hTB], [psUB[s]])
                        p.act(ACT(sl[s][:], psA[s][:], AF.Silu), reads=[psAB[s]], writes=[slB[s]])
                        p.dve(TT(hid[:, fc, :], psU[s][:], sl[s][:], ALU.mult), reads=[psUB[s], slB[s]], writes=[hidB[fc]])
                    for t in range(2):
                        for hf in range(2):
                            for fc in range(FC):
                                p.pe(MM(psX[hf][:], hid[:, fc, t * 128:(t + 1) * 128], wd[:, fc, hf * 512:(hf + 1) * 512],
                                        fc == 0, fc == FC - 1), reads=[hidB[fc], wdB], writes=[psXB[hf]])
                        ob = t % 2
                        post_norm_residual([psX[0][:], psX[1][:]], psXB, sq, sqB, ss2, ss2B, rstd2, rstd2B, gpo, gpoB,
                                           xt[cur][:, t, :], xtB[cur], yo[ob], yoB[ob], tmp, tmpB)
                        p.dma(DMA(xa[j * TD + t * 128:j * TD + (t + 1) * 128, :], yo[ob][:]), reads=[yoB[ob]],
                              writes=[xastB])
                p.flush()

        def sweep_e(l, final):
            TD = 256
            with contextlib.ExitStack() as st:
                def T(name, shape, dt):
                    return st.enter_context(nc.sbuf_tensor(uniq(name), shape, dt))

                def PS(name, shape, dt):
                    return st.enter_context(nc.psum_tensor(uniq(name), shape, dt))

                wp = T("wp", [128, 2, D], BF16); wpB = p.buf("wp")
                wq = T("wq", [128, 8, D], BF16); wqB = p.buf("wq")
                with contextlib.ExitStack() as wst:
                    stage = make_stage(wst)
                    load_w(stage, W["w_ple"][l], 2, 0, D, wp, 0, wpB)
                    load_w(stage, W["w_ple_gate"][l], 8, 0, D, wq, 0, wqB)
                    p.flush()
                xt = [T(f"xt{i}", [128, 2, D], F32) for i in range(3)]; xtB = p.bufs(3, "xt")
                pt = [T(f"pt{i}", [128, 2, 256], F32) for i in range(3)]; ptB = p.bufs(3, "pt")
                xb = [T(f"xb{i}", [128, 2, D], BF16) for i in range(2)]; xbB = [p.bufs(2, f"xb{i}_") for i in range(2)]
                pb = [T(f"pb{i}", [128, 2, 256], BF16) for i in range(2)]; pbB = p.bufs(2, "pb")
                xT = [T(f"xT{i}", [128, 8, TD], BF16) for i in range(2)]; xTB = p.bufs(2, "xT")
                pT = [T(f"pT{i}", [128, 2, TD], BF16) for i in range(2)]; pTB = p.bufs(2, "pT")
                tE = [T(f"tE{i}", [128, 512], F32) for i in range(2)]; tEB = p.bufs(2, "tE")
                t2 = [T(f"t2{i}", [128, 512], F32) for i in range(2)]; t2B = p.bufs(2, "t2")
                yo = [T(f"yo{i}", [128, D], F32) for i in range(2)]; yoB = p.bufs(2, "yo")
                psT = [PS(f"psT{i}", [128, 1024], BF16)[:, 0:TD] for i in range(2)]; psTB = p.bufs(2, "psT")
                psG = [PS(f"psG{i}", [128, 512], F32) for i in range(2)]; psGB = p.bufs(2, "psG")
                psP = [PS(f"psP{i}", [128, 512], F32) for i in range(2)]; psPB = p.bufs(2, "psP")

                if final:
                    js = list(range(cfg.out_lo // TD, (cfg.out_lo + cfg.out_n) // TD))
                else:
                    js = list(range(NT // TD))
                nj = len(js)

                def ld(it):
                    if it >= nj:
                        return
                    j = js[it]
                    s = it % 3
                    p.dma(DMA(xt[s][:], xa[j * TD:(j + 1) * TD, :].rearrange("(t p) d -> p t d", p=128)), writes=[xtB[s]])
                    p.dma(DMA(pt[s][:], p_in[l, j * TD:(j + 1) * TD, :].rearrange("(t p) d -> p t d", p=128)), writes=[ptB[s]])

                def prep(it):
                    if it >= nj:
                        return
                    s3 = it % 3
                    b = it % 2
                    p.act(ACP(xb[b][:, 0, :], xt[s3][:, 0, :]), reads=[xtB[s3]], writes=[xbB[b][0]])
                    p.pool(CP(xb[b][:, 1, :], xt[s3][:, 1, :]), reads=[xtB[s3]], writes=[xbB[b][1]])
                    p.pool(CP(pb[b][:], pt[s3][:]), reads=[ptB[s3]], writes=[pbB[b]])
                    for kc in range(8):
                        s = kc % 2
                        for t in range(2):
                            p.pe(TR(psT[s][:, t * 128:(t + 1) * 128], xb[b][:, t, kc * 128:(kc + 1) * 128], identb[:]),
                                 reads=[xbB[b][t], cB], writes=[psTB[s]])
                        if s == 0:
                            p.dve(CP(xT[b][:, kc, :], psT[s][:, 0:TD]), reads=[psTB[s]], writes=[xTB[b]])
                        else:
                            p.act(ACP(xT[b][:, kc, :], psT[s][:, 0:TD]), reads=[psTB[s]], writes=[xTB[b]])
                    for kc in range(2):
                        s = kc % 2
                        for t in range(2):
                            p.pe(TR(psT[s][:, t * 128:(t + 1) * 128], pb[b][:, t, kc * 128:(kc + 1) * 128], identb[:]),
                                 reads=[pbB[b], cB], writes=[psTB[s]])
                        p.dve(CP(pT[b][:, kc, :], psT[s][:, 0:TD]), reads=[psTB[s]], writes=[pTB[b]])

                x1stB = p.buf("x1st")

                def compute(it):
                    j = js[it]
                    s3 = it % 3
                    b = it % 2
                    for t in range(2):
                        ob = t % 2
                        for hf in range(2):
                            sl = slice(hf * 512, (hf + 1) * 512)
                            proj_tm(psG[hf][:], xT[b], t * 128, wq, hf * 512, 512, 8, [wqB, xTB[b]], [psGB[hf]])
                            proj_tm(psP[hf][:], pT[b], t * 128, wp, hf * 512, 512, 2, [wpB, pTB[b]], [psPB[hf]])
                            sigmoid_from_psum(psG[hf][:], psGB[hf], tE[hf][:], tEB[hf], tE[hf][:], tEB[hf])
                            p.dve(TT(t2[hf][:], psP[hf][:], tE[hf][:], ALU.mult), reads=[psPB[hf], tEB[hf]], writes=[t2B[hf]])
                            p.pool(TT(yo[ob][:, sl], t2[hf][:], xt[s3][:, t, sl], ALU.add), reads=[t2B[hf], xtB[s3]],
                                   writes=[yoB[ob]])
                        r0 = j * TD + t * 128
                        if final:
                            dst = out[r0 - cfg.out_lo:r0 - cfg.out_lo + 128, :]
                        else:
                            dst = x1[r0:r0 + 128, :]
                        p.dma(DMA(dst, yo[ob][:]), reads=[yoB[ob]], writes=[x1stB])

                ld(0)
                ld(1)
                prep(0)
                for it in range(nj):
                    ld(it + 2)
                    prep(it + 1)
                    compute(it)
                p.flush()

        steps = []
        for l in range(2):
            steps += [lambda l=l: sweep_hgrn(l, False), lambda l=l: sweep_hgrn(l, True), lambda l=l: sweep_c(l),
                      lambda l=l: sweep_d(l), lambda l=l: sweep_e(l, final=(l == 1))]
        if cfg.stop_after is not None:
            steps = steps[:cfg.stop_after] if isinstance(cfg.stop_after, int) else [steps[i] for i in cfg.stop_after]
        for f in steps:
            f()
        ninst = p.ninst
    return nc, ninst


SEG = 4096
HALO = 256
_WNAMES = ["norm_mix_pre", "w_in", "lb_gamma_fwd", "lb_gamma_bwd", "hg_norm", "sg_w", "sg_b", "sg_ln_g",
           "sg_ln_b", "w_a", "w_b", "w_out", "norm_mix_post", "norm_ffn_pre", "w_gate", "w_up", "w_down",
           "norm_ffn_post", "w_ple", "w_ple_gate"]
_CACHE = {}


def kernel(**inputs):
    x = np.asarray(inputs["x"], dtype=np.float32)
    pp = np.asarray(inputs["p"], dtype=np.float32)
    Bn, S, _ = x.shape
    nseg = S // SEG
    ncores = Bn * nseg
    NT = SEG + 2 * HALO
    key = (NT,)
    if key not in _CACHE:
        _CACHE[key] = build(Cfg(NT, HALO, SEG))[0]
    nc = _CACHE[key]
    wts = {k: np.ascontiguousarray(np.asarray(inputs[k], dtype=np.float32)) for k in _WNAMES}
    in_maps = []
    for b in range(Bn):
        for sgm in range(nseg):
            lo = sgm * SEG - HALO
            hi = (sgm + 1) * SEG + HALO
            xs = np.zeros((NT, D), np.float32)
            ps = np.zeros((2, NT, 256), np.float32)
            a, bnd = max(lo, 0), min(hi, S)
            xs[a - lo:bnd - lo] = x[b, a:bnd]
            ps[:, a - lo:bnd - lo] = pp[:, b, a:bnd]
            m = {"x": xs, "p": ps}
            m.update(wts)
            in_maps.append(m)
    res = run_bass_kernel_spmd(nc, in_maps, core_ids=list(range(ncores)))
    out = np.empty((Bn, S, D), np.float32)
    i = 0
    for b in range(Bn):
        for sgm in range(nseg):
            out[b, sgm * SEG:(sgm + 1) * SEG] = res.results[i]["out"]
            i += 1
    return out
```

```python
import contextlib
import numpy as np
import concourse.bass as bass
import concourse.mybir as mybir
from concourse.bass_utils import run_bass_kernel_spmd

F32 = mybir.dt.float32
BF16 = mybir.dt.bfloat16
AF = mybir.ActivationFunctionType
ALU = mybir.AluOpType
AX = mybir.AxisListType

ENGINES = ("pe", "act", "dve", "pool", "sp")
N_DSEM = 40

D = 1024
NIN = 8192
FH = 2816
FC = FH // 128
EPS = 1e-6
OQ, OFF, OFB, OI, OG, OU, OV, OGA, OGB = 0, 1024, 2048, 3072, 4096, 5120, 5632, 6144, 7168


class Buf:
    __slots__ = ("name", "w", "r")

    def __init__(self, name):
        self.name = name
        self.w = None
        self.r = []


class Op:
    __slots__ = ("eng", "fn", "deps", "sig", "sigval", "dma", "key", "flushed", "sem")

    def __init__(self, eng, fn, dma, key):
        self.eng = eng
        self.fn = fn
        self.deps = []
        self.sig = False
        self.sigval = 0
        self.dma = dma
        self.key = key
        self.flushed = False


def _flat(xs):
    out = []
    for x in xs:
        if isinstance(x, (list, tuple)):
            out.extend(_flat(x))
        elif x is not None:
            out.append(x)
    return out


class Prog:
    def __init__(self, nc, st):
        self.nc = nc
        self.ops = {e: [] for e in ENGINES}
        self.nbuf = 0
        self.sweep = 0
        self.esem = [{e: st.enter_context(nc.semaphore(f"s{k}_{e}")) for e in ENGINES} for k in range(2)]
        self.dsem = [[st.enter_context(nc.semaphore(f"d{k}_{i}")) for i in range(N_DSEM)] for k in range(2)]
        self.ninst = 0

    def buf(self, name=None):
        self.nbuf += 1
        return Buf(name or f"b{self.nbuf}")

    def bufs(self, n, name="b"):
        return [self.buf(f"{name}{i}") for i in range(n)]

    def op(self, eng, fn, reads=(), writes=(), dma=False):
        reads = _flat(reads)
        writes = _flat(writes)
        key = None
        if dma:
            key = writes[0]
        o = Op(eng, fn, dma, key)
        deps = []
        for b in reads:
            if b.w is not None:
                deps.append(b.w)
        for b in writes:
            if b.w is not None:
                deps.append(b.w)
            deps.extend(b.r)
        for b in reads:
            b.r.append(o)
        for b in writes:
            b.w = o
            b.r = []
        seen = set()
        for d in deps:
            if d is o or id(d) in seen or d.flushed:
                continue
            seen.add(id(d))
            if d.eng == "pe" and eng == "pe" and not d.dma:
                continue
            o.deps.append(d)
            d.sig = True
        self.ops[eng].append(o)
        return o

    def pe(self, fn, reads=(), writes=()):
        return self.op("pe", fn, reads, writes)

    def act(self, fn, reads=(), writes=()):
        return self.op("act", fn, reads, writes)

    def dve(self, fn, reads=(), writes=()):
        return self.op("dve", fn, reads, writes)

    def pool(self, fn, reads=(), writes=()):
        return self.op("pool", fn, reads, writes)

    def dma(self, fn, reads=(), writes=()):
        return self.op("sp", fn, reads, writes, dma=True)

    def flush(self):
        nc = self.nc
        k = self.sweep % 2
        esem = self.esem[k]
        dpool = self.dsem[k]
        other_e = self.esem[1 - k]
        other_d = self.dsem[1 - k]
        dma_keys = {}
        nsem = [0]
        allsems = []
        for e in ENGINES:
            cnt = 0
            for o in self.ops[e]:
                if o.dma:
                    kk = id(o.key)
                    if kk not in dma_keys or dma_keys[kk][1] + 16 > 224:
                        assert nsem[0] < N_DSEM, "too many DMA semaphores in one sweep"
                        dma_keys[kk] = [dpool[nsem[0]], 0]
                        allsems.append(dma_keys[kk])
                        nsem[0] += 1
                    dma_keys[kk][1] += 16
                    o.sigval = dma_keys[kk][1]
                    o.sem = dma_keys[kk][0]
                elif o.sig:
                    cnt += 1
                    o.sigval = cnt
        ops = self.ops
        first = self.sweep == 0

        def body(ename):
            def f(eng):
                waited = {}

                def wait_for(d):
                    s = d.sem if d.dma else esem[d.eng]
                    if waited.get(id(s), 0) >= d.sigval:
                        return
                    waited[id(s)] = d.sigval
                    eng.wait_ge(s, d.sigval)

                if ename == "sp" and not first:
                    for s in list(other_e.values()) + list(other_d):
                        eng.sem_clear(s)
                for o in ops[ename]:
                    for d in o.deps:
                        wait_for(d)
                    ins = o.fn(eng)
                    self.ninst += 1
                    if o.dma:
                        ins.then_inc(o.sem, 16)
                    elif o.sig:
                        ins.then_inc(esem[ename], 1)
                if ename == "sp":
                    for s, tot in allsems:
                        if tot > 0:
                            eng.wait_ge(s, tot)
            return f

        with nc.allow_low_precision(reason="bf16 matmul operands, fp32 accumulation"), nc.Block() as block:
            block.tensor(body("pe"))
            block.scalar(body("act"))
            block.vector(body("dve"))
            block.gpsimd(body("pool"))
            block.sync(body("sp"))
        for e in ENGINES:
            for o in self.ops[e]:
                o.flushed = True
                o.fn = None
        self.ops = {e: [] for e in ENGINES}
        self.sweep += 1


def MM(out, lhsT, rhs, start=True, stop=True):
    return lambda e: e.matmul(out, lhsT=lhsT, rhs=rhs, start=start, stop=stop)


def TR(out, in_, ident):
    return lambda e: e.transpose(out, in_, ident)


def ACT(out, in_, func, bias=None, scale=None, accum=None):
    kw = {}
    if bias is not None:
        kw["bias"] = bias
    if scale is not None:
        kw["scale"] = scale
    if accum is not None:
        kw["accum_out"] = accum
    return lambda e: e.activation(out, in_, func, **kw)


def CP(out, in_):
    return lambda e: e.tensor_copy(out, in_)


def ACP(out, in_):
    return lambda e: e.copy(out, in_)


def TT(out, a, b, op):
    return lambda e: e.tensor_tensor(out=out, in0=a, in1=b, op=op)


def TS(out, a, s1, s2, op0, op1):
    return lambda e: e.tensor_scalar(out=out, in0=a, scalar1=s1, scalar2=s2, op0=op0, op1=op1)


def STT(out, a, s, b, op0, op1):
    return lambda e: e.scalar_tensor_tensor(out=out, in0=a, scalar=s, in1=b, op0=op0, op1=op1)


def RECIP(out, in_):
    return lambda e: e.reciprocal(out, in_)


def MSET(ap, v):
    return lambda e: e.memset(ap, v)


def DMA(out, in_, slow=False):
    if slow:
        return lambda e: e.dma_start(out=out, in_=in_, allow_slow_non_contiguous=True)
    return lambda e: e.dma_start(out=out, in_=in_)


_UID = {"n": 0}


def uniq(name):
    _UID["n"] += 1
    return f"{name}_{_UID['n']}"


class Cfg:
    def __init__(self, NT, out_lo, out_n, debug=False, stop_after=None):
        self.stop_after = stop_after
        self.NT = NT
        self.out_lo = out_lo
        self.out_n = out_n
        self.debug = debug


def build(cfg):
    NT = cfg.NT
    NS = NT // 512
    nc = bass.Bass("TRN2", target_bir_lowering=False)

    def din(name, shape):
        return nc.dram_tensor(name, shape, F32, kind="ExternalInput").ap()

    x_in = din("x", [NT, D])
    p_in = din("p", [2, NT, 256])
    W = {}
    for nm, shp in [("norm_mix_pre", [2, D]), ("w_in", [2, D, NIN]), ("lb_gamma_fwd", [2, D]),
                    ("lb_gamma_bwd", [2, D]), ("hg_norm", [2, D]), ("sg_w", [2, 8, 128, 128]),
                    ("sg_b", [2, 8, 128]), ("sg_ln_g", [2, 512]), ("sg_ln_b", [2, 512]),
                    ("w_a", [2, D, D]), ("w_b", [2, 512, D]), ("w_out", [2, D, D]),
                    ("norm_mix_post", [2, D]), ("norm_ffn_pre", [2, D]), ("w_gate", [2, D, FH]),
                    ("w_up", [2, D, FH]), ("w_down", [2, FH, D]), ("norm_ffn_post", [2, D]),
                    ("w_ple", [2, 256, D]), ("w_ple_gate", [2, D, D])]:
        W[nm] = din(nm, shp)
    out = nc.dram_tensor("out", [cfg.out_n, D], F32, kind="ExternalOutput").ap()

    okind = "ExternalOutput" if cfg.debug else "Internal"

    def dscr(name, shape, dt):
        return nc.dram_tensor(name, shape, dt, kind=okind).ap()

    hT_st = dscr("hT_st", [NS, 128, 8 * 512], BF16)
    qT_st = dscr("qT_st", [NS, 128, 8 * 512], BF16)
    v_st = dscr("v_st", [NS, 128, 4 * 1024], BF16)
    of_st = dscr("of_st", [NS, 128, 4 * 1024], F32)
    mAT_st = dscr("mAT_st", [NS, 128, 8 * 512], BF16)
    xmid = dscr("xmid", [NT, D], F32)
    xa = dscr("xa", [NT, D], F32)
    x1 = dscr("x1", [NT, D], F32)

    with contextlib.ExitStack() as gst:
        p = Prog(nc, gst)

        def GT(name, shape, dt):
            return gst.enter_context(nc.sbuf_tensor(uniq(name), shape, dt))

        identf = GT("identf", [128, 128], F32)
        identb = GT("identb", [128, 128], BF16)
        scanm = GT("scanm", [128, 512], F32)
        maskf = GT("maskf", [128, 8, 64], F32)
        maskb = GT("maskb", [128, 8, 64], F32)
        ones1 = GT("ones1", [128, 1], F32)
        mhalf = GT("mhalf", [128, 8], F32)
        lbt = GT("lbt", [128, 2, 2, 8], F32)
        omlt = GT("omlt", [128, 2, 2, 8], F32)
        nomlt = GT("nomlt", [128, 2, 2, 8], F32)
        gam = GT("gam", [128, 2, 2, 8], F32)
        cB = p.buf("consts")

        p.pool(MSET(identf[:], 0.0), writes=[cB])
        p.pool(lambda e: e.affine_select(out=identf[:], in_=identf[:], pattern=[[-1, 128]],
                                         compare_op=ALU.not_equal, fill=1.0, base=0,
                                         channel_multiplier=1), reads=[cB], writes=[cB])
        p.dve(CP(identb[:], identf[:]), reads=[cB], writes=[cB])
        p.pool(MSET(scanm[:], 1.0), writes=[cB])
        p.pool(MSET(scanm[:].rearrange("p (c t) -> p c t", t=64)[:, :, 0:1], 0.0), writes=[cB])
        p.pool(MSET(ones1[:], 1.0), writes=[cB])
        p.pool(MSET(mhalf[:], -0.5), writes=[cB])
        p.pool(MSET(maskf[:], 1.0), writes=[cB])
        p.pool(MSET(maskb[:], 1.0), writes=[cB])
        for lo in (0, 64):
            p.pool(lambda e, lo=lo: e.affine_select(out=maskf[lo:lo + 64], in_=maskf[lo:lo + 64],
                                                    pattern=[[0, 8], [1, 64]], compare_op=ALU.is_ge,
                                                    fill=0.0, base=0, channel_multiplier=-1),
                   reads=[cB], writes=[cB])
            p.pool(lambda e, lo=lo: e.affine_select(out=maskb[lo:lo + 64], in_=maskb[lo:lo + 64],
                                                    pattern=[[0, 8], [-1, 64]], compare_op=ALU.is_ge,
                                                    fill=0.0, base=0, channel_multiplier=1),
                   reads=[cB], writes=[cB])
        gB = p.buf("gam")
        for di, nm in enumerate(("lb_gamma_fwd", "lb_gamma_bwd")):
            for l in range(2):
                p.dma(DMA(gam[:, di, l, :], W[nm][l].rearrange("(h p) -> p h", p=128), slow=True), writes=[gB])
        p.act(ACT(gam[:], gam[:], AF.Exp), reads=[gB], writes=[gB])
        for di in range(2):
            p.dve(TT(omlt[:, di, 0, :], gam[:, di, 0, :], gam[:, di, 1, :], ALU.add), reads=[gB], writes=[cB])
            p.dve(RECIP(omlt[:, di, 0, :], omlt[:, di, 0, :]), reads=[cB], writes=[cB])
            p.dve(TT(lbt[:, di, 1, :], gam[:, di, 1, :], omlt[:, di, 0, :], ALU.mult), reads=[gB, cB], writes=[cB])
            p.dve(MSET(lbt[:, di, 0, :], 0.0), reads=[cB], writes=[cB])
        p.dve(TS(omlt[:], lbt[:], -1.0, 1.0, ALU.mult, ALU.add), reads=[cB], writes=[cB])
        p.dve(TS(nomlt[:], omlt[:], -1.0, None, ALU.mult, ALU.bypass), reads=[cB], writes=[cB])
        p.flush()

        wl_state = {"i": 0}

        def load_rowscale(st, name, dram_vec, kc_n):
            t = st.enter_context(nc.sbuf_tensor(uniq(name), [128, kc_n], F32))
            b = p.buf(name)
            p.dma(DMA(t[:], dram_vec.rearrange("(c p) -> p c", p=128), slow=True), writes=[b])
            return t, b

        def load_bcast(st, name, dram_vec, n):
            t = st.enter_context(nc.sbuf_tensor(uniq(name), [128, n], F32))
            b = p.buf(name)
            p.dma(DMA(t[:], dram_vec.partition_broadcast(128)), writes=[b])
            return t, b

        def make_stage(st):
            stg = [st.enter_context(nc.sbuf_tensor(uniq(f"wstg{i}"), [128, 2048], F32)) for i in range(4)]
            return stg, p.bufs(4, "wstg")

        def load_w(stage, wd, kc_n, c0, ncols, dst, dc0, dstB, scale=None, scaleB=None):
            stg, sB = stage
            for kc in range(kc_n):
                for c in range(0, ncols, 2048):
                    n = min(2048, ncols - c)
                    i = wl_state["i"]
                    wl_state["i"] += 1
                    s = i % 4
                    p.dma(DMA(stg[s][:, :n], wd[kc * 128:(kc + 1) * 128, c0 + c:c0 + c + n]), writes=[sB[s]])
                    o_ap = dst[:, kc, dc0 + c:dc0 + c + n]
                    eng = ("dve", "act", "dve", "act", "pool")[i % 5]
                    if scale is None:
                        if eng == "act":
                            p.act(ACP(o_ap, stg[s][:, :n]), reads=[sB[s]], writes=[dstB])
                        else:
                            p.op(eng, CP(o_ap, stg[s][:, :n]), reads=[sB[s]], writes=[dstB])
                    else:
                        sc = scale[:, kc:kc + 1]
                        if eng == "act":
                            p.act(ACT(o_ap, stg[s][:, :n], AF.Copy, scale=sc), reads=[sB[s], scaleB], writes=[dstB])
                        else:
                            p.op(eng, TS(o_ap, stg[s][:, :n], sc, 0.0, ALU.mult, ALU.add),
                                 reads=[sB[s], scaleB], writes=[dstB])

        def proj_fm(ps, w, c0, inT, kc_n, tok0, ntok, rd, wr):
            for kc in range(kc_n):
                p.pe(MM(ps, w[:, kc, c0:c0 + 128], inT[:, kc, tok0:tok0 + ntok], kc == 0, kc == kc_n - 1),
                     reads=rd, writes=wr)

        def proj_tm(ps, inT, tok0, w, c0, ncols, kc_n, rd, wr):
            for kc in range(kc_n):
                p.pe(MM(ps, inT[:, kc, tok0:tok0 + 128], w[:, kc, c0:c0 + ncols], kc == 0, kc == kc_n - 1),
                     reads=rd, writes=wr)

        def sigmoid_from_psum(ps, psB, tmp, tmpB, out_ap, outB):
            p.act(ACT(tmp, ps, AF.Exp, scale=-1.0), reads=[psB], writes=[tmpB])
            p.act(ACT(tmp, tmp, AF.Ln, bias=1.0), reads=[tmpB], writes=[tmpB])
            p.act(ACT(out_ap, tmp, AF.Exp, scale=-1.0), reads=[tmpB], writes=[outB])

        def rms_rstd(ss, ssB, ncol, n, rstd, rstdB):
            p.dve(TS(ss, ss, 1.0 / n, EPS, ALU.mult, ALU.add), reads=[ssB], writes=[ssB])
            p.pool(TT(rstd, ss, mhalf[:, 0:ncol], ALU.pow), reads=[ssB, cB], writes=[rstdB])

        def norm_transpose(st_tiles, xt, xtB, ntile, hT, hTB, psT, psTB):
            junk, junkB, ss, ssB, rstd, rstdB, xs, xsB = st_tiles
            for t in range(ntile):
                p.act(ACT(junk[:], xt[:, t, :], AF.Square, accum=ss[:, t:t + 1]), reads=[xtB], writes=[junkB, ssB])
            rms_rstd(ss[:, 0:ntile], ssB, ntile, D, rstd[:, 0:ntile], rstdB)
            for t in range(ntile):
                if t % 2 == 0:
                    p.act(ACT(xs[:, t, :], xt[:, t, :], AF.Copy, scale=rstd[:, t:t + 1]), reads=[xtB, rstdB], writes=[xsB[t]])
                else:
                    p.dve(TS(xs[:, t, :], xt[:, t, :], rstd[:, t:t + 1], 0.0, ALU.mult, ALU.add),
                          reads=[xtB, rstdB], writes=[xsB[t]])
            for kc in range(8):
                s = kc % 2
                for t in range(ntile):
                    p.pe(TR(psT[s][:, t * 128:(t + 1) * 128], xs[:, t, kc * 128:(kc + 1) * 128], identb[:]),
                         reads=[xsB[t], cB], writes=[psTB[s]])
                if kc % 2 == 0:
                    p.dve(CP(hT[:, kc, 0:ntile * 128], psT[s][:, 0:ntile * 128]), reads=[psTB[s]], writes=[hTB])
                else:
                    p.act(ACP(hT[:, kc, 0:ntile * 128], psT[s][:, 0:ntile * 128]), reads=[psTB[s]], writes=[hTB])

        def post_norm_residual(psX, psXB, sq, sqB, ss, ssB, rstd, rstdB, gbc, gbcB, xres, xresB, yout, youtB, tmp, tmpB):
            for hf in range(2):
                p.act(ACT(sq[:, 0:512], psX[hf], AF.Square, accum=ss[:, hf:hf + 1]), reads=[psXB[hf]], writes=[sqB, ssB])
            p.dve(TT(ss[:, 0:1], ss[:, 0:1], ss[:, 1:2], ALU.add), reads=[ssB], writes=[ssB])
            rms_rstd(ss[:, 0:1], ssB, 1, D, rstd[:, 0:1], rstdB)
            for hf in range(2):
                sl = slice(hf * 512, (hf + 1) * 512)
                p.dve(STT(tmp[:, sl], psX[hf], rstd[:, 0:1], gbc[:, sl], ALU.mult, ALU.mult),
                      reads=[psXB[hf], rstdB, gbcB], writes=[tmpB])
                p.pool(TT(yout[:, sl], tmp[:, sl], xres[:, sl], ALU.add), reads=[tmpB, xresB], writes=[youtB])

        def sweep_hgrn(l, rev):
            x_src = x_in if l == 0 else x1
            di = 1 if rev else 0
            with contextlib.ExitStack() as st:
                def T(name, shape, dt):
                    return st.enter_context(nc.sbuf_tensor(uniq(name), shape, dt))

                def PS(name, shape, dt):
                    return st.enter_context(nc.psum_tensor(uniq(name), shape, dt))

                wl = W["w_in"][l]
                gpre, gpreB = load_rowscale(st, "gpre", W["norm_mix_pre"][l], 8)
                wr = T("wr", [128, 8, 3072], BF16)
                wrB = p.buf("wr")
                if rev:
                    gh, ghB = load_rowscale(st, "gh", W["hg_norm"][l], 8)
                    wa = T("wa", [128, 8, 1024], BF16)
                    waB = p.buf("wa")
                with contextlib.ExitStack() as wst:
                    stage = make_stage(wst)
                    if not rev:
                        load_w(stage, wl, 8, OQ, 1024, wr, 0, wrB, gpre, gpreB)
                        load_w(stage, wl, 8, OFF, 1024, wr, 1024, wrB, gpre, gpreB)
                        load_w(stage, wl, 8, OI, 1024, wr, 2048, wrB, gpre, gpreB)
                    else:
                        load_w(stage, wl, 8, OFB, 1024, wr, 0, wrB, gpre, gpreB)
                        load_w(stage, wl, 8, OG, 1024, wr, 1024, wrB, gpre, gpreB)
                        load_w(stage, wl, 8, OGA, 1024, wr, 2048, wrB, gpre, gpreB)
                        load_w(stage, W["w_a"][l], 8, 0, 1024, wa, 0, waB, gh, ghB)
                    p.flush()
                if not rev:
                    cQ, cF, cI = 0, 1024, 2048
                else:
                    cF, cG, cGA = 0, 1024, 2048

                hT = T("hT", [128, 8, 512], BF16); hTB = p.buf("hT")
                vtm = [T(f"vtm{i}", [128, 4, 1024], BF16) for i in range(2)]; vtmB = [p.bufs(4, f"vtm{i}_") for i in range(2)]
                qtT = [T(f"qtT{i}", [128, 8, 512], BF16) for i in range(2)]; qtTB = [p.bufs(8, f"qtT{i}_") for i in range(2)]
                ktT = [T(f"ktT{i}", [128, 8, 512], BF16) for i in range(2)]; ktTB = [p.bufs(8, f"ktT{i}_") for i in range(2)]
                dsv = [T(f"dsv{i}", [128, 8, 8], F32) for i in range(2)]; dsvB = [p.bufs(8, f"dsv{i}_") for i in range(2)]
                tE = [T(f"tE{i}", [128, 512], F32) for i in range(2)]; tEB = p.bufs(2, "tE")
                tS = [T(f"tS{i}", [128, 512], F32) for i in range(2)]; tSB = p.bufs(2, "tS")
                tEb = T("tEb", [128, 512], F32); tEbB = p.buf("tEb")
                tL = T("tL", [128, 512], F32); tLB = p.buf("tL")
                tK = T("tK", [128, 512], F32); tKB = p.buf("tK")
                tB = T("tB", [128, 512], F32); tBB = p.buf("tB")
                tN = T("tN", [128, 512], F32); tNB = p.buf("tN")
                ktm = T("ktm", [128, 8, 128], BF16); ktmB = p.bufs(2, "ktm")
                PT = T("PT", [128, 8, 64], BF16); PTB = p.buf("PT")
                Sp = T("Sp", [128, 8, 128], F32); SpB = p.bufs(8, "Sp")
                Sbf = T("Sbf", [128, 8, 128], BF16); SbfB = p.bufs(8, "Sbf")

                psP = [PS(f"psP{i}", [128, 512], F32) for i in range(2)]; psPB = p.bufs(2, "psP")
                psS = PS("psS", [128, 8, 64], F32); psSB = p.buf("psS")
                psO = PS("psO", [128, 8, 128], F32); psOB = p.buf("psO")
                psOf = psO[:].rearrange("p h v -> p (h v)")
                psM = PS("psM", [128, 8, 128], F32); psMB = [p.buf("psMa")] * 4 + [p.buf("psMb")] * 4
                psK = PS("psK", [128, 8, 128], BF16); psKB = [p.buf("psK")] * 2
                pp = {"i": 0}

                def next_ps():
                    i = pp["i"] % 2
                    pp["i"] += 1
                    return psP[i][:], psPB[i]

                if not rev:
                    xt = [T(f"xt{i}", [128, 4, D], F32) for i in range(2)]; xtB = p.bufs(2, "xt")
                    junk = T("junk", [128, D], BF16); junkB = p.buf("junk")
                    ss = T("ss", [128, 4], F32); ssB = p.buf("ss")
                    rstd = T("rstd", [128, 4], F32); rstdB = p.buf("rstd")
                    xs = T("xs", [128, 4, D], BF16); xsB = p.bufs(4, "xs")
                    osb = [T(f"osb{i}", [128, 1024], F32) for i in range(2)]; osbB = p.bufs(2, "osb")
                    psKf = psK[:].rearrange("p h k -> p (h k)")
                    psT = [psKf[:, 0:512], psKf[:, 512:1024]]; psTB = [psKB[0]] * 2
                    hTstB, qTstB, vstB, ofstB = p.buf("hTst"), p.buf("qTst"), p.buf("vst"), p.buf("ofst")
                else:
                    oft = [T(f"oft{i}", [128, 1024], F32) for i in range(2)]; oftB = p.bufs(2, "oft")
                    gT = T("gT", [128, 8, 512], BF16); gTB = p.bufs(8, "gT")
                    sga = T("sga", [128, 8, 512], BF16); sgaB = p.bufs(8, "sga")
                    AT = T("AT", [128, 8, 512], BF16); ATB = p.bufs(4, "AT")
                    mATr = [T(f"mATr{i}", [128, 512], BF16) for i in range(2)]; mATrB = p.bufs(2, "mATr")
                    osum = T("osum", [128, 1024], F32); osumB = p.buf("osum")
                    sq = T("sq", [128, 1024], F32); sqB = p.buf("sq")
                    ss8 = T("ss8", [128, 8], F32); ss8B = p.buf("ss8")
                    rs8 = T("rs8", [128, 8], F32); rs8B = p.buf("rs8")
                    on = [T(f"on{i}", [128, 8, 128], BF16) for i in range(2)]; onB = p.bufs(2, "on")
                    psOT = psK; psOTB = psKB[0]
                    mATstB = p.buf("mATst")

                for h in range(8):
                    p.pool(MSET(Sp[:, h, :], 0.0), writes=[SpB[h]])
                    p.pool(MSET(Sbf[:, h, :], 0.0), writes=[SbfB[h]])
                mask = maskb if rev else maskf

                order = list(range(NS))
                if rev:
                    order = order[::-1]

                def xload(it):
                    j = order[it]
                    p.dma(DMA(xt[it % 2][:], x_src[j * 512:(j + 1) * 512, :].rearrange("(t p) d -> p t d", p=128)),
                          writes=[xtB[it % 2]])

                def gate_head(b, h):
                    s = h % 2
                    ps, psB_ = next_ps()
                    proj_fm(ps, wr, cF + h * 128, hT, 8, 0, 512, [wrB, hTB], [psB_])
                    sigmoid_from_psum(ps, psB_, tE[s][:], tEB[s], tS[s][:], tSB[s])
                    lb_c = lbt[:, di, l, h:h + 1]
                    oml_c = omlt[:, di, l, h:h + 1]
                    noml_c = nomlt[:, di, l, h:h + 1]
                    p.act(ACT(tL[:], tS[s][:], AF.Ln, bias=lb_c, scale=oml_c), reads=[tSB[s], cB], writes=[tLB])
                    p.dve(TS(tK[:], tS[s][:], noml_c, oml_c, ALU.mult, ALU.add), reads=[tSB[s], cB], writes=[tKB])
                    p.dve(lambda e: e.tensor_tensor_scan(out=tB[:], data0=scanm[:], data1=tL[:], initial=0.0,
                                                         op0=ALU.mult, op1=ALU.add),
                          reads=[tLB, cB], writes=[tBB])
                    if rev:
                        p.dve(TT(tL[:], tL[:], tB[:], ALU.subtract), reads=[tLB, tBB], writes=[tLB])
                        tot = tB[:].rearrange("p (c t) -> p c t", t=64)[:, :, 63:64].to_broadcast([128, 8, 64])
                        p.dve(TT(tL[:].rearrange("p (c t) -> p c t", t=64), tL[:].rearrange("p (c t) -> p c t", t=64),
                                 tot, ALU.add), reads=[tLB, tBB], writes=[tLB])
                        bsrc, bsrcB = tL, tLB
                    else:
                        bsrc, bsrcB = tB, tBB
                    p.act(ACT(tEb[:], bsrc[:], AF.Exp), reads=[bsrcB], writes=[tEbB])
                    p.act(ACT(tN[:], bsrc[:], AF.Exp, scale=-1.0), reads=[bsrcB], writes=[tNB])
                    dc_ = 0 if rev else 63
                    p.pool(CP(dsv[b][:, h, :], tEb[:].rearrange("p (c t) -> p c t", t=64)[:, :, dc_]),
                           reads=[tEbB], writes=[dsvB[b][h]])
                    p.dve(TT(qtT[b][:, h, :], qtT[b][:, h, :], tEb[:], ALU.mult), reads=[qtTB[b][h], tEbB], writes=[qtTB[b][h]])
                    p.dve(TT(ktT[b][:, h, :], tK[:], tN[:], ALU.mult), reads=[tKB, tNB], writes=[ktTB[b][h]])

                def front(it):
                    b = it % 2
                    j = order[it]
                    if not rev:
                        if it + 1 < NS:
                            xload(it + 1)
                        norm_transpose((junk, junkB, ss, ssB, rstd, rstdB, xs, xsB), xt[b], xtB[b], 4, hT, hTB, psT, psTB)
                        p.dma(DMA(hT_st[j], hT[:].rearrange("p k t -> p (k t)")), reads=[hTB], writes=[hTstB])
                        yield
                        for h in range(8):
                            ps, psB_ = next_ps()
                            proj_fm(ps, wr, cQ + h * 128, hT, 8, 0, 512, [wrB, hTB], [psB_])
                            s = h % 2
                            sigmoid_from_psum(ps, psB_, tE[s][:], tEB[s], tS[s][:], tSB[s])
                            p.dve(TT(qtT[b][:, h, :], ps, tS[s][:], ALU.mult), reads=[psB_, tSB[s]], writes=[qtTB[b][h]])
                            yield
                        p.dma(DMA(qT_st[j], qtT[b][:].rearrange("p k t -> p (k t)")), reads=qtTB[b], writes=[qTstB])
                        for t in range(4):
                            for hf in range(2):
                                ps, psB_ = next_ps()
                                proj_tm(ps, hT, t * 128, wr, cI + hf * 512, 512, 8, [wrB, hTB], [psB_])
                                if hf == 0:
                                    p.act(ACP(vtm[b][:, t, 0:512], ps), reads=[psB_], writes=[vtmB[b][t]])
                                else:
                                    p.dve(CP(vtm[b][:, t, 512:1024], ps), reads=[psB_], writes=[vtmB[b][t]])
                            yield
                        p.dma(DMA(v_st[j], vtm[b][:].rearrange("p t d -> p (t d)")), reads=vtmB[b], writes=[vstB])
                        for h in range(8):
                            gate_head(b, h)
                            yield
                    else:
                        p.dma(DMA(hT[:].rearrange("p k t -> p (k t)"), hT_st[j]), writes=[hTB])
                        p.dma(DMA(qtT[b][:].rearrange("p k t -> p (k t)"), qT_st[j]), writes=qtTB[b])
                        p.dma(DMA(vtm[b][:].rearrange("p t d -> p (t d)"), v_st[j]), writes=vtmB[b])
                        yield
                        for h in range(8):
                            gate_head(b, h)
                            yield

                def frontB(it):
                    if True:
                        for h in range(8):
                            ps, psB_ = next_ps()
                            proj_fm(ps, wr, cG + h * 128, hT, 8, 0, 512, [wrB, hTB], [psB_])
                            s = h % 2
                            sigmoid_from_psum(ps, psB_, tE[s][:], tEB[s], tS[s][:], tSB[s])
                            p.dve(TT(gT[:, h, :], ps, tS[s][:], ALU.mult), reads=[psB_, tSB[s]], writes=[gTB[h]])
                            yield
                        for h in range(8):
                            ps, psB_ = next_ps()
                            proj_fm(ps, wr, cGA + h * 128, hT, 8, 0, 512, [wrB, hTB], [psB_])
                            s = h % 2
                            sigmoid_from_psum(ps, psB_, tE[s][:], tEB[s], sga[:, h, :], sgaB[h])
                            yield

                def pump(gens, n):
                    for _ in range(n):
                        done = False
                        for g in gens:
                            try:
                                next(g)
                                done = True
                                break
                            except StopIteration:
                                continue
                        if not done:
                            return

                def drain(gen):
                    if gen is None:
                        return
                    for _ in gen:
                        pass

                state = {"dprev": [ones1[:, 0:1]] * 8, "dprevB": [cB] * 8, "oi": 0}

                def chain2(g1, g2):
                    if g1 is not None:
                        yield from g1
                    if g2 is not None:
                        yield from g2

                def emit_pending():
                    if state.get("pend") is None:
                        return
                    t_, o_ = state["pend"]
                    state["pend"] = None
                    tk_ = slice(t_ * 128, (t_ + 1) * 128)
                    for h in range(8):
                        p.pe(TR(psOT[:, h, :], on[o_][:, h, :], identb[:]), reads=[onB[o_], cB], writes=[psOTB])
                    p.act(ACP(AT[:, :, tk_], psOT[:]), reads=[psOTB], writes=[ATB[t_]])

                def back(it, genB, genA):
                    gen = [g for g in (genB, genA) if g is not None]
                    b = it % 2
                    j = order[it]
                    dprev, dprevB = state["dprev"], state["dprevB"]
                    tiles = [3, 2, 1, 0] if rev else [0, 1, 2, 3]
                    chunks = [1, 0] if rev else [0, 1]
                    for t in tiles:
                        tk = slice(t * 128, (t + 1) * 128)
                        if rev:
                            ofs = state["oi"] % 2
                            state["oi"] += 1
                            p.dma(DMA(oft[ofs][:], of_st[j][:, t * 1024:(t + 1) * 1024]), writes=[oftB[ofs]])
                        for half in range(2):
                            for h in range(half * 4, half * 4 + 4):
                                p.pe(TR(psK[:, h, :], ktT[b][:, h, tk], identb[:]), reads=[ktTB[b][h], cB], writes=[psKB[half]])
                            if half == 0:
                                p.act(ACP(ktm[:, 0:4, :], psK[:, 0:4, :]), reads=[psKB[0]], writes=[ktmB[0]])
                            else:
                                p.dve(CP(ktm[:, 4:8, :], psK[:, 4:8, :]), reads=[psKB[1]], writes=[ktmB[1]])
                        for c in range(2):
                            ck = slice(t * 128 + c * 64, t * 128 + c * 64 + 64)
                            for h in range(8):
                                p.pe(MM(psS[c * 64:(c + 1) * 64, h, :], ktT[b][:, h, ck], qtT[b][:, h, ck]),
                                     reads=[ktTB[b][h], qtTB[b][h]], writes=[psSB])
                        p.dve(TT(PT[:], psS[:], mask[:], ALU.mult), reads=[psSB, cB], writes=[PTB])
                        if rev:
                            emit_pending()
                        for c in chunks:
                            pr = slice(c * 64, (c + 1) * 64)
                            ck = slice(t * 128 + c * 64, t * 128 + c * 64 + 64)
                            cidx = t * 2 + c
                            for h in range(8):
                                vs = vtm[b][pr, t, h * 128:(h + 1) * 128]
                                p.pe(MM(psO[pr, h, :], qtT[b][:, h, ck], Sbf[:, h, :], True, False),
                                     reads=[qtTB[b][h], SbfB[h]], writes=[psOB])
                                p.pe(MM(psO[pr, h, :], PT[pr, h, :], vs, False, True),
                                     reads=[PTB, vtmB[b][t]], writes=[psOB])
                            for half in range(2):
                                for h in range(half * 4, half * 4 + 4):
                                    vs = vtm[b][pr, t, h * 128:(h + 1) * 128]
                                    p.pe(MM(psM[:, h, :], ktm[pr, h, :], vs), reads=[ktmB[half], vtmB[b][t]], writes=[psMB[h]])
                            for half in range(2):
                                for h in range(half * 4, half * 4 + 4):
                                    p.dve(STT(Sp[:, h, :], Sp[:, h, :], dprev[h], psM[:, h, :], ALU.mult, ALU.add),
                                          reads=[SpB[h], dprevB[h], psMB[h]], writes=[SpB[h]])
                                    dcur = dsv[b][:, h, cidx:cidx + 1]
                                    if h % 2 == 0:
                                        p.act(ACT(Sbf[:, h, :], Sp[:, h, :], AF.Copy, scale=dcur),
                                              reads=[SpB[h], dsvB[b][h]], writes=[SbfB[h]])
                                    else:
                                        p.pool(TS(Sbf[:, h, :], Sp[:, h, :], dcur, 0.0, ALU.mult, ALU.add),
                                               reads=[SpB[h], dsvB[b][h]], writes=[SbfB[h]])
                                    dprev[h] = dcur
                                    dprevB[h] = dsvB[b][h]
                            pump(gen, 3)
                        if not rev:
                            ob = t % 2
                            p.act(ACP(osb[ob][:, 0:512], psOf[:, 0:512]), reads=[psOB], writes=[osbB[ob]])
                            p.dve(CP(osb[ob][:, 512:1024], psOf[:, 512:1024]), reads=[psOB], writes=[osbB[ob]])
                            p.dma(DMA(of_st[j][:, t * 1024:(t + 1) * 1024], osb[ob][:]), reads=[osbB[ob]], writes=[ofstB])
                        else:
                            p.dve(TT(osum[:], psOf, oft[ofs][:], ALU.add), reads=[psOB, oftB[ofs]], writes=[osumB])
                            p.act(ACT(sq[:], osum[:], AF.Square), reads=[osumB], writes=[sqB])
                            p.dve(lambda e: e.tensor_reduce(out=ss8[:], in_=sq[:].rearrange("p (h v) -> p h v", v=128),
                                                            op=ALU.add, axis=AX.X), reads=[sqB], writes=[ss8B])
                            rms_rstd(ss8[:], ss8B, 8, 128, rs8[:], rs8B)
                            p.pool(TT(on[ofs][:], osum[:].rearrange("p (h v) -> p h v", v=128),
                                      rs8[:].unsqueeze(2).to_broadcast([128, 8, 128]), ALU.mult),
                                   reads=[osumB, rs8B], writes=[onB[ofs]])
                            state["pend"] = (t, ofs)
                    if rev:
                        emit_pending()
                    drain(genB)
                    for h in range(8):
                        p.pool(TS(Sp[:, h, :], Sp[:, h, :], dprev[h], 0.0, ALU.mult, ALU.add),
                               reads=[SpB[h], dprevB[h]], writes=[SpB[h]])
                        dprev[h] = ones1[:, 0:1]
                        dprevB[h] = cB
                    if rev:
                        for dc in range(8):
                            if dc % 2 == 0:
                                p.pool(TT(AT[:, dc, :], AT[:, dc, :], gT[:, dc, :], ALU.mult), reads=[ATB, gTB[dc]], writes=[ATB])
                            else:
                                p.dve(TT(AT[:, dc, :], AT[:, dc, :], gT[:, dc, :], ALU.mult), reads=[ATB, gTB[dc]], writes=[ATB])
                        for dc in range(8):
                            ps, psB_ = next_ps()
                            proj_fm(ps, wa, dc * 128, AT, 8, 0, 512, [waB, ATB], [psB_])
                            s = dc % 2
                            p.dve(TT(mATr[s][:], ps, sga[:, dc, :], ALU.mult), reads=[psB_, sgaB[dc]], writes=[mATrB[s]])
                            p.dma(DMA(mAT_st[j][:, dc * 512:(dc + 1) * 512], mATr[s][:]), reads=[mATrB[s]], writes=[mATstB])

                if not rev:
                    xload(0)
                drain(front(0))
                for it in range(NS):
                    genA = front(it + 1) if it + 1 < NS else None
                    genB = frontB(it) if rev else None
                    back(it, genB, genA)
                    drain(genA)
                p.flush()

        def sweep_c(l):
            x_src = x_in if l == 0 else x1
            with contextlib.ExitStack() as st:
                def T(name, shape, dt):
                    return st.enter_context(nc.sbuf_tensor(uniq(name), shape, dt))

                def PS(name, shape, dt):
                    return st.enter_context(nc.psum_tensor(uniq(name), shape, dt))

                gpre, gpreB = load_rowscale(st, "gpre", W["norm_mix_pre"][l], 8)
                wr = T("wr", [128, 8, 2048], BF16); wrB = p.buf("wr")
                wb = T("wb", [128, 4, 1024], BF16); wbB = p.buf("wb")
                wo = T("wo", [128, 8, 1024], BF16); woB = p.buf("wo")
                cU, cV, cGB = 0, 512, 1024
                with contextlib.ExitStack() as wst:
                    stage = make_stage(wst)
                    load_w(stage, W["w_in"][l], 8, OU, 512, wr, 0, wrB, gpre, gpreB)
                    load_w(stage, W["w_in"][l], 8, OV, 512, wr, 512, wrB, gpre, gpreB)
                    load_w(stage, W["w_in"][l], 8, OGB, 1024, wr, 1024, wrB, gpre, gpreB)
                    load_w(stage, W["w_b"][l], 4, 0, 1024, wb, 0, wbB)
                    load_w(stage, W["w_out"][l], 8, 0, 1024, wo, 0, woB)
                    p.flush()
                lng, lngB = load_bcast(st, "lng", W["sg_ln_g"][l], 512)
                lnb, lnbB = load_bcast(st, "lnb", W["sg_ln_b"][l], 512)
                gpo, gpoB = load_bcast(st, "gpo", W["norm_mix_post"][l], 1024)
                wsf = T("wsf", [128, 8, 128], F32); wsfB = p.buf("wsf")
                wsT = T("wsT", [128, 8, 128], BF16); wsTB = p.buf("wsT")
                bsb = T("bsb", [128, 4, 128], F32); bsbB = p.buf("bsb")
                psW = PS("psW", [128, 4, 128], F32); psWB = p.buf("psW")
                p.dma(DMA(wsf[:], W["sg_w"][l].rearrange("g t s -> t g s")), writes=[wsfB])
                for half in range(2):
                    for g in range(half * 4, half * 4 + 4):
                        p.pe(TR(psW[:, g % 4, :], wsf[:, g, :], identf[:]), reads=[wsfB, cB], writes=[psWB])
                    p.dve(CP(wsT[:, half * 4:half * 4 + 4, :], psW[:]), reads=[psWB], writes=[wsTB])
                for g in range(8):
                    p.dma(DMA(bsb[(g % 2) * 64:(g % 2) * 64 + 64, g // 2, :], W["sg_b"][l, g, :].partition_broadcast(64)),
                          writes=[bsbB])

                hT = T("hT", [128, 8, 512], BF16); hTB = p.buf("hT")
                mAT = T("mAT", [128, 8, 512], BF16); mATB = p.buf("mAT")
                xt = T("xt", [128, 4, D], F32); xtB = p.buf("xt")
                uT = T("uT", [128, 4, 512], BF16); uTB = p.bufs(4, "uT")
                gv = T("gv", [128, 512], F32); gvB = p.buf("gv")
                st6 = T("st6", [128, 6], F32); st6B = p.buf("st6")
                mv = T("mv", [128, 2], F32); mvB = p.buf("mv")
                rs = T("rs", [128, 1], F32); rsB = p.buf("rs")
                vh = T("vh", [128, 512], F32); vhB = p.buf("vh")
                vn = T("vn", [128, 512], BF16); vnB = p.buf("vn")
                tg = T("tg", [128, 4, 128], F32); tgB = p.buf("tg")
                BT = T("BT", [128, 4, 512], BF16); BTB = p.bufs(4, "BT")
                tE = [T(f"tE{i}", [128, 512], F32) for i in range(2)]; tEB = p.bufs(2, "tE")
                sgb = T("sgb", [128, 8, 512], BF16); sgbB = p.bufs(8, "sgb")
                tm = [T(f"tm{i}", [128, 512], F32) for i in range(2)]; tmB = p.bufs(2, "tm")
                mg = T("mg", [128, 8, 512], BF16); mgB = p.bufs(8, "mg")
                sq = T("sq", [128, 512], F32); sqB = p.buf("sq")
                ss = T("ss", [128, 2], F32); ssB = p.buf("ss")
                rstd = T("rstd", [128, 1], F32); rstdB = p.buf("rstd")
                tmp = T("tmp", [128, D], F32); tmpB = p.buf("tmp")
                yo = [T(f"yo{i}", [128, D], F32) for i in range(2)]; yoB = p.bufs(2, "yo")

                psP = [PS(f"psP{i}", [128, 512], F32) for i in range(3)]; psPB = p.bufs(3, "psP")
                psG = PS("psG", [128, 4, 128], F32); psGB = p.buf("psG")
                psX = [PS(f"psX{i}", [128, 512], F32) for i in range(2)]; psXB = p.bufs(2, "psX")
                pp = {"i": 0}

                def next_ps():
                    i = pp["i"] % 3
                    pp["i"] += 1
                    return psP[i][:], psPB[i]

                xmidstB = p.buf("xmidst")
                for j in range(NS):
                    p.dma(DMA(hT[:].rearrange("p k t -> p (k t)"), hT_st[j]), writes=[hTB])
                    p.dma(DMA(mAT[:].rearrange("p k t -> p (k t)"), mAT_st[j]), writes=[mATB])
                    p.dma(DMA(xt[:], x_src[j * 512:(j + 1) * 512, :].rearrange("(t p) d -> p t d", p=128)), writes=[xtB])
                    for c in range(4):
                        ps, psB_ = next_ps()
                        proj_fm(ps, wr, cU + c * 128, hT, 8, 0, 512, [wrB, hTB], [psB_])
                        p.act(ACT(uT[:, c, :], ps, AF.Gelu), reads=[psB_], writes=[uTB[c]])
                    for t in range(4):
                        tk = slice(t * 128, (t + 1) * 128)
                        ps, psB_ = next_ps()
                        proj_tm(ps, hT, t * 128, wr, cV, 512, 8, [wrB, hTB], [psB_])
                        p.act(ACT(gv[:], ps, AF.Gelu), reads=[psB_], writes=[gvB])
                        p.dve(lambda e: e.bn_stats(st6[:], gv[:]), reads=[gvB], writes=[st6B])
                        p.dve(lambda e: e.bn_aggr(mv[:], st6[:]), reads=[st6B], writes=[mvB])
                        p.dve(TS(rs[:], mv[:, 1:2], 1.0, EPS, ALU.mult, ALU.add), reads=[mvB], writes=[rsB])
                        p.pool(TT(rs[:], rs[:], mhalf[:, 0:1], ALU.pow), reads=[rsB, cB], writes=[rsB])
                        p.dve(TS(vh[:], gv[:], mv[:, 0:1], rs[:, 0:1], ALU.subtract, ALU.mult), reads=[gvB, mvB, rsB], writes=[vhB])
                        p.pool(TT(vh[:], vh[:], lng[:], ALU.mult), reads=[vhB, lngB], writes=[vhB])
                        p.pool(TT(vn[:], vh[:], lnb[:], ALU.add), reads=[vhB, lnbB], writes=[vnB])
                        for g in range(8):
                            p.pe(MM(psG[(g % 2) * 64:(g % 2) * 64 + 64, g // 2, :], vn[:, g * 64:(g + 1) * 64], wsT[:, g, :]),
                                 reads=[vnB, wsTB], writes=[psGB])
                        p.dve(TT(tg[:], psG[:], bsb[:], ALU.add), reads=[psGB, bsbB], writes=[tgB])
                        p.pool(TT(BT[:, :, tk], tg[:], uT[:, :, tk], ALU.mult), reads=[tgB, uTB], writes=[BTB[t]])
                    for dc in range(8):
                        ps, psB_ = next_ps()
                        proj_fm(ps, wr, cGB + dc * 128, hT, 8, 0, 512, [wrB, hTB], [psB_])
                        s = dc % 2
                        sigmoid_from_psum(ps, psB_, tE[s][:], tEB[s], sgb[:, dc, :], sgbB[dc])
                    for dc in range(8):
                        ps, psB_ = next_ps()
                        proj_fm(ps, wb, dc * 128, BT, 4, 0, 512, [wbB, BTB], [psB_])
                        s = dc % 2
                        p.dve(TT(tm[s][:], ps, sgb[:, dc, :], ALU.mult), reads=[psB_, sgbB[dc]], writes=[tmB[s]])
                        p.pool(TT(mg[:, dc, :], tm[s][:], mAT[:, dc, :], ALU.add), reads=[tmB[s], mATB], writes=[mgB[dc]])
                    for t in range(4):
                        for hf in range(2):
                            proj_tm(psX[hf][:], mg, t * 128, wo, hf * 512, 512, 8, [woB, mgB], [psXB[hf]])
                        ob = t % 2
                        post_norm_residual([psX[0][:], psX[1][:]], psXB, sq, sqB, ss, ssB, rstd, rstdB, gpo, gpoB,
                                           xt[:, t, :], xtB, yo[ob], yoB[ob], tmp, tmpB)
                        p.dma(DMA(xmid[j * 512 + t * 128:j * 512 + (t + 1) * 128, :], yo[ob][:]), reads=[yoB[ob]],
                              writes=[xmidstB])
                p.flush()

        def sweep_d(l):
            TD = 256
            with contextlib.ExitStack() as st:
                def T(name, shape, dt):
                    return st.enter_context(nc.sbuf_tensor(uniq(name), shape, dt))

                def PS(name, shape, dt):
                    return st.enter_context(nc.psum_tensor(uniq(name), shape, dt))

                gpre, gpreB = load_rowscale(st, "gpre", W["norm_ffn_pre"][l], 8)
                wg = T("wg", [128, 8, FH], BF16); wgB = p.buf("wg")
                wu = T("wu", [128, 8, FH], BF16); wuB = p.buf("wu")
                wd = T("wd", [128, FC, D], BF16); wdB = p.buf("wd")
                with contextlib.ExitStack() as wst:
                    stage = make_stage(wst)
                    load_w(stage, W["w_gate"][l], 8, 0, FH, wg, 0, wgB, gpre, gpreB)
                    load_w(stage, W["w_up"][l], 8, 0, FH, wu, 0, wuB, gpre, gpreB)
                    load_w(stage, W["w_down"][l], FC, 0, D, wd, 0, wdB)
                    p.flush()
                gpo, gpoB = load_bcast(st, "gpo", W["norm_ffn_post"][l], 1024)

                xt = [T(f"xt{i}", [128, 2, D], F32) for i in range(2)]; xtB = p.bufs(2, "xt")
                junk = T("junk", [128, D], BF16); junkB = p.buf("junk")
                ss = T("ss", [128, 4], F32); ssB = p.buf("ss")
                rstd = T("rstd", [128, 4], F32); rstdB = p.buf("rstd")
                xs = T("xs", [128, 2, D], BF16); xsB = p.bufs(2, "xs")
                hT = T("hT", [128, 8, TD], BF16); hTB = p.buf("hT")
                sl = [T(f"sl{i}", [128, TD], F32) for i in range(2)]; slB = p.bufs(2, "sl")
                hid = T("hid", [128, FC, TD], BF16); hidB = p.bufs(FC, "hid")
                sq = T("sq", [128, 512], F32); sqB = p.buf("sq")
                ss2 = T("ss2", [128, 2], F32); ss2B = p.buf("ss2")
                rstd2 = T("rstd2", [128, 1], F32); rstd2B = p.buf("rstd2")
                tmp = T("tmp", [128, D], F32); tmpB = p.buf("tmp")
                yo = [T(f"yo{i}", [128, D], F32) for i in range(2)]; yoB = p.bufs(2, "yo")

                psTt = PS("psTt", [128, 2, 512], BF16); psT = [psTt[:, 0, :], psTt[:, 1, :]]; psTB = [p.buf("psT")] * 2
                psA = [PS(f"psA{i}", [128, 512], F32)[:, 0:TD] for i in range(2)]; psAB = p.bufs(2, "psA")
                psU = [PS(f"psU{i}", [128, 512], F32)[:, 0:TD] for i in range(2)]; psUB = p.bufs(2, "psU")
                psX = [PS(f"psX{i}", [128, 512], F32) for i in range(2)]; psXB = p.bufs(2, "psX")

                if l == 1:
                    djs = list(range(cfg.out_lo // TD, (cfg.out_lo + cfg.out_n) // TD))
                else:
                    djs = list(range(NT // TD))
                xastB = p.buf("xast")
                p.dma(DMA(xt[0][:], xmid[djs[0] * TD:(djs[0] + 1) * TD, :].rearrange("(t p) d -> p t d", p=128)), writes=[xtB[0]])
                for dit, j in enumerate(djs):
                    cur = dit % 2
                    if dit + 1 < len(djs):
                        jn = djs[dit + 1]
                        p.dma(DMA(xt[1 - cur][:], xmid[jn * TD:(jn + 1) * TD, :].rearrange("(t p) d -> p t d", p=128)),
                              writes=[xtB[1 - cur]])
                    norm_transpose((junk, junkB, ss, ssB, rstd, rstdB, xs, xsB), xt[cur], xtB[cur], 2, hT, hTB, psT, psTB)
                    for fc in range(FC):
                        s = fc % 2
                        proj_fm(psA[s][:], wg, fc * 128, hT, 8, 0, TD, [wgB, hTB], [psAB[s]])
                        proj_fm(psU[s][:], wu, fc * 128, hT, 8, 0, TD, [wuB, hTB], [psUB[s]])
                        p.act(ACT(sl[s][:], psA[s][:], AF.Silu), reads=[psAB[s]], writes=[slB[s]])
                        p.dve(TT(hid[:, fc, :], psU[s][:], sl[s][:], ALU.mult), reads=[psUB[s], slB[s]], writes=[hidB[fc]])
                    for t in range(2):
                        for hf in range(2):
                            for fc in range(FC):
                                p.pe(MM(psX[hf][:], hid[:, fc, t * 128:(t + 1) * 128], wd[:, fc, hf * 512:(hf + 1) * 512],
                                        fc == 0, fc == FC - 1), reads=[hidB[fc], wdB], writes=[psXB[hf]])
                        ob = t % 2
                        post_norm_residual([psX[0][:], psX[1][:]], psXB, sq, sqB, ss2, ss2B, rstd2, rstd2B, gpo, gpoB,
                                           xt[cur][:, t, :], xtB[cur], yo[ob], yoB[ob], tmp, tmpB)
                        p.dma(DMA(xa[j * TD + t * 128:j * TD + (t + 1) * 128, :], yo[ob][:]), reads=[yoB[ob]],
                              writes=[xastB])
                p.flush()

        def sweep_e(l, final):
            TD = 256
            with contextlib.ExitStack() as st:
                def T(name, shape, dt):
                    return st.enter_context(nc.sbuf_tensor(uniq(name), shape, dt))

                def PS(name, shape, dt):
                    return st.enter_context(nc.psum_tensor(uniq(name), shape, dt))

                wp = T("wp", [128, 2, D], BF16); wpB = p.buf("wp")
                wq = T("wq", [128, 8, D], BF16); wqB = p.buf("wq")
                with contextlib.ExitStack() as wst:
                    stage = make_stage(wst)
                    load_w(stage, W["w_ple"][l], 2, 0, D, wp, 0, wpB)
                    load_w(stage, W["w_ple_gate"][l], 8, 0, D, wq, 0, wqB)
                    p.flush()
                xt = [T(f"xt{i}", [128, 2, D], F32) for i in range(3)]; xtB = p.bufs(3, "xt")
                pt = [T(f"pt{i}", [128, 2, 256], F32) for i in range(3)]; ptB = p.bufs(3, "pt")
                xb = [T(f"xb{i}", [128, 2, D], BF16) for i in range(2)]; xbB = [p.bufs(2, f"xb{i}_") for i in range(2)]
                pb = [T(f"pb{i}", [128, 2, 256], BF16) for i in range(2)]; pbB = p.bufs(2, "pb")
                xT = [T(f"xT{i}", [128, 8, TD], BF16) for i in range(2)]; xTB = p.bufs(2, "xT")
                pT = [T(f"pT{i}", [128, 2, TD], BF16) for i in range(2)]; pTB = p.bufs(2, "pT")
                tE = [T(f"tE{i}", [128, 512], F32) for i in range(2)]; tEB = p.bufs(2, "tE")
                t2 = [T(f"t2{i}", [128, 512], F32) for i in range(2)]; t2B = p.bufs(2, "t2")
                yo = [T(f"yo{i}", [128, D], F32) for i in range(2)]; yoB = p.bufs(2, "yo")
                psT = [PS(f"psT{i}", [128, 1024], BF16)[:, 0:TD] for i in range(2)]; psTB = p.bufs(2, "psT")
                psG = [PS(f"psG{i}", [128, 512], F32) for i in range(2)]; psGB = p.bufs(2, "psG")
                psP = [PS(f"psP{i}", [128, 512], F32) for i in range(2)]; psPB = p.bufs(2, "psP")

                if final:
                    js = list(range(cfg.out_lo // TD, (cfg.out_lo + cfg.out_n) // TD))
                else:
                    js = list(range(NT // TD))
                nj = len(js)

                def ld(it):
                    if it >= nj:
                        return
                    j = js[it]
                    s = it % 3
                    p.dma(DMA(xt[s][:], xa[j * TD:(j + 1) * TD, :].rearrange("(t p) d -> p t d", p=128)), writes=[xtB[s]])
                    p.dma(DMA(pt[s][:], p_in[l, j * TD:(j + 1) * TD, :].rearrange("(t p) d -> p t d", p=128)), writes=[ptB[s]])

                def prep(it):
                    if it >= nj:
                        return
                    s3 = it % 3
                    b = it % 2
                    p.act(ACP(xb[b][:, 0, :], xt[s3][:, 0, :]), reads=[xtB[s3]], writes=[xbB[b][0]])
                    p.pool(CP(xb[b][:, 1, :], xt[s3][:, 1, :]), reads=[xtB[s3]], writes=[xbB[b][1]])
                    p.pool(CP(pb[b][:], pt[s3][:]), reads=[ptB[s3]], writes=[pbB[b]])
                    for kc in range(8):
                        s = kc % 2
                        for t in range(2):
                            p.pe(TR(psT[s][:, t * 128:(t + 1) * 128], xb[b][:, t, kc * 128:(kc + 1) * 128], identb[:]),
                                 reads=[xbB[b][t], cB], writes=[psTB[s]])
                        if s == 0:
                            p.dve(CP(xT[b][:, kc, :], psT[s][:, 0:TD]), reads=[psTB[s]], writes=[xTB[b]])
                        else:
                            p.act(ACP(xT[b][:, kc, :], psT[s][:, 0:TD]), reads=[psTB[s]], writes=[xTB[b]])
                    for kc in range(2):
                        s = kc % 2
                        for t in range(2):
                            p.pe(TR(psT[s][:, t * 128:(t + 1) * 128], pb[b][:, t, kc * 128:(kc + 1) * 128], identb[:]),
                                 reads=[pbB[b], cB], writes=[psTB[s]])
                        p.dve(CP(pT[b][:, kc, :], psT[s][:, 0:TD]), reads=[psTB[s]], writes=[pTB[b]])

                x1stB = p.buf("x1st")

                def compute(it):
                    j = js[it]
                    s3 = it % 3
                    b = it % 2
                    for t in range(2):
                        ob = t % 2
                        for hf in range(2):
                            sl = slice(hf * 512, (hf + 1) * 512)
                            proj_tm(psG[hf][:], xT[b], t * 128, wq, hf * 512, 512, 8, [wqB, xTB[b]], [psGB[hf]])
                            proj_tm(psP[hf][:], pT[b], t * 128, wp, hf * 512, 512, 2, [wpB, pTB[b]], [psPB[hf]])
                            sigmoid_from_psum(psG[hf][:], psGB[hf], tE[hf][:], tEB[hf], tE[hf][:], tEB[hf])
                            p.dve(TT(t2[hf][:], psP[hf][:], tE[hf][:], ALU.mult), reads=[psPB[hf], tEB[hf]], writes=[t2B[hf]])
                            p.pool(TT(yo[ob][:, sl], t2[hf][:], xt[s3][:, t, sl], ALU.add), reads=[t2B[hf], xtB[s3]],
                                   writes=[yoB[ob]])
                        r0 = j * TD + t * 128
                        if final:
                            dst = out[r0 - cfg.out_lo:r0 - cfg.out_lo + 128, :]
                        else:
                            dst = x1[r0:r0 + 128, :]
                        p.dma(DMA(dst, yo[ob][:]), reads=[yoB[ob]], writes=[x1stB])

                ld(0)
                ld(1)
                prep(0)
                for it in range(nj):
                    ld(it + 2)
                    prep(it + 1)
                    compute(it)
                p.flush()

        steps = []
        for l in range(2):
            steps += [lambda l=l: sweep_hgrn(l, False), lambda l=l: sweep_hgrn(l, True), lambda l=l: sweep_c(l),
                      lambda l=l: sweep_d(l), lambda l=l: sweep_e(l, final=(l == 1))]
        if cfg.stop_after is not None:
            steps = steps[:cfg.stop_after] if isinstance(cfg.stop_after, int) else [steps[i] for i in cfg.stop_after]
        for f in steps:
            f()
        ninst = p.ninst
    return nc, ninst


SEG = 4096
HALO = 256
_WNAMES = ["norm_mix_pre", "w_in", "lb_gamma_fwd", "lb_gamma_bwd", "hg_norm", "sg_w", "sg_b", "sg_ln_g",
           "sg_ln_b", "w_a", "w_b", "w_out", "norm_mix_post", "norm_ffn_pre", "w_gate", "w_up", "w_down",
           "norm_ffn_post", "w_ple", "w_ple_gate"]
_CACHE = {}


def kernel(**inputs):
    x = np.asarray(inputs["x"], dtype=np.float32)
    pp = np.asarray(inputs["p"], dtype=np.float32)
    Bn, S, _ = x.shape
    nseg = S // SEG
    ncores = Bn * nseg
    NT = SEG + 2 * HALO
    key = (NT,)
    if key not in _CACHE:
        _CACHE[key] = build(Cfg(NT, HALO, SEG))[0]
    nc = _CACHE[key]
    wts = {k: np.ascontiguousarray(np.asarray(inputs[k], dtype=np.float32)) for k in _WNAMES}
    in_maps = []
    for b in range(Bn):
        for sgm in range(nseg):
            lo = sgm * SEG - HALO
            hi = (sgm + 1) * SEG + HALO
            xs = np.zeros((NT, D), np.float32)
            ps = np.zeros((2, NT, 256), np.float32)
            a, bnd = max(lo, 0), min(hi, S)
            xs[a - lo:bnd - lo] = x[b, a:bnd]
            ps[:, a - lo:bnd - lo] = pp[:, b, a:bnd]
            m = {"x": xs, "p": ps}
            m.update(wts)
            in_maps.append(m)
    res = run_bass_kernel_spmd(nc, in_maps, core_ids=list(range(ncores)))
    out = np.empty((Bn, S, D), np.float32)
    i = 0
    for b in range(Bn):
        for sgm in range(nseg):
            out[b, sgm * SEG:(sgm + 1) * SEG] = res.results[i]["out"]
            i += 1
    return out
```

```python
import contextlib
import numpy as np
import concourse.bass as bass
import concourse.mybir as mybir
from concourse.bass_utils import run_bass_kernel_spmd

F32 = mybir.dt.float32
BF16 = mybir.dt.bfloat16
AF = mybir.ActivationFunctionType
ALU = mybir.AluOpType
AX = mybir.AxisListType

ENGINES = ("pe", "act", "dve", "pool", "sp")
N_DSEM = 40

D = 1024
NIN = 8192
FH = 2816
FC = FH // 128
EPS = 1e-6
OQ, OFF, OFB, OI, OG, OU, OV, OGA, OGB = 0, 1024, 2048, 3072, 4096, 5120, 5632, 6144, 7168


class Buf:
    __slots__ = ("name", "w", "r")

    def __init__(self, name):
        self.name = name
        self.w = None
        self.r = []


class Op:
    __slots__ = ("eng", "fn", "deps", "sig", "sigval", "dma", "key", "flushed", "sem")

    def __init__(self, eng, fn, dma, key):
        self.eng = eng
        self.fn = fn
        self.deps = []
        self.sig = False
        self.sigval = 0
        self.dma = dma
        self.key = key
        self.flushed = False


def _flat(xs):
    out = []
    for x in xs:
        if isinstance(x, (list, tuple)):
            out.extend(_flat(x))
        elif x is not None:
            out.append(x)
    return out


class Prog:
    def __init__(self, nc, st):
        self.nc = nc
        self.ops = {e: [] for e in ENGINES}
        self.nbuf = 0
        self.sweep = 0
        self.esem = [{e: st.enter_context(nc.semaphore(f"s{k}_{e}")) for e in ENGINES} for k in range(2)]
        self.dsem = [[st.enter_context(nc.semaphore(f"d{k}_{i}")) for i in range(N_DSEM)] for k in range(2)]
        self.ninst = 0

    def buf(self, name=None):
        self.nbuf += 1
        return Buf(name or f"b{self.nbuf}")

    def bufs(self, n, name="b"):
        return [self.buf(f"{name}{i}") for i in range(n)]

    def op(self, eng, fn, reads=(), writes=(), dma=False):
        reads = _flat(reads)
        writes = _flat(writes)
        key = None
        if dma:
            key = writes[0]
        o = Op(eng, fn, dma, key)
        deps = []
        for b in reads:
            if b.w is not None:
                deps.append(b.w)
        for b in writes:
            if b.w is not None:
                deps.append(b.w)
            deps.extend(b.r)
        for b in reads:
            b.r.append(o)
        for b in writes:
            b.w = o
            b.r = []
        seen = set()
        for d in deps:
            if d is o or id(d) in seen or d.flushed:
                continue
            seen.add(id(d))
            if d.eng == "pe" and eng == "pe" and not d.dma:
                continue
            o.deps.append(d)
            d.sig = True
        self.ops[eng].append(o)
        return o

    def pe(self, fn, reads=(), writes=()):
        return self.op("pe", fn, reads, writes)

    def act(self, fn, reads=(), writes=()):
        return self.op("act", fn, reads, writes)

    def dve(self, fn, reads=(), writes=()):
        return self.op("dve", fn, reads, writes)

    def pool(self, fn, reads=(), writes=()):
        return self.op("pool", fn, reads, writes)

    def dma(self, fn, reads=(), writes=()):
        return self.op("sp", fn, reads, writes, dma=True)

    def flush(self):
        nc = self.nc
        k = self.sweep % 2
        esem = self.esem[k]
        dpool = self.dsem[k]
        other_e = self.esem[1 - k]
        other_d = self.dsem[1 - k]
        dma_keys = {}
        nsem = [0]
        allsems = []
        for e in ENGINES:
            cnt = 0
            for o in self.ops[e]:
                if o.dma:
                    kk = id(o.key)
                    if kk not in dma_keys or dma_keys[kk][1] + 16 > 224:
                        assert nsem[0] < N_DSEM, "too many DMA semaphores in one sweep"
                        dma_keys[kk] = [dpool[nsem[0]], 0]
                        allsems.append(dma_keys[kk])
                        nsem[0] += 1
                    dma_keys[kk][1] += 16
                    o.sigval = dma_keys[kk][1]
                    o.sem = dma_keys[kk][0]
                elif o.sig:
                    cnt += 1
                    o.sigval = cnt
        ops = self.ops
        first = self.sweep == 0

        def body(ename):
            def f(eng):
                waited = {}

                def wait_for(d):
                    s = d.sem if d.dma else esem[d.eng]
                    if waited.get(id(s), 0) >= d.sigval:
                        return
                    waited[id(s)] = d.sigval
                    eng.wait_ge(s, d.sigval)

                if ename == "sp" and not first:
                    for s in list(other_e.values()) + list(other_d):
                        eng.sem_clear(s)
                for o in ops[ename]:
                    for d in o.deps:
                        wait_for(d)
                    ins = o.fn(eng)
                    self.ninst += 1
                    if o.dma:
                        ins.then_inc(o.sem, 16)
                    elif o.sig:
                        ins.then_inc(esem[ename], 1)
                if ename == "sp":
                    for s, tot in allsems:
                        if tot > 0:
                            eng.wait_ge(s, tot)
            return f

        with nc.allow_low_precision(reason="bf16 matmul operands, fp32 accumulation"), nc.Block() as block:
            block.tensor(body("pe"))
            block.scalar(body("act"))
            block.vector(body("dve"))
            block.gpsimd(body("pool"))
            block.sync(body("sp"))
        for e in ENGINES:
            for o in self.ops[e]:
                o.flushed = True
                o.fn = None
        self.ops = {e: [] for e in ENGINES}
        self.sweep += 1


def MM(out, lhsT, rhs, start=True, stop=True):
    return lambda e: e.matmul(out, lhsT=lhsT, rhs=rhs, start=start, stop=stop)


def TR(out, in_, ident):
    return lambda e: e.transpose(out, in_, ident)


def ACT(out, in_, func, bias=None, scale=None, accum=None):
    kw = {}
    if bias is not None:
        kw["bias"] = bias
    if scale is not None:
        kw["scale"] = scale
    if accum is not None:
        kw["accum_out"] = accum
    return lambda e: e.activation(out, in_, func, **kw)


def CP(out, in_):
    return lambda e: e.tensor_copy(out, in_)


def ACP(out, in_):
    return lambda e: e.copy(out, in_)


def TT(out, a, b, op):
    return lambda e: e.tensor_tensor(out=out, in0=a, in1=b, op=op)


def TS(out, a, s1, s2, op0, op1):
    return lambda e: e.tensor_scalar(out=out, in0=a, scalar1=s1, scalar2=s2, op0=op0, op1=op1)


def STT(out, a, s, b, op0, op1):
    return lambda e: e.scalar_tensor_tensor(out=out, in0=a, scalar=s, in1=b, op0=op0, op1=op1)


def RECIP(out, in_):
    return lambda e: e.reciprocal(out, in_)


def MSET(ap, v):
    return lambda e: e.memset(ap, v)


def DMA(out, in_, slow=False):
    if slow:
        return lambda e: e.dma_start(out=out, in_=in_, allow_slow_non_contiguous=True)
    return lambda e: e.dma_start(out=out, in_=in_)


_UID = {"n": 0}


def uniq(name):
    _UID["n"] += 1
    return f"{name}_{_UID['n']}"


class Cfg:
    def __init__(self, NT, out_lo, out_n, debug=False, stop_after=None):
        self.stop_after = stop_after
        self.NT = NT
        self.out_lo = out_lo
        self.out_n = out_n
        self.debug = debug


def build(cfg):
    NT = cfg.NT
    NS = NT // 512
    nc = bass.Bass("TRN2", target_bir_lowering=False)

    def din(name, shape):
        return nc.dram_tensor(name, shape, F32, kind="ExternalInput").ap()

    x_in = din("x", [NT, D])
    p_in = din("p", [2, NT, 256])
    W = {}
    for nm, shp in [("norm_mix_pre", [2, D]), ("w_in", [2, D, NIN]), ("lb_gamma_fwd", [2, D]),
                    ("lb_gamma_bwd", [2, D]), ("hg_norm", [2, D]), ("sg_w", [2, 8, 128, 128]),
                    ("sg_b", [2, 8, 128]), ("sg_ln_g", [2, 512]), ("sg_ln_b", [2, 512]),
                    ("w_a", [2, D, D]), ("w_b", [2, 512, D]), ("w_out", [2, D, D]),
                    ("norm_mix_post", [2, D]), ("norm_ffn_pre", [2, D]), ("w_gate", [2, D, FH]),
                    ("w_up", [2, D, FH]), ("w_down", [2, FH, D]), ("norm_ffn_post", [2, D]),
                    ("w_ple", [2, 256, D]), ("w_ple_gate", [2, D, D])]:
        W[nm] = din(nm, shp)
    out = nc.dram_tensor("out", [cfg.out_n, D], F32, kind="ExternalOutput").ap()

    okind = "ExternalOutput" if cfg.debug else "Internal"

    def dscr(name, shape, dt):
        return nc.dram_tensor(name, shape, dt, kind=okind).ap()

    hT_st = dscr("hT_st", [NS, 128, 8 * 512], BF16)
    qT_st = dscr("qT_st", [NS, 128, 8 * 512], BF16)
    v_st = dscr("v_st", [NS, 128, 4 * 1024], BF16)
    of_st = dscr("of_st", [NS, 128, 4 * 1024], F32)
    mAT_st = dscr("mAT_st", [NS, 128, 8 * 512], BF16)
    xmid = dscr("xmid", [NT, D], F32)
    xa = dscr("xa", [NT, D], F32)
    x1 = dscr("x1", [NT, D], F32)

    with contextlib.ExitStack() as gst:
        p = Prog(nc, gst)

        def GT(name, shape, dt):
            return gst.enter_context(nc.sbuf_tensor(uniq(name), shape, dt))

        identf = GT("identf", [128, 128], F32)
        identb = GT("identb", [128, 128], BF16)
        scanm = GT("scanm", [128, 512], F32)
        maskf = GT("maskf", [128, 8, 64], F32)
        maskb = GT("maskb", [128, 8, 64], F32)
        ones1 = GT("ones1", [128, 1], F32)
        mhalf = GT("mhalf", [128, 8], F32)
        lbt = GT("lbt", [128, 2, 2, 8], F32)
        omlt = GT("omlt", [128, 2, 2, 8], F32)
        nomlt = GT("nomlt", [128, 2, 2, 8], F32)
        gam = GT("gam", [128, 2, 2, 8], F32)
        cB = p.buf("consts")

        p.pool(MSET(identf[:], 0.0), writes=[cB])
        p.pool(lambda e: e.affine_select(out=identf[:], in_=identf[:], pattern=[[-1, 128]],
                                         compare_op=ALU.not_equal, fill=1.0, base=0,
                                         channel_multiplier=1), reads=[cB], writes=[cB])
        p.dve(CP(identb[:], identf[:]), reads=[cB], writes=[cB])
        p.pool(MSET(scanm[:], 1.0), writes=[cB])
        p.pool(MSET(scanm[:].rearrange("p (c t) -> p c t", t=64)[:, :, 0:1], 0.0), writes=[cB])
        p.pool(MSET(ones1[:], 1.0), writes=[cB])
        p.pool(MSET(mhalf[:], -0.5), writes=[cB])
        p.pool(MSET(maskf[:], 1.0), writes=[cB])
        p.pool(MSET(maskb[:], 1.0), writes=[cB])
        for lo in (0, 64):
            p.pool(lambda e, lo=lo: e.affine_select(out=maskf[lo:lo + 64], in_=maskf[lo:lo + 64],
                                                    pattern=[[0, 8], [1, 64]], compare_op=ALU.is_ge,
                                                    fill=0.0, base=0, channel_multiplier=-1),
                   reads=[cB], writes=[cB])
            p.pool(lambda e, lo=lo: e.affine_select(out=maskb[lo:lo + 64], in_=maskb[lo:lo + 64],
                                                    pattern=[[0, 8], [-1, 64]], compare_op=ALU.is_ge,
                                                    fill=0.0, base=0, channel_multiplier=1),
                   reads=[cB], writes=[cB])
        gB = p.buf("gam")
        for di, nm in enumerate(("lb_gamma_fwd", "lb_gamma_bwd")):
            for l in range(2):
                p.dma(DMA(gam[:, di, l, :], W[nm][l].rearrange("(h p) -> p h", p=128), slow=True), writes=[gB])
        p.act(ACT(gam[:], gam[:], AF.Exp), reads=[gB], writes=[gB])
        for di in range(2):
            p.dve(TT(omlt[:, di, 0, :], gam[:, di, 0, :], gam[:, di, 1, :], ALU.add), reads=[gB], writes=[cB])
            p.dve(RECIP(omlt[:, di, 0, :], omlt[:, di, 0, :]), reads=[cB], writes=[cB])
            p.dve(TT(lbt[:, di, 1, :], gam[:, di, 1, :], omlt[:, di, 0, :], ALU.mult), reads=[gB, cB], writes=[cB])
            p.dve(MSET(lbt[:, di, 0, :], 0.0), reads=[cB], writes=[cB])
        p.dve(TS(omlt[:], lbt[:], -1.0, 1.0, ALU.mult, ALU.add), reads=[cB], writes=[cB])
        p.dve(TS(nomlt[:], omlt[:], -1.0, None, ALU.mult, ALU.bypass), reads=[cB], writes=[cB])
        p.flush()

        wl_state = {"i": 0}

        def load_rowscale(st, name, dram_vec, kc_n):
            t = st.enter_context(nc.sbuf_tensor(uniq(name), [128, kc_n], F32))
            b = p.buf(name)
            p.dma(DMA(t[:], dram_vec.rearrange("(c p) -> p c", p=128), slow=True), writes=[b])
            return t, b

        def load_bcast(st, name, dram_vec, n):
            t = st.enter_context(nc.sbuf_tensor(uniq(name), [128, n], F32))
            b = p.buf(name)
            p.dma(DMA(t[:], dram_vec.partition_broadcast(128)), writes=[b])
            return t, b

        def make_stage(st):
            stg = [st.enter_context(nc.sbuf_tensor(uniq(f"wstg{i}"), [128, 2048], F32)) for i in range(4)]
            return stg, p.bufs(4, "wstg")

        def load_w(stage, wd, kc_n, c0, ncols, dst, dc0, dstB, scale=None, scaleB=None):
            stg, sB = stage
            for kc in range(kc_n):
                for c in range(0, ncols, 2048):
                    n = min(2048, ncols - c)
                    i = wl_state["i"]
                    wl_state["i"] += 1
                    s = i % 4
                    p.dma(DMA(stg[s][:, :n], wd[kc * 128:(kc + 1) * 128, c0 + c:c0 + c + n]), writes=[sB[s]])
                    o_ap = dst[:, kc, dc0 + c:dc0 + c + n]
                    eng = ("dve", "act", "dve", "act", "pool")[i % 5]
                    if scale is None:
                        if eng == "act":
                            p.act(ACP(o_ap, stg[s][:, :n]), reads=[sB[s]], writes=[dstB])
                        else:
                            p.op(eng, CP(o_ap, stg[s][:, :n]), reads=[sB[s]], writes=[dstB])
                    else:
                        sc = scale[:, kc:kc + 1]
                        if eng == "act":
                            p.act(ACT(o_ap, stg[s][:, :n], AF.Copy, scale=sc), reads=[sB[s], scaleB], writes=[dstB])
                        else:
                            p.op(eng, TS(o_ap, stg[s][:, :n], sc, 0.0, ALU.mult, ALU.add),
                                 reads=[sB[s], scaleB], writes=[dstB])

        def proj_fm(ps, w, c0, inT, kc_n, tok0, ntok, rd, wr):
            for kc in range(kc_n):
                p.pe(MM(ps, w[:, kc, c0:c0 + 128], inT[:, kc, tok0:tok0 + ntok], kc == 0, kc == kc_n - 1),
                     reads=rd, writes=wr)

        def proj_tm(ps, inT, tok0, w, c0, ncols, kc_n, rd, wr):
            for kc in range(kc_n):
                p.pe(MM(ps, inT[:, kc, tok0:tok0 + 128], w[:, kc, c0:c0 + ncols], kc == 0, kc == kc_n - 1),
                     reads=rd, writes=wr)

        def sigmoid_from_psum(ps, psB, tmp, tmpB, out_ap, outB):
            p.act(ACT(tmp, ps, AF.Exp, scale=-1.0), reads=[psB], writes=[tmpB])
            p.act(ACT(tmp, tmp, AF.Ln, bias=1.0), reads=[tmpB], writes=[tmpB])
            p.act(ACT(out_ap, tmp, AF.Exp, scale=-1.0), reads=[tmpB], writes=[outB])

        def rms_rstd(ss, ssB, ncol, n, rstd, rstdB):
            p.dve(TS(ss, ss, 1.0 / n, EPS, ALU.mult, ALU.add), reads=[ssB], writes=[ssB])
            p.pool(TT(rstd, ss, mhalf[:, 0:ncol], ALU.pow), reads=[ssB, cB], writes=[rstdB])

        def norm_transpose(st_tiles, xt, xtB, ntile, hT, hTB, psT, psTB):
            junk, junkB, ss, ssB, rstd, rstdB, xs, xsB = st_tiles
            for t in range(ntile):
                p.act(ACT(junk[:], xt[:, t, :], AF.Square, accum=ss[:, t:t + 1]), reads=[xtB], writes=[junkB, ssB])
            rms_rstd(ss[:, 0:ntile], ssB, ntile, D, rstd[:, 0:ntile], rstdB)
            for t in range(ntile):
                if t % 2 == 0:
                    p.act(ACT(xs[:, t, :], xt[:, t, :], AF.Copy, scale=rstd[:, t:t + 1]), reads=[xtB, rstdB], writes=[xsB[t]])
                else:
                    p.dve(TS(xs[:, t, :], xt[:, t, :], rstd[:, t:t + 1], 0.0, ALU.mult, ALU.add),
                          reads=[xtB, rstdB], writes=[xsB[t]])
            for kc in range(8):
                s = kc % 2
                for t in range(ntile):
                    p.pe(TR(psT[s][:, t * 128:(t + 1) * 128], xs[:, t, kc * 128:(kc + 1) * 128], identb[:]),
                         reads=[xsB[t], cB], writes=[psTB[s]])
                if kc % 2 == 0:
                    p.dve(CP(hT[:, kc, 0:ntile * 128], psT[s][:, 0:ntile * 128]), reads=[psTB[s]], writes=[hTB])
                else:
                    p.act(ACP(hT[:, kc, 0:ntile * 128], psT[s][:, 0:ntile * 128]), reads=[psTB[s]], writes=[hTB])

        def post_norm_residual(psX, psXB, sq, sqB, ss, ssB, rstd, rstdB, gbc, gbcB, xres, xresB, yout, youtB, tmp, tmpB):
            for hf in range(2):
                p.act(ACT(sq[:, 0:512], psX[hf], AF.Square, accum=ss[:, hf:hf + 1]), reads=[psXB[hf]], writes=[sqB, ssB])
            p.dve(TT(ss[:, 0:1], ss[:, 0:1], ss[:, 1:2], ALU.add), reads=[ssB], writes=[ssB])
            rms_rstd(ss[:, 0:1], ssB, 1, D, rstd[:, 0:1], rstdB)
            for hf in range(2):
                sl = slice(hf * 512, (hf + 1) * 512)
                p.dve(STT(tmp[:, sl], psX[hf], rstd[:, 0:1], gbc[:, sl], ALU.mult, ALU.mult),
                      reads=[psXB[hf], rstdB, gbcB], writes=[tmpB])
                p.pool(TT(yout[:, sl], tmp[:, sl], xres[:, sl], ALU.add), reads=[tmpB, xresB], writes=[youtB])

        def sweep_hgrn(l, rev):
            x_src = x_in if l == 0 else x1
            di = 1 if rev else 0
            with contextlib.ExitStack() as st:
                def T(name, shape, dt):
                    return st.enter_context(nc.sbuf_tensor(uniq(name), shape, dt))

                def PS(name, shape, dt):
                    return st.enter_context(nc.psum_tensor(uniq(name), shape, dt))

                wl = W["w_in"][l]
                gpre, gpreB = load_rowscale(st, "gpre", W["norm_mix_pre"][l], 8)
                wr = T("wr", [128, 8, 3072], BF16)
                wrB = p.buf("wr")
                if rev:
                    gh, ghB = load_rowscale(st, "gh", W["hg_norm"][l], 8)
                    wa = T("wa", [128, 8, 1024], BF16)
                    waB = p.buf("wa")
                with contextlib.ExitStack() as wst:
                    stage = make_stage(wst)
                    if not rev:
                        load_w(stage, wl, 8, OQ, 1024, wr, 0, wrB, gpre, gpreB)
                        load_w(stage, wl, 8, OFF, 1024, wr, 1024, wrB, gpre, gpreB)
                        load_w(stage, wl, 8, OI, 1024, wr, 2048, wrB, gpre, gpreB)
                    else:
                        load_w(stage, wl, 8, OFB, 1024, wr, 0, wrB, gpre, gpreB)
                        load_w(stage, wl, 8, OG, 1024, wr, 1024, wrB, gpre, gpreB)
                        load_w(stage, wl, 8, OGA, 1024, wr, 2048, wrB, gpre, gpreB)
                        load_w(stage, W["w_a"][l], 8, 0, 1024, wa, 0, waB, gh, ghB)
                    p.flush()
                if not rev:
                    cQ, cF, cI = 0, 1024, 2048
                else:
                    cF, cG, cGA = 0, 1024, 2048

                hT = T("hT", [128, 8, 512], BF16); hTB = p.buf("hT")
                vtm = [T(f"vtm{i}", [128, 4, 1024], BF16) for i in range(2)]; vtmB = [p.bufs(4, f"vtm{i}_") for i in range(2)]
                qtT = [T(f"qtT{i}", [128, 8, 512], BF16) for i in range(2)]; qtTB = [p.bufs(8, f"qtT{i}_") for i in range(2)]
                ktT = [T(f"ktT{i}", [128, 8, 512], BF16) for i in range(2)]; ktTB = [p.bufs(8, f"ktT{i}_") for i in range(2)]
                dsv = [T(f"dsv{i}", [128, 8, 8], F32) for i in range(2)]; dsvB = [p.bufs(8, f"dsv{i}_") for i in range(2)]
                tE = [T(f"tE{i}", [128, 512], F32) for i in range(2)]; tEB = p.bufs(2, "tE")
                tS = [T(f"tS{i}", [128, 512], F32) for i in range(2)]; tSB = p.bufs(2, "tS")
                tEb = T("tEb", [128, 512], F32); tEbB = p.buf("tEb")
                tL = T("tL", [128, 512], F32); tLB = p.buf("tL")
                tK = T("tK", [128, 512], F32); tKB = p.buf("tK")
                tB = T("tB", [128, 512], F32); tBB = p.buf("tB")
                tN = T("tN", [128, 512], F32); tNB = p.buf("tN")
                ktm = T("ktm", [128, 8, 128], BF16); ktmB = p.bufs(2, "ktm")
                PT = T("PT", [128, 8, 64], BF16); PTB = p.buf("PT")
                Sp = T("Sp", [128, 8, 128], F32); SpB = p.bufs(8, "Sp")
                Sbf = T("Sbf", [128, 8, 128], BF16); SbfB = p.bufs(8, "Sbf")

                psP = [PS(f"psP{i}", [128, 512], F32) for i in range(2)]; psPB = p.bufs(2, "psP")
                psS = PS("psS", [128, 8, 64], F32); psSB = p.buf("psS")
                psO = PS("psO", [128, 8, 128], F32); psOB = p.buf("psO")
                psOf = psO[:].rearrange("p h v -> p (h v)")
                psM = PS("psM", [128, 8, 128], F32); psMB = [p.buf("psMa")] * 4 + [p.buf("psMb")] * 4
                psK = PS("psK", [128, 8, 128], BF16); psKB = [p.buf("psK")] * 2
                pp = {"i": 0}

                def next_ps():
                    i = pp["i"] % 2
                    pp["i"] += 1
                    return psP[i][:], psPB[i]

                if not rev:
                    xt = [T(f"xt{i}", [128, 4, D], F32) for i in range(2)]; xtB = p.bufs(2, "xt")
                    junk = T("junk", [128, D], BF16); junkB = p.buf("junk")
                    ss = T("ss", [128, 4], F32); ssB = p.buf("ss")
                    rstd = T("rstd", [128, 4], F32); rstdB = p.buf("rstd")
                    xs = T("xs", [128, 4, D], BF16); xsB = p.bufs(4, "xs")
                    osb = [T(f"osb{i}", [128, 1024], F32) for i in range(2)]; osbB = p.bufs(2, "osb")
                    psKf = psK[:].rearrange("p h k -> p (h k)")
                    psT = [psKf[:, 0:512], psKf[:, 512:1024]]; psTB = [psKB[0]] * 2
                    hTstB, qTstB, vstB, ofstB = p.buf("hTst"), p.buf("qTst"), p.buf("vst"), p.buf("ofst")
                else:
                    oft = [T(f"oft{i}", [128, 1024], F32) for i in range(2)]; oftB = p.bufs(2, "oft")
                    gT = T("gT", [128, 8, 512], BF16); gTB = p.bufs(8, "gT")
                    sga = T("sga", [128, 8, 512], BF16); sgaB = p.bufs(8, "sga")
                    AT = T("AT", [128, 8, 512], BF16); ATB = p.bufs(4, "AT")
                    mATr = [T(f"mATr{i}", [128, 512], BF16) for i in range(2)]; mATrB = p.bufs(2, "mATr")
                    osum = T("osum", [128, 1024], F32); osumB = p.buf("osum")
                    sq = T("sq", [128, 1024], F32); sqB = p.buf("sq")
                    ss8 = T("ss8", [128, 8], F32); ss8B = p.buf("ss8")
                    rs8 = T("rs8", [128, 8], F32); rs8B = p.buf("rs8")
                    on = [T(f"on{i}", [128, 8, 128], BF16) for i in range(2)]; onB = p.bufs(2, "on")
                    psOT = psK; psOTB = psKB[0]
                    mATstB = p.buf("mATst")

                for h in range(8):
                    p.pool(MSET(Sp[:, h, :], 0.0), writes=[SpB[h]])
                    p.pool(MSET(Sbf[:, h, :], 0.0), writes=[SbfB[h]])
                mask = maskb if rev else maskf

                order = list(range(NS))
                if rev:
                    order = order[::-1]

                def xload(it):
                    j = order[it]
                    p.dma(DMA(xt[it % 2][:], x_src[j * 512:(j + 1) * 512, :].rearrange("(t p) d -> p t d", p=128)),
                          writes=[xtB[it % 2]])

                def gate_head(b, h):
                    s = h % 2
                    ps, psB_ = next_ps()
                    proj_fm(ps, wr, cF + h * 128, hT, 8, 0, 512, [wrB, hTB], [psB_])
                    sigmoid_from_psum(ps, psB_, tE[s][:], tEB[s], tS[s][:], tSB[s])
                    lb_c = lbt[:, di, l, h:h + 1]
                    oml_c = omlt[:, di, l, h:h + 1]
                    noml_c = nomlt[:, di, l, h:h + 1]
                    p.act(ACT(tL[:], tS[s][:], AF.Ln, bias=lb_c, scale=oml_c), reads=[tSB[s], cB], writes=[tLB])
                    p.dve(TS(tK[:], tS[s][:], noml_c, oml_c, ALU.mult, ALU.add), reads=[tSB[s], cB], writes=[tKB])
                    p.dve(lambda e: e.tensor_tensor_scan(out=tB[:], data0=scanm[:], data1=tL[:], initial=0.0,
                                                         op0=ALU.mult, op1=ALU.add),
                          reads=[tLB, cB], writes=[tBB])
                    if rev:
                        p.dve(TT(tL[:], tL[:], tB[:], ALU.subtract), reads=[tLB, tBB], writes=[tLB])
                        tot = tB[:].rearrange("p (c t) -> p c t", t=64)[:, :, 63:64].to_broadcast([128, 8, 64])
                        p.dve(TT(tL[:].rearrange("p (c t) -> p c t", t=64), tL[:].rearrange("p (c t) -> p c t", t=64),
                                 tot, ALU.add), reads=[tLB, tBB], writes=[tLB])
                        bsrc, bsrcB = tL, tLB
                    else:
                        bsrc, bsrcB = tB, tBB
                    p.act(ACT(tEb[:], bsrc[:], AF.Exp), reads=[bsrcB], writes=[tEbB])
                    p.act(ACT(tN[:], bsrc[:], AF.Exp, scale=-1.0), reads=[bsrcB], writes=[tNB])
                    dc_ = 0 if rev else 63
                    p.pool(CP(dsv[b][:, h, :], tEb[:].rearrange("p (c t) -> p c t", t=64)[:, :, dc_]),
                           reads=[tEbB], writes=[dsvB[b][h]])
                    p.dve(TT(qtT[b][:, h, :], qtT[b][:, h, :], tEb[:], ALU.mult), reads=[qtTB[b][h], tEbB], writes=[qtTB[b][h]])
                    p.dve(TT(ktT[b][:, h, :], tK[:], tN[:], ALU.mult), reads=[tKB, tNB], writes=[ktTB[b][h]])

                def front(it):
                    b = it % 2
                    j = order[it]
                    if not rev:
                        if it + 1 < NS:
                            xload(it + 1)
                        norm_transpose((junk, junkB, ss, ssB, rstd, rstdB, xs, xsB), xt[b], xtB[b], 4, hT, hTB, psT, psTB)
                        p.dma(DMA(hT_st[j], hT[:].rearrange("p k t -> p (k t)")), reads=[hTB], writes=[hTstB])
                        yield
                        for h in range(8):
                            ps, psB_ = next_ps()
                            proj_fm(ps, wr, cQ + h * 128, hT, 8, 0, 512, [wrB, hTB], [psB_])
                            s = h % 2
                            sigmoid_from_psum(ps, psB_, tE[s][:], tEB[s], tS[s][:], tSB[s])
                            p.dve(TT(qtT[b][:, h, :], ps, tS[s][:], ALU.mult), reads=[psB_, tSB[s]], writes=[qtTB[b][h]])
                            yield
                        p.dma(DMA(qT_st[j], qtT[b][:].rearrange("p k t -> p (k t)")), reads=qtTB[b], writes=[qTstB])
                        for t in range(4):
                            for hf in range(2):
                                ps, psB_ = next_ps()
                                proj_tm(ps, hT, t * 128, wr, cI + hf * 512, 512, 8, [wrB, hTB], [psB_])
                                if hf == 0:
                                    p.act(ACP(vtm[b][:, t, 0:512], ps), reads=[psB_], writes=[vtmB[b][t]])
                                else:
                                    p.dve(CP(vtm[b][:, t, 512:1024], ps), reads=[psB_], writes=[vtmB[b][t]])
                            yield
                        p.dma(DMA(v_st[j], vtm[b][:].rearrange("p t d -> p (t d)")), reads=vtmB[b], writes=[vstB])
                        for h in range(8):
                            gate_head(b, h)
                            yield
                    else:
                        p.dma(DMA(hT[:].rearrange("p k t -> p (k t)"), hT_st[j]), writes=[hTB])
                        p.dma(DMA(qtT[b][:].rearrange("p k t -> p (k t)"), qT_st[j]), writes=qtTB[b])
                        p.dma(DMA(vtm[b][:].rearrange("p t d -> p (t d)"), v_st[j]), writes=vtmB[b])
                        yield
                        for h in range(8):
                            gate_head(b, h)
                            yield

                def frontB(it):
                    if True:
                        for h in range(8):
                            ps, psB_ = next_ps()
                            proj_fm(ps, wr, cG + h * 128, hT, 8, 0, 512, [wrB, hTB], [psB_])
                            s = h % 2
                            sigmoid_from_psum(ps, psB_, tE[s][:], tEB[s], tS[s][:], tSB[s])
                            p.dve(TT(gT[:, h, :], ps, tS[s][:], ALU.mult), reads=[psB_, tSB[s]], writes=[gTB[h]])
                            yield
                        for h in range(8):
                            ps, psB_ = next_ps()
                            proj_fm(ps, wr, cGA + h * 128, hT, 8, 0, 512, [wrB, hTB], [psB_])
                            s = h % 2
                            sigmoid_from_psum(ps, psB_, tE[s][:], tEB[s], sga[:, h, :], sgaB[h])
                            yield

                def pump(gens, n):
                    for _ in range(n):
                        done = False
                        for g in gens:
                            try:
                                next(g)
                                done = True
                                break
                            except StopIteration:
                                continue
                        if not done:
                            return

                def drain(gen):
                    if gen is None:
                        return
                    for _ in gen:
                        pass

                state = {"dprev": [ones1[:, 0:1]] * 8, "dprevB": [cB] * 8, "oi": 0}

                def chain2(g1, g2):
                    if g1 is not None:
                        yield from g1
                    if g2 is not None:
                        yield from g2

                def emit_pending():
                    if state.get("pend") is None:
                        return
                    t_, o_ = state["pend"]
                    state["pend"] = None
                    tk_ = slice(t_ * 128, (t_ + 1) * 128)
                    for h in range(8):
                        p.pe(TR(psOT[:, h, :], on[o_][:, h, :], identb[:]), reads=[onB[o_], cB], writes=[psOTB])
                    p.act(ACP(AT[:, :, tk_], psOT[:]), reads=[psOTB], writes=[ATB[t_]])

                def back(it, genB, genA):
                    gen = [g for g in (genB, genA) if g is not None]
                    b = it % 2
                    j = order[it]
                    dprev, dprevB = state["dprev"], state["dprevB"]
                    tiles = [3, 2, 1, 0] if rev else [0, 1, 2, 3]
                    chunks = [1, 0] if rev else [0, 1]
                    for t in tiles:
                        tk = slice(t * 128, (t + 1) * 128)
                        if rev:
                            ofs = state["oi"] % 2
                            state["oi"] += 1
                            p.dma(DMA(oft[ofs][:], of_st[j][:, t * 1024:(t + 1) * 1024]), writes=[oftB[ofs]])
                        for half in range(2):
                            for h in range(half * 4, half * 4 + 4):
                                p.pe(TR(psK[:, h, :], ktT[b][:, h, tk], identb[:]), reads=[ktTB[b][h], cB], writes=[psKB[half]])
                            if half == 0:
                                p.act(ACP(ktm[:, 0:4, :], psK[:, 0:4, :]), reads=[psKB[0]], writes=[ktmB[0]])
                            else:
                                p.dve(CP(ktm[:, 4:8, :], psK[:, 4:8, :]), reads=[psKB[1]], writes=[ktmB[1]])
                        for c in range(2):
                            ck = slice(t * 128 + c * 64, t * 128 + c * 64 + 64)
                            for h in range(8):
                                p.pe(MM(psS[c * 64:(c + 1) * 64, h, :], ktT[b][:, h, ck], qtT[b][:, h, ck]),
                                     reads=[ktTB[b][h], qtTB[b][h]], writes=[psSB])
                        p.dve(TT(PT[:], psS[:], mask[:], ALU.mult), reads=[psSB, cB], writes=[PTB])
                        if rev:
                            emit_pending()
                        for c in chunks:
                            pr = slice(c * 64, (c + 1) * 64)
                            ck = slice(t * 128 + c * 64, t * 128 + c * 64 + 64)
                            cidx = t * 2 + c
                            for h in range(8):
                                vs = vtm[b][pr, t, h * 128:(h + 1) * 128]
                                p.pe(MM(psO[pr, h, :], qtT[b][:, h, ck], Sbf[:, h, :], True, False),
                                     reads=[qtTB[b][h], SbfB[h]], writes=[psOB])
                                p.pe(MM(psO[pr, h, :], PT[pr, h, :], vs, False, True),
                                     reads=[PTB, vtmB[b][t]], writes=[psOB])
                            for half in range(2):
                                for h in range(half * 4, half * 4 + 4):
                                    vs = vtm[b][pr, t, h * 128:(h + 1) * 128]
                                    p.pe(MM(psM[:, h, :], ktm[pr, h, :], vs), reads=[ktmB[half], vtmB[b][t]], writes=[psMB[h]])
                            for half in range(2):
                                for h in range(half * 4, half * 4 + 4):
                                    p.dve(STT(Sp[:, h, :], Sp[:, h, :], dprev[h], psM[:, h, :], ALU.mult, ALU.add),
                                          reads=[SpB[h], dprevB[h], psMB[h]], writes=[SpB[h]])
                                    dcur = dsv[b][:, h, cidx:cidx + 1]
                                    if h % 2 == 0:
                                        p.act(ACT(Sbf[:, h, :], Sp[:, h, :], AF.Copy, scale=dcur),
                                              reads=[SpB[h], dsvB[b][h]], writes=[SbfB[h]])
                                    else:
                                        p.pool(TS(Sbf[:, h, :], Sp[:, h, :], dcur, 0.0, ALU.mult, ALU.add),
                                               reads=[SpB[h], dsvB[b][h]], writes=[SbfB[h]])
                                    dprev[h] = dcur
                                    dprevB[h] = dsvB[b][h]
                            pump(gen, 3)
                        if not rev:
                            ob = t % 2
                            p.act(ACP(osb[ob][:, 0:512], psOf[:, 0:512]), reads=[psOB], writes=[osbB[ob]])
                            p.dve(CP(osb[ob][:, 512:1024], psOf[:, 512:1024]), reads=[psOB], writes=[osbB[ob]])
                            p.dma(DMA(of_st[j][:, t * 1024:(t + 1) * 1024], osb[ob][:]), reads=[osbB[ob]], writes=[ofstB])
                        else:
                            p.dve(TT(osum[:], psOf, oft[ofs][:], ALU.add), reads=[psOB, oftB[ofs]], writes=[osumB])
                            p.act(ACT(sq[:], osum[:], AF.Square), reads=[osumB], writes=[sqB])
                            p.dve(lambda e: e.tensor_reduce(out=ss8[:], in_=sq[:].rearrange("p (h v) -> p h v", v=128),
                                                            op=ALU.add, axis=AX.X), reads=[sqB], writes=[ss8B])
                            rms_rstd(ss8[:], ss8B, 8, 128, rs8[:], rs8B)
                            p.pool(TT(on[ofs][:], osum[:].rearrange("p (h v) -> p h v", v=128),
                                      rs8[:].unsqueeze(2).to_broadcast([128, 8, 128]), ALU.mult),
                                   reads=[osumB, rs8B], writes=[onB[ofs]])
                            state["pend"] = (t, ofs)
                    if rev:
                        emit_pending()
                    drain(genB)
                    for h in range(8):
                        p.pool(TS(Sp[:, h, :], Sp[:, h, :], dprev[h], 0.0, ALU.mult, ALU.add),
                               reads=[SpB[h], dprevB[h]], writes=[SpB[h]])
                        dprev[h] = ones1[:, 0:1]
                        dprevB[h] = cB
                    if rev:
                        for dc in range(8):
                            if dc % 2 == 0:
                                p.pool(TT(AT[:, dc, :], AT[:, dc, :], gT[:, dc, :], ALU.mult), reads=[ATB, gTB[dc]], writes=[ATB])
                            else:
                                p.dve(TT(AT[:, dc, :], AT[:, dc, :], gT[:, dc, :], ALU.mult), reads=[ATB, gTB[dc]], writes=[ATB])
                        for dc in range(8):
                            ps, psB_ = next_ps()
                            proj_fm(ps, wa, dc * 128, AT, 8, 0, 512, [waB, ATB], [psB_])
                            s = dc % 2
                            p.dve(TT(mATr[s][:], ps, sga[:, dc, :], ALU.mult), reads=[psB_, sgaB[dc]], writes=[mATrB[s]])
                            p.dma(DMA(mAT_st[j][:, dc * 512:(dc + 1) * 512], mATr[s][:]), reads=[mATrB[s]], writes=[mATstB])

                if not rev:
                    xload(0)
                drain(front(0))
                for it in range(NS):
                    genA = front(it + 1) if it + 1 < NS else None
                    genB = frontB(it) if rev else None
                    back(it, genB, genA)
                    drain(genA)
                p.flush()

        def sweep_c(l):
            x_src = x_in if l == 0 else x1
            with contextlib.ExitStack() as st:
                def T(name, shape, dt):
                    return st.enter_context(nc.sbuf_tensor(uniq(name), shape, dt))

                def PS(name, shape, dt):
                    return st.enter_context(nc.psum_tensor(uniq(name), shape, dt))

                gpre, gpreB = load_rowscale(st, "gpre", W["norm_mix_pre"][l], 8)
                wr = T("wr", [128, 8, 2048], BF16); wrB = p.buf("wr")
                wb = T("wb", [128, 4, 1024], BF16); wbB = p.buf("wb")
                wo = T("wo", [128, 8, 1024], BF16); woB = p.buf("wo")
                cU, cV, cGB = 0, 512, 1024
                with contextlib.ExitStack() as wst:
                    stage = make_stage(wst)
                    load_w(stage, W["w_in"][l], 8, OU, 512, wr, 0, wrB, gpre, gpreB)
                    load_w(stage, W["w_in"][l], 8, OV, 512, wr, 512, wrB, gpre, gpreB)
                    load_w(stage, W["w_in"][l], 8, OGB, 1024, wr, 1024, wrB, gpre, gpreB)
                    load_w(stage, W["w_b"][l], 4, 0, 1024, wb, 0, wbB)
                    load_w(stage, W["w_out"][l], 8, 0, 1024, wo, 0, woB)
                    p.flush()
                lng, lngB = load_bcast(st, "lng", W["sg_ln_g"][l], 512)
                lnb, lnbB = load_bcast(st, "lnb", W["sg_ln_b"][l], 512)
                gpo, gpoB = load_bcast(st, "gpo", W["norm_mix_post"][l], 1024)
                wsf = T("wsf", [128, 8, 128], F32); wsfB = p.buf("wsf")
                wsT = T("wsT", [128, 8, 128], BF16); wsTB = p.buf("wsT")
                bsb = T("bsb", [128, 4, 128], F32); bsbB = p.buf("bsb")
                psW = PS("psW", [128, 4, 128], F32); psWB = p.buf("psW")
                p.dma(DMA(wsf[:], W["sg_w"][l].rearrange("g t s -> t g s")), writes=[wsfB])
                for half in range(2):
                    for g in range(half * 4, half * 4 + 4):
                        p.pe(TR(psW[:, g % 4, :], wsf[:, g, :], identf[:]), reads=[wsfB, cB], writes=[psWB])
                    p.dve(CP(wsT[:, half * 4:half * 4 + 4, :], psW[:]), reads=[psWB], writes=[wsTB])
                for g in range(8):
                    p.dma(DMA(bsb[(g % 2) * 64:(g % 2) * 64 + 64, g // 2, :], W["sg_b"][l, g, :].partition_broadcast(64)),
                          writes=[bsbB])

                hT = T("hT", [128, 8, 512], BF16); hTB = p.buf("hT")
                mAT = T("mAT", [128, 8, 512], BF16); mATB = p.buf("mAT")
                xt = T("xt", [128, 4, D], F32); xtB = p.buf("xt")
                uT = T("uT", [128, 4, 512], BF16); uTB = p.bufs(4, "uT")
                gv = T("gv", [128, 512], F32); gvB = p.buf("gv")
                st6 = T("st6", [128, 6], F32); st6B = p.buf("st6")
                mv = T("mv", [128, 2], F32); mvB = p.buf("mv")
                rs = T("rs", [128, 1], F32); rsB = p.buf("rs")
                vh = T("vh", [128, 512], F32); vhB = p.buf("vh")
                vn = T("vn", [128, 512], BF16); vnB = p.buf("vn")
                tg = T("tg", [128, 4, 128], F32); tgB = p.buf("tg")
                BT = T("BT", [128, 4, 512], BF16); BTB = p.bufs(4, "BT")
                tE = [T(f"tE{i}", [128, 512], F32) for i in range(2)]; tEB = p.bufs(2, "tE")
                sgb = T("sgb", [128, 8, 512], BF16); sgbB = p.bufs(8, "sgb")
                tm = [T(f"tm{i}", [128, 512], F32) for i in range(2)]; tmB = p.bufs(2, "tm")
                mg = T("mg", [128, 8, 512], BF16); mgB = p.bufs(8, "mg")
                sq = T("sq", [128, 512], F32); sqB = p.buf("sq")
                ss = T("ss", [128, 2], F32); ssB = p.buf("ss")
                rstd = T("rstd", [128, 1], F32); rstdB = p.buf("rstd")
                tmp = T("tmp", [128, D], F32); tmpB = p.buf("tmp")
                yo = [T(f"yo{i}", [128, D], F32) for i in range(2)]; yoB = p.bufs(2, "yo")

                psP = [PS(f"psP{i}", [128, 512], F32) for i in range(3)]; psPB = p.bufs(3, "psP")
                psG = PS("psG", [128, 4, 128], F32); psGB = p.buf("psG")
                psX = [PS(f"psX{i}", [128, 512], F32) for i in range(2)]; psXB = p.bufs(2, "psX")
                pp = {"i": 0}

                def next_ps():
                    i = pp["i"] % 3
                    pp["i"] += 1
                    return psP[i][:], psPB[i]

                xmidstB = p.buf("xmidst")
                for j in range(NS):
                    p.dma(DMA(hT[:].rearrange("p k t -> p (k t)"), hT_st[j]), writes=[hTB])
                    p.dma(DMA(mAT[:].rearrange("p k t -> p (k t)"), mAT_st[j]), writes=[mATB])
                    p.dma(DMA(xt[:], x_src[j * 512:(j + 1) * 512, :].rearrange("(t p) d -> p t d", p=128)), writes=[xtB])
                    for c in range(4):
                        ps, psB_ = next_ps()
                        proj_fm(ps, wr, cU + c * 128, hT, 8, 0, 512, [wrB, hTB], [psB_])
                        p.act(ACT(uT[:, c, :], ps, AF.Gelu), reads=[psB_], writes=[uTB[c]])
                    for t in range(4):
                        tk = slice(t * 128, (t + 1) * 128)
                        ps, psB_ = next_ps()
                        proj_tm(ps, hT, t * 128, wr, cV, 512, 8, [wrB, hTB], [psB_])
                        p.act(ACT(gv[:], ps, AF.Gelu), reads=[psB_], writes=[gvB])
                        p.dve(lambda e: e.bn_stats(st6[:], gv[:]), reads=[gvB], writes=[st6B])
                        p.dve(lambda e: e.bn_aggr(mv[:], st6[:]), reads=[st6B], writes=[mvB])
                        p.dve(TS(rs[:], mv[:, 1:2], 1.0, EPS, ALU.mult, ALU.add), reads=[mvB], writes=[rsB])
                        p.pool(TT(rs[:], rs[:], mhalf[:, 0:1], ALU.pow), reads=[rsB, cB], writes=[rsB])
                        p.dve(TS(vh[:], gv[:], mv[:, 0:1], rs[:, 0:1], ALU.subtract, ALU.mult), reads=[gvB, mvB, rsB], writes=[vhB])
                        p.dve(TT(vh[:], vh[:], lng[:], ALU.mult), reads=[vhB, lngB], writes=[vhB])
                        p.dve(TT(vn[:], vh[:], lnb[:], ALU.add), reads=[vhB, lnbB], writes=[vnB])
                        for g in range(8):
                            p.pe(MM(psG[(g % 2) * 64:(g % 2) * 64 + 64, g // 2, :], vn[:, g * 64:(g + 1) * 64], wsT[:, g, :]),
                                 reads=[vnB, wsTB], writes=[psGB])
                        p.dve(TT(tg[:], psG[:], bsb[:], ALU.add), reads=[psGB, bsbB], writes=[tgB])
                        p.pool(TT(BT[:, :, tk], tg[:], uT[:, :, tk], ALU.mult), reads=[tgB, uTB], writes=[BTB[t]])
                    for dc in range(8):
                        ps, psB_ = next_ps()
                        proj_fm(ps, wr, cGB + dc * 128, hT, 8, 0, 512, [wrB, hTB], [psB_])
                        s = dc % 2
                        sigmoid_from_psum(ps, psB_, tE[s][:], tEB[s], sgb[:, dc, :], sgbB[dc])
                    for dc in range(8):
                        ps, psB_ = next_ps()
                        proj_fm(ps, wb, dc * 128, BT, 4, 0, 512, [wbB, BTB], [psB_])
                        s = dc % 2
                        p.dve(TT(tm[s][:], ps, sgb[:, dc, :], ALU.mult), reads=[psB_, sgbB[dc]], writes=[tmB[s]])
                        p.op("pool" if dc % 2 == 0 else "dve", TT(mg[:, dc, :], tm[s][:], mAT[:, dc, :], ALU.add),
                             reads=[tmB[s], mATB], writes=[mgB[dc]])
                    for t in range(4):
                        for hf in range(2):
                            proj_tm(psX[hf][:], mg, t * 128, wo, hf * 512, 512, 8, [woB, mgB], [psXB[hf]])
                        ob = t % 2
                        post_norm_residual([psX[0][:], psX[1][:]], psXB, sq, sqB, ss, ssB, rstd, rstdB, gpo, gpoB,
                                           xt[:, t, :], xtB, yo[ob], yoB[ob], tmp, tmpB)
                        p.dma(DMA(xmid[j * 512 + t * 128:j * 512 + (t + 1) * 128, :], yo[ob][:]), reads=[yoB[ob]],
                              writes=[xmidstB])
                p.flush()

        def sweep_d(l):
            TD = 256
            with contextlib.ExitStack() as st:
                def T(name, shape, dt):
                    return st.enter_context(nc.sbuf_tensor(uniq(name), shape, dt))

                def PS(name, shape, dt):
                    return st.enter_context(nc.psum_tensor(uniq(name), shape, dt))

                gpre, gpreB = load_rowscale(st, "gpre", W["norm_ffn_pre"][l], 8)
                wg = T("wg", [128, 8, FH], BF16); wgB = p.buf("wg")
                wu = T("wu", [128, 8, FH], BF16); wuB = p.buf("wu")
                wd = T("wd", [128, FC, D], BF16); wdB = p.buf("wd")
                with contextlib.ExitStack() as wst:
                    stage = make_stage(wst)
                    load_w(stage, W["w_gate"][l], 8, 0, FH, wg, 0, wgB, gpre, gpreB)
                    load_w(stage, W["w_up"][l], 8, 0, FH, wu, 0, wuB, gpre, gpreB)
                    load_w(stage, W["w_down"][l], FC, 0, D, wd, 0, wdB)
                    p.flush()
                gpo, gpoB = load_bcast(st, "gpo", W["norm_ffn_post"][l], 1024)

                xt = [T(f"xt{i}", [128, 2, D], F32) for i in range(2)]; xtB = p.bufs(2, "xt")
                junk = T("junk", [128, D], BF16); junkB = p.buf("junk")
                ss = T("ss", [128, 4], F32); ssB = p.buf("ss")
                rstd = T("rstd", [128, 4], F32); rstdB = p.buf("rstd")
                xs = T("xs", [128, 2, D], BF16); xsB = p.bufs(2, "xs")
                hT = T("hT", [128, 8, TD], BF16); hTB = p.buf("hT")
                sl = [T(f"sl{i}", [128, TD], F32) for i in range(2)]; slB = p.bufs(2, "sl")
                hid = T("hid", [128, FC, TD], BF16); hidB = p.bufs(FC, "hid")
                sq = T("sq", [128, 512], F32); sqB = p.buf("sq")
                ss2 = T("ss2", [128, 2], F32); ss2B = p.buf("ss2")
                rstd2 = T("rstd2", [128, 1], F32); rstd2B = p.buf("rstd2")
                tmp = T("tmp", [128, D], F32); tmpB = p.buf("tmp")
                yo = [T(f"yo{i}", [128, D], F32) for i in range(2)]; yoB = p.bufs(2, "yo")

                psTt = PS("psTt", [128, 2, 512], BF16); psT = [psTt[:, 0, :], psTt[:, 1, :]]; psTB = [p.buf("psT")] * 2
                psA = [PS(f"psA{i}", [128, 512], F32)[:, 0:TD] for i in range(2)]; psAB = p.bufs(2, "psA")
                psU = [PS(f"psU{i}", [128, 512], F32)[:, 0:TD] for i in range(2)]; psUB = p.bufs(2, "psU")
                psX = [PS(f"psX{i}", [128, 512], F32) for i in range(2)]; psXB = p.bufs(2, "psX")

                if l == 1:
                    djs = list(range(cfg.out_lo // TD, (cfg.out_lo + cfg.out_n) // TD))
                else:
                    djs = list(range(NT // TD))
                xastB = p.buf("xast")
                p.dma(DMA(xt[0][:], xmid[djs[0] * TD:(djs[0] + 1) * TD, :].rearrange("(t p) d -> p t d", p=128)), writes=[xtB[0]])
                for dit, j in enumerate(djs):
                    cur = dit % 2
                    if dit + 1 < len(djs):
                        jn = djs[dit + 1]
                        p.dma(DMA(xt[1 - cur][:], xmid[jn * TD:(jn + 1) * TD, :].rearrange("(t p) d -> p t d", p=128)),
                              writes=[xtB[1 - cur]])
                    norm_transpose((junk, junkB, ss, ssB, rstd, rstdB, xs, xsB), xt[cur], xtB[cur], 2, hT, hTB, psT, psTB)
                    for fc in range(FC):
                        s = fc % 2
                        proj_fm(psA[s][:], wg, fc * 128, hT, 8, 0, TD, [wgB, hTB], [psAB[s]])
                        proj_fm(psU[s][:], wu, fc * 128, hT, 8, 0, TD, [wuB, hTB], [psUB[s]])
                        p.act(ACT(sl[s][:], psA[s][:], AF.Silu), reads=[psAB[s]], writes=[slB[s]])
                        p.dve(TT(hid[:, fc, :], psU[s][:], sl[s][:], ALU.mult), reads=[psUB[s], slB[s]], writes=[hidB[fc]])
                    for t in range(2):
                        for hf in range(2):
                            for fc in range(FC):
                                p.pe(MM(psX[hf][:], hid[:, fc, t * 128:(t + 1) * 128], wd[:, fc, hf * 512:(hf + 1) * 512],
                                        fc == 0, fc == FC - 1), reads=[hidB[fc], wdB], writes=[psXB[hf]])
                        ob = t % 2
                        post_norm_residual([psX[0][:], psX[1][:]], psXB, sq, sqB, ss2, ss2B, rstd2, rstd2B, gpo, gpoB,
                                           xt[cur][:, t, :], xtB[cur], yo[ob], yoB[ob], tmp, tmpB)
                        p.dma(DMA(xa[j * TD + t * 128:j * TD + (t + 1) * 128, :], yo[ob][:]), reads=[yoB[ob]],
                              writes=[xastB])
                p.flush()

        def sweep_e(l, final):
            TD = 256
            with contextlib.ExitStack() as st:
                def T(name, shape, dt):
                    return st.enter_context(nc.sbuf_tensor(uniq(name), shape, dt))

                def PS(name, shape, dt):
                    return st.enter_context(nc.psum_tensor(uniq(name), shape, dt))

                wp = T("wp", [128, 2, D], BF16); wpB = p.buf("wp")
                wq = T("wq", [128, 8, D], BF16); wqB = p.buf("wq")
                with contextlib.ExitStack() as wst:
                    stage = make_stage(wst)
                    load_w(stage, W["w_ple"][l], 2, 0, D, wp, 0, wpB)
                    load_w(stage, W["w_ple_gate"][l], 8, 0, D, wq, 0, wqB)
                    p.flush()
                xt = [T(f"xt{i}", [128, 2, D], F32) for i in range(3)]; xtB = p.bufs(3, "xt")
                pt = [T(f"pt{i}", [128, 2, 256], F32) for i in range(3)]; ptB = p.bufs(3, "pt")
                xb = [T(f"xb{i}", [128, 2, D], BF16) for i in range(2)]; xbB = [p.bufs(2, f"xb{i}_") for i in range(2)]
                pb = [T(f"pb{i}", [128, 2, 256], BF16) for i in range(2)]; pbB = p.bufs(2, "pb")
                xT = [T(f"xT{i}", [128, 8, TD], BF16) for i in range(2)]; xTB = p.bufs(2, "xT")
                pT = [T(f"pT{i}", [128, 2, TD], BF16) for i in range(2)]; pTB = p.bufs(2, "pT")
                tE = [T(f"tE{i}", [128, 512], F32) for i in range(2)]; tEB = p.bufs(2, "tE")
                t2 = [T(f"t2{i}", [128, 512], F32) for i in range(2)]; t2B = p.bufs(2, "t2")
                yo = [T(f"yo{i}", [128, D], F32) for i in range(2)]; yoB = p.bufs(2, "yo")
                psT = [PS(f"psT{i}", [128, 1024], BF16)[:, 0:TD] for i in range(2)]; psTB = p.bufs(2, "psT")
                psG = [PS(f"psG{i}", [128, 512], F32) for i in range(2)]; psGB = p.bufs(2, "psG")
                psP = [PS(f"psP{i}", [128, 512], F32) for i in range(2)]; psPB = p.bufs(2, "psP")

                if final:
                    js = list(range(cfg.out_lo // TD, (cfg.out_lo + cfg.out_n) // TD))
                else:
                    js = list(range(NT // TD))
                nj = len(js)

                def ld(it):
                    if it >= nj:
                        return
                    j = js[it]
                    s = it % 3
                    p.dma(DMA(xt[s][:], xa[j * TD:(j + 1) * TD, :].rearrange("(t p) d -> p t d", p=128)), writes=[xtB[s]])
                    p.dma(DMA(pt[s][:], p_in[l, j * TD:(j + 1) * TD, :].rearrange("(t p) d -> p t d", p=128)), writes=[ptB[s]])

                def prep(it):
                    if it >= nj:
                        return
                    s3 = it % 3
                    b = it % 2
                    p.act(ACP(xb[b][:, 0, :], xt[s3][:, 0, :]), reads=[xtB[s3]], writes=[xbB[b][0]])
                    p.pool(CP(xb[b][:, 1, :], xt[s3][:, 1, :]), reads=[xtB[s3]], writes=[xbB[b][1]])
                    p.pool(CP(pb[b][:], pt[s3][:]), reads=[ptB[s3]], writes=[pbB[b]])
                    for kc in range(8):
                        s = kc % 2
                        for t in range(2):
                            p.pe(TR(psT[s][:, t * 128:(t + 1) * 128], xb[b][:, t, kc * 128:(kc + 1) * 128], identb[:]),
                                 reads=[xbB[b][t], cB], writes=[psTB[s]])
                        if s == 0:
                            p.dve(CP(xT[b][:, kc, :], psT[s][:, 0:TD]), reads=[psTB[s]], writes=[xTB[b]])
                        else:
                            p.act(ACP(xT[b][:, kc, :], psT[s][:, 0:TD]), reads=[psTB[s]], writes=[xTB[b]])
                    for kc in range(2):
                        s = kc % 2
                        for t in range(2):
                            p.pe(TR(psT[s][:, t * 128:(t + 1) * 128], pb[b][:, t, kc * 128:(kc + 1) * 128], identb[:]),
                                 reads=[pbB[b], cB], writes=[psTB[s]])
                        p.dve(CP(pT[b][:, kc, :], psT[s][:, 0:TD]), reads=[psTB[s]], writes=[pTB[b]])

                x1stB = p.buf("x1st")

                def compute(it):
                    j = js[it]
                    s3 = it % 3
                    b = it % 2
                    for t in range(2):
                        ob = t % 2
                        for hf in range(2):
                            sl = slice(hf * 512, (hf + 1) * 512)
                            proj_tm(psG[hf][:], xT[b], t * 128, wq, hf * 512, 512, 8, [wqB, xTB[b]], [psGB[hf]])
                            proj_tm(psP[hf][:], pT[b], t * 128, wp, hf * 512, 512, 2, [wpB, pTB[b]], [psPB[hf]])
                            sigmoid_from_psum(psG[hf][:], psGB[hf], tE[hf][:], tEB[hf], tE[hf][:], tEB[hf])
                            p.dve(TT(t2[hf][:], psP[hf][:], tE[hf][:], ALU.mult), reads=[psPB[hf], tEB[hf]], writes=[t2B[hf]])
                            p.pool(TT(yo[ob][:, sl], t2[hf][:], xt[s3][:, t, sl], ALU.add), reads=[t2B[hf], xtB[s3]],
                                   writes=[yoB[ob]])
                        r0 = j * TD + t * 128
                        if final:
                            dst = out[r0 - cfg.out_lo:r0 - cfg.out_lo + 128, :]
                        else:
                            dst = x1[r0:r0 + 128, :]
                        p.dma(DMA(dst, yo[ob][:]), reads=[yoB[ob]], writes=[x1stB])

                ld(0)
                ld(1)
                prep(0)
                for it in range(nj):
                    ld(it + 2)
                    prep(it + 1)
                    compute(it)
                p.flush()

        steps = []
        for l in range(2):
            steps += [lambda l=l: sweep_hgrn(l, False), lambda l=l: sweep_hgrn(l, True), lambda l=l: sweep_c(l),
                      lambda l=l: sweep_d(l), lambda l=l: sweep_e(l, final=(l == 1))]
        if cfg.stop_after is not None:
            steps = steps[:cfg.stop_after] if isinstance(cfg.stop_after, int) else [steps[i] for i in cfg.stop_after]
        for f in steps:
            f()
        ninst = p.ninst
    return nc, ninst


SEG = 4096
HALO = 256
_WNAMES = ["norm_mix_pre", "w_in", "lb_gamma_fwd", "lb_gamma_bwd", "hg_norm", "sg_w", "sg_b", "sg_ln_g",
           "sg_ln_b", "w_a", "w_b", "w_out", "norm_mix_post", "norm_ffn_pre", "w_gate", "w_up", "w_down",
           "norm_ffn_post", "w_ple", "w_ple_gate"]
_CACHE = {}


def kernel(**inputs):
    x = np.asarray(inputs["x"], dtype=np.float32)
    pp = np.asarray(inputs["p"], dtype=np.float32)
    Bn, S, _ = x.shape
    nseg = S // SEG
    ncores = Bn * nseg
    NT = SEG + 2 * HALO
    key = (NT,)
    if key not in _CACHE:
        _CACHE[key] = build(Cfg(NT, HALO, SEG))[0]
    nc = _CACHE[key]
    wts = {k: np.ascontiguousarray(np.asarray(inputs[k], dtype=np.float32)) for k in _WNAMES}
    in_maps = []
    for b in range(Bn):
        for sgm in range(nseg):
            lo = sgm * SEG - HALO
            hi = (sgm + 1) * SEG + HALO
            xs = np.zeros((NT, D), np.float32)
            ps = np.zeros((2, NT, 256), np.float32)
            a, bnd = max(lo, 0), min(hi, S)
            xs[a - lo:bnd - lo] = x[b, a:bnd]
            ps[:, a - lo:bnd - lo] = pp[:, b, a:bnd]
            m = {"x": xs, "p": ps}
            m.update(wts)
            in_maps.append(m)
    res = run_bass_kernel_spmd(nc, in_maps, core_ids=list(range(ncores)))
    out = np.empty((Bn, S, D), np.float32)
    i = 0
    for b in range(Bn):
        for sgm in range(nseg):
            out[b, sgm * SEG:(sgm + 1) * SEG] = res.results[i]["out"]
            i += 1
    return out
```

```python
import contextlib
import numpy as np
import concourse.bass as bass
import concourse.mybir as mybir
from concourse.bass_utils import run_bass_kernel_spmd

F32 = mybir.dt.float32
BF16 = mybir.dt.bfloat16
AF = mybir.ActivationFunctionType
ALU = mybir.AluOpType
AX = mybir.AxisListType

ENGINES = ("pe", "act", "dve", "pool", "sp")
N_DSEM = 40

D = 1024
NIN = 8192
FH = 2816
FC = FH // 128
EPS = 1e-6
OQ, OFF, OFB, OI, OG, OU, OV, OGA, OGB = 0, 1024, 2048, 3072, 4096, 5120, 5632, 6144, 7168


class Buf:
    __slots__ = ("name", "w", "r")

    def __init__(self, name):
        self.name = name
        self.w = None
        self.r = []


class Op:
    __slots__ = ("eng", "fn", "deps", "sig", "sigval", "dma", "key", "flushed", "sem")

    def __init__(self, eng, fn, dma, key):
        self.eng = eng
        self.fn = fn
        self.deps = []
        self.sig = False
        self.sigval = 0
        self.dma = dma
        self.key = key
        self.flushed = False


def _flat(xs):
    out = []
    for x in xs:
        if isinstance(x, (list, tuple)):
            out.extend(_flat(x))
        elif x is not None:
            out.append(x)
    return out


class Prog:
    def __init__(self, nc, st):
        self.nc = nc
        self.ops = {e: [] for e in ENGINES}
        self.nbuf = 0
        self.sweep = 0
        self.esem = [{e: st.enter_context(nc.semaphore(f"s{k}_{e}")) for e in ENGINES} for k in range(2)]
        self.dsem = [[st.enter_context(nc.semaphore(f"d{k}_{i}")) for i in range(N_DSEM)] for k in range(2)]
        self.ninst = 0

    def buf(self, name=None):
        self.nbuf += 1
        return Buf(name or f"b{self.nbuf}")

    def bufs(self, n, name="b"):
        return [self.buf(f"{name}{i}") for i in range(n)]

    def op(self, eng, fn, reads=(), writes=(), dma=False):
        reads = _flat(reads)
        writes = _flat(writes)
        key = None
        if dma:
            key = writes[0]
        o = Op(eng, fn, dma, key)
        deps = []
        for b in reads:
            if b.w is not None:
                deps.append(b.w)
        for b in writes:
            if b.w is not None:
                deps.append(b.w)
            deps.extend(b.r)
        for b in reads:
            b.r.append(o)
        for b in writes:
            b.w = o
            b.r = []
        seen = set()
        for d in deps:
            if d is o or id(d) in seen or d.flushed:
                continue
            seen.add(id(d))
            if d.eng == "pe" and eng == "pe" and not d.dma:
                continue
            o.deps.append(d)
            d.sig = True
        self.ops[eng].append(o)
        return o

    def pe(self, fn, reads=(), writes=()):
        return self.op("pe", fn, reads, writes)

    def act(self, fn, reads=(), writes=()):
        return self.op("act", fn, reads, writes)

    def dve(self, fn, reads=(), writes=()):
        return self.op("dve", fn, reads, writes)

    def pool(self, fn, reads=(), writes=()):
        return self.op("pool", fn, reads, writes)

    def dma(self, fn, reads=(), writes=()):
        return self.op("sp", fn, reads, writes, dma=True)

    def flush(self):
        nc = self.nc
        k = self.sweep % 2
        esem = self.esem[k]
        dpool = self.dsem[k]
        other_e = self.esem[1 - k]
        other_d = self.dsem[1 - k]
        dma_keys = {}
        nsem = [0]
        allsems = []
        for e in ENGINES:
            cnt = 0
            for o in self.ops[e]:
                if o.dma:
                    kk = id(o.key)
                    if kk not in dma_keys or dma_keys[kk][1] + 16 > 224:
                        assert nsem[0] < N_DSEM, "too many DMA semaphores in one sweep"
                        dma_keys[kk] = [dpool[nsem[0]], 0]
                        allsems.append(dma_keys[kk])
                        nsem[0] += 1
                    dma_keys[kk][1] += 16
                    o.sigval = dma_keys[kk][1]
                    o.sem = dma_keys[kk][0]
                elif o.sig:
                    cnt += 1
                    o.sigval = cnt
        ops = self.ops
        first = self.sweep == 0

        def body(ename):
            def f(eng):
                waited = {}

                def wait_for(d):
                    s = d.sem if d.dma else esem[d.eng]
                    if waited.get(id(s), 0) >= d.sigval:
                        return
                    waited[id(s)] = d.sigval
                    eng.wait_ge(s, d.sigval)

                if ename == "sp" and not first:
                    for s in list(other_e.values()) + list(other_d):
                        eng.sem_clear(s)
                for o in ops[ename]:
                    for d in o.deps:
                        wait_for(d)
                    ins = o.fn(eng)
                    self.ninst += 1
                    if o.dma:
                        ins.then_inc(o.sem, 16)
                    elif o.sig:
                        ins.then_inc(esem[ename], 1)
                if ename == "sp":
                    for s, tot in allsems:
                        if tot > 0:
                            eng.wait_ge(s, tot)
            return f

        with nc.allow_low_precision(reason="bf16 matmul operands, fp32 accumulation"), nc.Block() as block:
            block.tensor(body("pe"))
            block.scalar(body("act"))
            block.vector(body("dve"))
            block.gpsimd(body("pool"))
            block.sync(body("sp"))
        for e in ENGINES:
            for o in self.ops[e]:
                o.flushed = True
                o.fn = None
        self.ops = {e: [] for e in ENGINES}
        self.sweep += 1


def MM(out, lhsT, rhs, start=True, stop=True):
    return lambda e: e.matmul(out, lhsT=lhsT, rhs=rhs, start=start, stop=stop)


def TR(out, in_, ident):
    return lambda e: e.transpose(out, in_, ident)


def ACT(out, in_, func, bias=None, scale=None, accum=None):
    kw = {}
    if bias is not None:
        kw["bias"] = bias
    if scale is not None:
        kw["scale"] = scale
    if accum is not None:
        kw["accum_out"] = accum
    return lambda e: e.activation(out, in_, func, **kw)


def CP(out, in_):
    return lambda e: e.tensor_copy(out, in_)


def ACP(out, in_):
    return lambda e: e.copy(out, in_)


def TT(out, a, b, op):
    return lambda e: e.tensor_tensor(out=out, in0=a, in1=b, op=op)


def TS(out, a, s1, s2, op0, op1):
    return lambda e: e.tensor_scalar(out=out, in0=a, scalar1=s1, scalar2=s2, op0=op0, op1=op1)


def STT(out, a, s, b, op0, op1):
    return lambda e: e.scalar_tensor_tensor(out=out, in0=a, scalar=s, in1=b, op0=op0, op1=op1)


def RECIP(out, in_):
    return lambda e: e.reciprocal(out, in_)


def MSET(ap, v):
    return lambda e: e.memset(ap, v)


def DMA(out, in_, slow=False):
    if slow:
        return lambda e: e.dma_start(out=out, in_=in_, allow_slow_non_contiguous=True)
    return lambda e: e.dma_start(out=out, in_=in_)


_UID = {"n": 0}


def uniq(name):
    _UID["n"] += 1
    return f"{name}_{_UID['n']}"


class Cfg:
    def __init__(self, NT, out_lo, out_n, debug=False, stop_after=None):
        self.stop_after = stop_after
        self.NT = NT
        self.out_lo = out_lo
        self.out_n = out_n
        self.debug = debug


def build(cfg):
    NT = cfg.NT
    NS = NT // 512
    nc = bass.Bass("TRN2", target_bir_lowering=False)

    def din(name, shape):
        return nc.dram_tensor(name, shape, F32, kind="ExternalInput").ap()

    x_in = din("x", [NT, D])
    p_in = din("p", [2, NT, 256])
    W = {}
    for nm, shp in [("norm_mix_pre", [2, D]), ("w_in", [2, D, NIN]), ("lb_gamma_fwd", [2, D]),
                    ("lb_gamma_bwd", [2, D]), ("hg_norm", [2, D]), ("sg_w", [2, 8, 128, 128]),
                    ("sg_b", [2, 8, 128]), ("sg_ln_g", [2, 512]), ("sg_ln_b", [2, 512]),
                    ("w_a", [2, D, D]), ("w_b", [2, 512, D]), ("w_out", [2, D, D]),
                    ("norm_mix_post", [2, D]), ("norm_ffn_pre", [2, D]), ("w_gate", [2, D, FH]),
                    ("w_up", [2, D, FH]), ("w_down", [2, FH, D]), ("norm_ffn_post", [2, D]),
                    ("w_ple", [2, 256, D]), ("w_ple_gate", [2, D, D])]:
        W[nm] = din(nm, shp)
    out = nc.dram_tensor("out", [cfg.out_n, D], F32, kind="ExternalOutput").ap()

    okind = "ExternalOutput" if cfg.debug else "Internal"

    def dscr(name, shape, dt):
        return nc.dram_tensor(name, shape, dt, kind=okind).ap()

    hT_st = dscr("hT_st", [NS, 128, 8 * 512], BF16)
    qT_st = dscr("qT_st", [NS, 128, 8 * 512], BF16)
    v_st = dscr("v_st", [NS, 128, 4 * 1024], BF16)
    of_st = dscr("of_st", [NS, 128, 4 * 1024], F32)
    mAT_st = dscr("mAT_st", [NS, 128, 8 * 512], BF16)
    xmid = dscr("xmid", [NT, D], F32)
    xa = dscr("xa", [NT, D], F32)
    x1 = dscr("x1", [NT, D], F32)

    with contextlib.ExitStack() as gst:
        p = Prog(nc, gst)

        def GT(name, shape, dt):
            return gst.enter_context(nc.sbuf_tensor(uniq(name), shape, dt))

        identf = GT("identf", [128, 128], F32)
        identb = GT("identb", [128, 128], BF16)
        scanm = GT("scanm", [128, 512], F32)
        maskf = GT("maskf", [128, 8, 64], F32)
        maskb = GT("maskb", [128, 8, 64], F32)
        ones1 = GT("ones1", [128, 1], F32)
        mhalf = GT("mhalf", [128, 8], F32)
        lbt = GT("lbt", [128, 2, 2, 8], F32)
        omlt = GT("omlt", [128, 2, 2, 8], F32)
        nomlt = GT("nomlt", [128, 2, 2, 8], F32)
        gam = GT("gam", [128, 2, 2, 8], F32)
        cB = p.buf("consts")

        p.pool(MSET(identf[:], 0.0), writes=[cB])
        p.pool(lambda e: e.affine_select(out=identf[:], in_=identf[:], pattern=[[-1, 128]],
                                         compare_op=ALU.not_equal, fill=1.0, base=0,
                                         channel_multiplier=1), reads=[cB], writes=[cB])
        p.dve(CP(identb[:], identf[:]), reads=[cB], writes=[cB])
        p.pool(MSET(scanm[:], 1.0), writes=[cB])
        p.pool(MSET(scanm[:].rearrange("p (c t) -> p c t", t=64)[:, :, 0:1], 0.0), writes=[cB])
        p.pool(MSET(ones1[:], 1.0), writes=[cB])
        p.pool(MSET(mhalf[:], -0.5), writes=[cB])
        p.pool(MSET(maskf[:], 1.0), writes=[cB])
        p.pool(MSET(maskb[:], 1.0), writes=[cB])
        for lo in (0, 64):
            p.pool(lambda e, lo=lo: e.affine_select(out=maskf[lo:lo + 64], in_=maskf[lo:lo + 64],
                                                    pattern=[[0, 8], [1, 64]], compare_op=ALU.is_ge,
                                                    fill=0.0, base=0, channel_multiplier=-1),
                   reads=[cB], writes=[cB])
            p.pool(lambda e, lo=lo: e.affine_select(out=maskb[lo:lo + 64], in_=maskb[lo:lo + 64],
                                                    pattern=[[0, 8], [-1, 64]], compare_op=ALU.is_ge,
                                                    fill=0.0, base=0, channel_multiplier=1),
                   reads=[cB], writes=[cB])
        gB = p.buf("gam")
        for di, nm in enumerate(("lb_gamma_fwd", "lb_gamma_bwd")):
            for l in range(2):
                p.dma(DMA(gam[:, di, l, :], W[nm][l].rearrange("(h p) -> p h", p=128), slow=True), writes=[gB])
        p.act(ACT(gam[:], gam[:], AF.Exp), reads=[gB], writes=[gB])
        for di in range(2):
            p.dve(TT(omlt[:, di, 0, :], gam[:, di, 0, :], gam[:, di, 1, :], ALU.add), reads=[gB], writes=[cB])
            p.dve(RECIP(omlt[:, di, 0, :], omlt[:, di, 0, :]), reads=[cB], writes=[cB])
            p.dve(TT(lbt[:, di, 1, :], gam[:, di, 1, :], omlt[:, di, 0, :], ALU.mult), reads=[gB, cB], writes=[cB])
            p.dve(MSET(lbt[:, di, 0, :], 0.0), reads=[cB], writes=[cB])
        p.dve(TS(omlt[:], lbt[:], -1.0, 1.0, ALU.mult, ALU.add), reads=[cB], writes=[cB])
        p.dve(TS(nomlt[:], omlt[:], -1.0, None, ALU.mult, ALU.bypass), reads=[cB], writes=[cB])
        p.flush()

        wl_state = {"i": 0}

        def load_rowscale(st, name, dram_vec, kc_n):
            t = st.enter_context(nc.sbuf_tensor(uniq(name), [128, kc_n], F32))
            b = p.buf(name)
            p.dma(DMA(t[:], dram_vec.rearrange("(c p) -> p c", p=128), slow=True), writes=[b])
            return t, b

        def load_bcast(st, name, dram_vec, n):
            t = st.enter_context(nc.sbuf_tensor(uniq(name), [128, n], F32))
            b = p.buf(name)
            p.dma(DMA(t[:], dram_vec.partition_broadcast(128)), writes=[b])
            return t, b

        def make_stage(st):
            stg = [st.enter_context(nc.sbuf_tensor(uniq(f"wstg{i}"), [128, 2048], F32)) for i in range(4)]
            return stg, p.bufs(4, "wstg")

        def load_w(stage, wd, kc_n, c0, ncols, dst, dc0, dstB, scale=None, scaleB=None):
            stg, sB = stage
            for kc in range(kc_n):
                for c in range(0, ncols, 2048):
                    n = min(2048, ncols - c)
                    i = wl_state["i"]
                    wl_state["i"] += 1
                    s = i % 4
                    p.dma(DMA(stg[s][:, :n], wd[kc * 128:(kc + 1) * 128, c0 + c:c0 + c + n]), writes=[sB[s]])
                    o_ap = dst[:, kc, dc0 + c:dc0 + c + n]
                    eng = ("dve", "act", "dve", "act", "pool")[i % 5]
                    if scale is None:
                        if eng == "act":
                            p.act(ACP(o_ap, stg[s][:, :n]), reads=[sB[s]], writes=[dstB])
                        else:
                            p.op(eng, CP(o_ap, stg[s][:, :n]), reads=[sB[s]], writes=[dstB])
                    else:
                        sc = scale[:, kc:kc + 1]
                        if eng == "act":
                            p.act(ACT(o_ap, stg[s][:, :n], AF.Copy, scale=sc), reads=[sB[s], scaleB], writes=[dstB])
                        else:
                            p.op(eng, TS(o_ap, stg[s][:, :n], sc, 0.0, ALU.mult, ALU.add),
                                 reads=[sB[s], scaleB], writes=[dstB])

        def proj_fm(ps, w, c0, inT, kc_n, tok0, ntok, rd, wr):
            for kc in range(kc_n):
                p.pe(MM(ps, w[:, kc, c0:c0 + 128], inT[:, kc, tok0:tok0 + ntok], kc == 0, kc == kc_n - 1),
                     reads=rd, writes=wr)

        def proj_tm(ps, inT, tok0, w, c0, ncols, kc_n, rd, wr):
            for kc in range(kc_n):
                p.pe(MM(ps, inT[:, kc, tok0:tok0 + 128], w[:, kc, c0:c0 + ncols], kc == 0, kc == kc_n - 1),
                     reads=rd, writes=wr)

        def sigmoid_from_psum(ps, psB, tmp, tmpB, out_ap, outB):
            p.act(ACT(tmp, ps, AF.Exp, scale=-1.0), reads=[psB], writes=[tmpB])
            p.act(ACT(tmp, tmp, AF.Ln, bias=1.0), reads=[tmpB], writes=[tmpB])
            p.act(ACT(out_ap, tmp, AF.Exp, scale=-1.0), reads=[tmpB], writes=[outB])

        def rms_rstd(ss, ssB, ncol, n, rstd, rstdB):
            p.dve(TS(ss, ss, 1.0 / n, EPS, ALU.mult, ALU.add), reads=[ssB], writes=[ssB])
            p.pool(TT(rstd, ss, mhalf[:, 0:ncol], ALU.pow), reads=[ssB, cB], writes=[rstdB])

        def norm_transpose(st_tiles, xt, xtB, ntile, hT, hTB, psT, psTB):
            junk, junkB, ss, ssB, rstd, rstdB, xs, xsB = st_tiles
            for t in range(ntile):
                p.act(ACT(junk[:], xt[:, t, :], AF.Square, accum=ss[:, t:t + 1]), reads=[xtB], writes=[junkB, ssB])
            rms_rstd(ss[:, 0:ntile], ssB, ntile, D, rstd[:, 0:ntile], rstdB)
            for t in range(ntile):
                if t % 2 == 0:
                    p.act(ACT(xs[:, t, :], xt[:, t, :], AF.Copy, scale=rstd[:, t:t + 1]), reads=[xtB, rstdB], writes=[xsB[t]])
                else:
                    p.dve(TS(xs[:, t, :], xt[:, t, :], rstd[:, t:t + 1], 0.0, ALU.mult, ALU.add),
                          reads=[xtB, rstdB], writes=[xsB[t]])
            for kc in range(8):
                s = kc % 2
                for t in range(ntile):
                    p.pe(TR(psT[s][:, t * 128:(t + 1) * 128], xs[:, t, kc * 128:(kc + 1) * 128], identb[:]),
                         reads=[xsB[t], cB], writes=[psTB[s]])
                if kc % 2 == 0:
                    p.dve(CP(hT[:, kc, 0:ntile * 128], psT[s][:, 0:ntile * 128]), reads=[psTB[s]], writes=[hTB])
                else:
                    p.act(ACP(hT[:, kc, 0:ntile * 128], psT[s][:, 0:ntile * 128]), reads=[psTB[s]], writes=[hTB])

        _tmp_halves = {}

        def post_norm_residual(psX, psXB, sq, sqB, ss, ssB, rstd, rstdB, gbc, gbcB, xres, xresB, yout, youtB, tmp, tmpB):
            for hf in range(2):
                p.act(ACT(sq[:, 0:512], psX[hf], AF.Square, accum=ss[:, hf:hf + 1]), reads=[psXB[hf]], writes=[sqB, ssB])
            p.dve(TT(ss[:, 0:1], ss[:, 0:1], ss[:, 1:2], ALU.add), reads=[ssB], writes=[ssB])
            rms_rstd(ss[:, 0:1], ssB, 1, D, rstd[:, 0:1], rstdB)
            tb = _tmp_halves.setdefault(id(tmpB), [tmpB, p.buf("tmp_hi")])
            for hf in range(2):
                sl = slice(hf * 512, (hf + 1) * 512)
                p.dve(STT(tmp[:, sl], psX[hf], rstd[:, 0:1], gbc[:, sl], ALU.mult, ALU.mult),
                      reads=[psXB[hf], rstdB, gbcB], writes=[tb[hf]])
            for hf in range(2):
                sl = slice(hf * 512, (hf + 1) * 512)
                p.op("pool" if hf == 0 else "dve", TT(yout[:, sl], tmp[:, sl], xres[:, sl], ALU.add),
                     reads=[tb[hf], xresB], writes=[youtB])

        def sweep_hgrn(l, rev):
            x_src = x_in if l == 0 else x1
            di = 1 if rev else 0
            with contextlib.ExitStack() as st:
                def T(name, shape, dt):
                    return st.enter_context(nc.sbuf_tensor(uniq(name), shape, dt))

                def PS(name, shape, dt):
                    return st.enter_context(nc.psum_tensor(uniq(name), shape, dt))

                wl = W["w_in"][l]
                gpre, gpreB = load_rowscale(st, "gpre", W["norm_mix_pre"][l], 8)
                wr = T("wr", [128, 8, 3072], BF16)
                wrB = p.buf("wr")
                if rev:
                    gh, ghB = load_rowscale(st, "gh", W["hg_norm"][l], 8)
                    wa = T("wa", [128, 8, 1024], BF16)
                    waB = p.buf("wa")
                with contextlib.ExitStack() as wst:
                    stage = make_stage(wst)
                    if not rev:
                        load_w(stage, wl, 8, OQ, 1024, wr, 0, wrB, gpre, gpreB)
                        load_w(stage, wl, 8, OFF, 1024, wr, 1024, wrB, gpre, gpreB)
                        load_w(stage, wl, 8, OI, 1024, wr, 2048, wrB, gpre, gpreB)
                    else:
                        load_w(stage, wl, 8, OFB, 1024, wr, 0, wrB, gpre, gpreB)
                        load_w(stage, wl, 8, OG, 1024, wr, 1024, wrB, gpre, gpreB)
                        load_w(stage, wl, 8, OGA, 1024, wr, 2048, wrB, gpre, gpreB)
                        load_w(stage, W["w_a"][l], 8, 0, 1024, wa, 0, waB, gh, ghB)
                    p.flush()
                if not rev:
                    cQ, cF, cI = 0, 1024, 2048
                else:
                    cF, cG, cGA = 0, 1024, 2048

                hT = T("hT", [128, 8, 512], BF16); hTB = p.buf("hT")
                vtm = [T(f"vtm{i}", [128, 4, 1024], BF16) for i in range(2)]; vtmB = [p.bufs(4, f"vtm{i}_") for i in range(2)]
                qtT = [T(f"qtT{i}", [128, 8, 512], BF16) for i in range(2)]; qtTB = [p.bufs(8, f"qtT{i}_") for i in range(2)]
                ktT = [T(f"ktT{i}", [128, 8, 512], BF16) for i in range(2)]; ktTB = [p.bufs(8, f"ktT{i}_") for i in range(2)]
                dsv = [T(f"dsv{i}", [128, 8, 8], F32) for i in range(2)]; dsvB = [p.bufs(8, f"dsv{i}_") for i in range(2)]
                tE = [T(f"tE{i}", [128, 512], F32) for i in range(2)]; tEB = p.bufs(2, "tE")
                tS = [T(f"tS{i}", [128, 512], F32) for i in range(2)]; tSB = p.bufs(2, "tS")
                tEb = T("tEb", [128, 512], F32); tEbB = p.buf("tEb")
                tL = T("tL", [128, 512], F32); tLB = p.buf("tL")
                tK = T("tK", [128, 512], F32); tKB = p.buf("tK")
                tB = T("tB", [128, 512], F32); tBB = p.buf("tB")
                tN = T("tN", [128, 512], F32); tNB = p.buf("tN")
                ktm = T("ktm", [128, 8, 128], BF16); ktmB = p.bufs(2, "ktm")
                PT = T("PT", [128, 8, 64], BF16); PTB = p.buf("PT")
                Sp = T("Sp", [128, 8, 128], F32); SpB = p.bufs(8, "Sp")
                Sbf = T("Sbf", [128, 8, 128], BF16); SbfB = p.bufs(8, "Sbf")

                psP = [PS(f"psP{i}", [128, 512], F32) for i in range(2)]; psPB = p.bufs(2, "psP")
                psS = PS("psS", [128, 8, 64], F32); psSB = p.buf("psS")
                psO = PS("psO", [128, 8, 128], F32); psOB = p.buf("psO")
                psOf = psO[:].rearrange("p h v -> p (h v)")
                psM = PS("psM", [128, 8, 128], F32); psMB = [p.buf("psMa")] * 4 + [p.buf("psMb")] * 4
                psK = PS("psK", [128, 8, 128], BF16); psKB = [p.buf("psK")] * 2
                pp = {"i": 0}

                def next_ps():
                    i = pp["i"] % 2
                    pp["i"] += 1
                    return psP[i][:], psPB[i]

                if not rev:
                    xt = [T(f"xt{i}", [128, 4, D], F32) for i in range(2)]; xtB = p.bufs(2, "xt")
                    junk = T("junk", [128, D], BF16); junkB = p.buf("junk")
                    ss = T("ss", [128, 4], F32); ssB = p.buf("ss")
                    rstd = T("rstd", [128, 4], F32); rstdB = p.buf("rstd")
                    xs = T("xs", [128, 4, D], BF16); xsB = p.bufs(4, "xs")
                    osb = [T(f"osb{i}", [128, 1024], F32) for i in range(2)]; osbB = p.bufs(2, "osb")
                    psKf = psK[:].rearrange("p h k -> p (h k)")
                    psT = [psKf[:, 0:512], psKf[:, 512:1024]]; psTB = [psKB[0]] * 2
                    hTstB, qTstB, vstB, ofstB = p.buf("hTst"), p.buf("qTst"), p.buf("vst"), p.buf("ofst")
                else:
                    oft = [T(f"oft{i}", [128, 1024], F32) for i in range(2)]; oftB = p.bufs(2, "oft")
                    gT = T("gT", [128, 8, 512], BF16); gTB = p.bufs(8, "gT")
                    sga = T("sga", [128, 8, 512], BF16); sgaB = p.bufs(8, "sga")
                    AT = T("AT", [128, 8, 512], BF16); ATB = p.bufs(4, "AT")
                    mATr = [T(f"mATr{i}", [128, 512], BF16) for i in range(2)]; mATrB = p.bufs(2, "mATr")
                    osum = T("osum", [128, 1024], F32); osumB = p.buf("osum")
                    sq = T("sq", [128, 1024], F32); sqB = p.buf("sq")
                    ss8 = T("ss8", [128, 8], F32); ss8B = p.buf("ss8")
                    rs8 = T("rs8", [128, 8], F32); rs8B = p.buf("rs8")
                    on = [T(f"on{i}", [128, 8, 128], BF16) for i in range(2)]; onB = p.bufs(2, "on")
                    psOT = psK; psOTB = psKB[0]
                    mATstB = p.buf("mATst")

                for h in range(8):
                    p.pool(MSET(Sp[:, h, :], 0.0), writes=[SpB[h]])
                    p.pool(MSET(Sbf[:, h, :], 0.0), writes=[SbfB[h]])
                mask = maskb if rev else maskf

                order = list(range(NS))
                if rev:
                    order = order[::-1]

                def xload(it):
                    j = order[it]
                    p.dma(DMA(xt[it % 2][:], x_src[j * 512:(j + 1) * 512, :].rearrange("(t p) d -> p t d", p=128)),
                          writes=[xtB[it % 2]])

                def gate_head(b, h):
                    s = h % 2
                    ps, psB_ = next_ps()
                    proj_fm(ps, wr, cF + h * 128, hT, 8, 0, 512, [wrB, hTB], [psB_])
                    sigmoid_from_psum(ps, psB_, tE[s][:], tEB[s], tS[s][:], tSB[s])
                    lb_c = lbt[:, di, l, h:h + 1]
                    oml_c = omlt[:, di, l, h:h + 1]
                    noml_c = nomlt[:, di, l, h:h + 1]
                    p.act(ACT(tL[:], tS[s][:], AF.Ln, bias=lb_c, scale=oml_c), reads=[tSB[s], cB], writes=[tLB])
                    p.dve(TS(tK[:], tS[s][:], noml_c, oml_c, ALU.mult, ALU.add), reads=[tSB[s], cB], writes=[tKB])
                    p.dve(lambda e: e.tensor_tensor_scan(out=tB[:], data0=scanm[:], data1=tL[:], initial=0.0,
                                                         op0=ALU.mult, op1=ALU.add),
                          reads=[tLB, cB], writes=[tBB])
                    if rev:
                        p.dve(TT(tL[:], tL[:], tB[:], ALU.subtract), reads=[tLB, tBB], writes=[tLB])
                        tot = tB[:].rearrange("p (c t) -> p c t", t=64)[:, :, 63:64].to_broadcast([128, 8, 64])
                        p.dve(TT(tL[:].rearrange("p (c t) -> p c t", t=64), tL[:].rearrange("p (c t) -> p c t", t=64),
                                 tot, ALU.add), reads=[tLB, tBB], writes=[tLB])
                        bsrc, bsrcB = tL, tLB
                    else:
                        bsrc, bsrcB = tB, tBB
                    p.act(ACT(tEb[:], bsrc[:], AF.Exp), reads=[bsrcB], writes=[tEbB])
                    p.act(ACT(tN[:], bsrc[:], AF.Exp, scale=-1.0), reads=[bsrcB], writes=[tNB])
                    dc_ = 0 if rev else 63
                    p.pool(CP(dsv[b][:, h, :], tEb[:].rearrange("p (c t) -> p c t", t=64)[:, :, dc_]),
                           reads=[tEbB], writes=[dsvB[b][h]])
                    p.dve(TT(qtT[b][:, h, :], qtT[b][:, h, :], tEb[:], ALU.mult), reads=[qtTB[b][h], tEbB], writes=[qtTB[b][h]])
                    p.dve(TT(ktT[b][:, h, :], tK[:], tN[:], ALU.mult), reads=[tKB, tNB], writes=[ktTB[b][h]])

                def front(it):
                    b = it % 2
                    j = order[it]
                    if not rev:
                        if it + 1 < NS:
                            xload(it + 1)
                        norm_transpose((junk, junkB, ss, ssB, rstd, rstdB, xs, xsB), xt[b], xtB[b], 4, hT, hTB, psT, psTB)
                        p.dma(DMA(hT_st[j], hT[:].rearrange("p k t -> p (k t)")), reads=[hTB], writes=[hTstB])
                        yield
                        for h in range(8):
                            ps, psB_ = next_ps()
                            proj_fm(ps, wr, cQ + h * 128, hT, 8, 0, 512, [wrB, hTB], [psB_])
                            s = h % 2
                            sigmoid_from_psum(ps, psB_, tE[s][:], tEB[s], tS[s][:], tSB[s])
                            p.dve(TT(qtT[b][:, h, :], ps, tS[s][:], ALU.mult), reads=[psB_, tSB[s]], writes=[qtTB[b][h]])
                            yield
                        p.dma(DMA(qT_st[j], qtT[b][:].rearrange("p k t -> p (k t)")), reads=qtTB[b], writes=[qTstB])
                        for t in range(4):
                            for hf in range(2):
                                ps, psB_ = next_ps()
                                proj_tm(ps, hT, t * 128, wr, cI + hf * 512, 512, 8, [wrB, hTB], [psB_])
                                if hf == 0:
                                    p.act(ACP(vtm[b][:, t, 0:512], ps), reads=[psB_], writes=[vtmB[b][t]])
                                else:
                                    p.dve(CP(vtm[b][:, t, 512:1024], ps), reads=[psB_], writes=[vtmB[b][t]])
                            yield
                        p.dma(DMA(v_st[j], vtm[b][:].rearrange("p t d -> p (t d)")), reads=vtmB[b], writes=[vstB])
                        for h in range(8):
                            gate_head(b, h)
                            yield
                    else:
                        p.dma(DMA(hT[:].rearrange("p k t -> p (k t)"), hT_st[j]), writes=[hTB])
                        p.dma(DMA(qtT[b][:].rearrange("p k t -> p (k t)"), qT_st[j]), writes=qtTB[b])
                        p.dma(DMA(vtm[b][:].rearrange("p t d -> p (t d)"), v_st[j]), writes=vtmB[b])
                        yield
                        for h in range(8):
                            gate_head(b, h)
                            yield

                def frontB(it):
                    if True:
                        for h in range(8):
                            ps, psB_ = next_ps()
                            proj_fm(ps, wr, cG + h * 128, hT, 8, 0, 512, [wrB, hTB], [psB_])
                            s = h % 2
                            sigmoid_from_psum(ps, psB_, tE[s][:], tEB[s], tS[s][:], tSB[s])
                            p.dve(TT(gT[:, h, :], ps, tS[s][:], ALU.mult), reads=[psB_, tSB[s]], writes=[gTB[h]])
                            yield
                        for h in range(8):
                            ps, psB_ = next_ps()
                            proj_fm(ps, wr, cGA + h * 128, hT, 8, 0, 512, [wrB, hTB], [psB_])
                            s = h % 2
                            sigmoid_from_psum(ps, psB_, tE[s][:], tEB[s], sga[:, h, :], sgaB[h])
                            yield

                def pump(gens, n):
                    for _ in range(n):
                        done = False
                        for g in gens:
                            try:
                                next(g)
                                done = True
                                break
                            except StopIteration:
                                continue
                        if not done:
                            return

                def drain(gen):
                    if gen is None:
                        return
                    for _ in gen:
                        pass

                state = {"dprev": [ones1[:, 0:1]] * 8, "dprevB": [cB] * 8, "oi": 0}

                def chain2(g1, g2):
                    if g1 is not None:
                        yield from g1
                    if g2 is not None:
                        yield from g2

                def emit_pending():
                    if state.get("pend") is None:
                        return
                    t_, o_ = state["pend"]
                    state["pend"] = None
                    tk_ = slice(t_ * 128, (t_ + 1) * 128)
                    for h in range(8):
                        p.pe(TR(psOT[:, h, :], on[o_][:, h, :], identb[:]), reads=[onB[o_], cB], writes=[psOTB])
                    p.act(ACP(AT[:, :, tk_], psOT[:]), reads=[psOTB], writes=[ATB[t_]])

                def back(it, genB, genA):
                    gen = [g for g in (genB, genA) if g is not None]
                    b = it % 2
                    j = order[it]
                    dprev, dprevB = state["dprev"], state["dprevB"]
                    tiles = [3, 2, 1, 0] if rev else [0, 1, 2, 3]
                    chunks = [1, 0] if rev else [0, 1]
                    for t in tiles:
                        tk = slice(t * 128, (t + 1) * 128)
                        if rev:
                            ofs = state["oi"] % 2
                            state["oi"] += 1
                            p.dma(DMA(oft[ofs][:], of_st[j][:, t * 1024:(t + 1) * 1024]), writes=[oftB[ofs]])
                        for half in range(2):
                            for h in range(half * 4, half * 4 + 4):
                                p.pe(TR(psK[:, h, :], ktT[b][:, h, tk], identb[:]), reads=[ktTB[b][h], cB], writes=[psKB[half]])
                            if half == 0:
                                p.act(ACP(ktm[:, 0:4, :], psK[:, 0:4, :]), reads=[psKB[0]], writes=[ktmB[0]])
                            else:
                                p.dve(CP(ktm[:, 4:8, :], psK[:, 4:8, :]), reads=[psKB[1]], writes=[ktmB[1]])
                        for c in range(2):
                            ck = slice(t * 128 + c * 64, t * 128 + c * 64 + 64)
                            for h in range(8):
                                p.pe(MM(psS[c * 64:(c + 1) * 64, h, :], ktT[b][:, h, ck], qtT[b][:, h, ck]),
                                     reads=[ktTB[b][h], qtTB[b][h]], writes=[psSB])
                        p.dve(TT(PT[:], psS[:], mask[:], ALU.mult), reads=[psSB, cB], writes=[PTB])
                        if rev:
                            emit_pending()
                        for c in chunks:
                            pr = slice(c * 64, (c + 1) * 64)
                            ck = slice(t * 128 + c * 64, t * 128 + c * 64 + 64)
                            cidx = t * 2 + c
                            for h in range(8):
                                vs = vtm[b][pr, t, h * 128:(h + 1) * 128]
                                p.pe(MM(psO[pr, h, :], qtT[b][:, h, ck], Sbf[:, h, :], True, False),
                                     reads=[qtTB[b][h], SbfB[h]], writes=[psOB])
                                p.pe(MM(psO[pr, h, :], PT[pr, h, :], vs, False, True),
                                     reads=[PTB, vtmB[b][t]], writes=[psOB])
                            for half in range(2):
                                for h in range(half * 4, half * 4 + 4):
                                    vs = vtm[b][pr, t, h * 128:(h + 1) * 128]
                                    p.pe(MM(psM[:, h, :], ktm[pr, h, :], vs), reads=[ktmB[half], vtmB[b][t]], writes=[psMB[h]])
                            for half in range(2):
                                for h in range(half * 4, half * 4 + 4):
                                    p.dve(STT(Sp[:, h, :], Sp[:, h, :], dprev[h], psM[:, h, :], ALU.mult, ALU.add),
                                          reads=[SpB[h], dprevB[h], psMB[h]], writes=[SpB[h]])
                                    dcur = dsv[b][:, h, cidx:cidx + 1]
                                    if h % 2 == 0:
                                        p.act(ACT(Sbf[:, h, :], Sp[:, h, :], AF.Copy, scale=dcur),
                                              reads=[SpB[h], dsvB[b][h]], writes=[SbfB[h]])
                                    else:
                                        p.pool(TS(Sbf[:, h, :], Sp[:, h, :], dcur, 0.0, ALU.mult, ALU.add),
                                               reads=[SpB[h], dsvB[b][h]], writes=[SbfB[h]])
                                    dprev[h] = dcur
                                    dprevB[h] = dsvB[b][h]
                            pump(gen, 3)
                        if not rev:
                            ob = t % 2
                            p.act(ACP(osb[ob][:, 0:512], psOf[:, 0:512]), reads=[psOB], writes=[osbB[ob]])
                            p.dve(CP(osb[ob][:, 512:1024], psOf[:, 512:1024]), reads=[psOB], writes=[osbB[ob]])
                            p.dma(DMA(of_st[j][:, t * 1024:(t + 1) * 1024], osb[ob][:]), reads=[osbB[ob]], writes=[ofstB])
                        else:
                            p.dve(TT(osum[:], psOf, oft[ofs][:], ALU.add), reads=[psOB, oftB[ofs]], writes=[osumB])
                            p.act(ACT(sq[:], osum[:], AF.Square), reads=[osumB], writes=[sqB])
                            p.dve(lambda e: e.tensor_reduce(out=ss8[:], in_=sq[:].rearrange("p (h v) -> p h v", v=128),
                                                            op=ALU.add, axis=AX.X), reads=[sqB], writes=[ss8B])
                            rms_rstd(ss8[:], ss8B, 8, 128, rs8[:], rs8B)
                            p.pool(TT(on[ofs][:], osum[:].rearrange("p (h v) -> p h v", v=128),
                                      rs8[:].unsqueeze(2).to_broadcast([128, 8, 128]), ALU.mult),
                                   reads=[osumB, rs8B], writes=[onB[ofs]])
                            state["pend"] = (t, ofs)
                    if rev:
                        emit_pending()
                    drain(genB)
                    for h in range(8):
                        p.pool(TS(Sp[:, h, :], Sp[:, h, :], dprev[h], 0.0, ALU.mult, ALU.add),
                               reads=[SpB[h], dprevB[h]], writes=[SpB[h]])
                        dprev[h] = ones1[:, 0:1]
                        dprevB[h] = cB
                    if rev:
                        for dc in range(8):
                            if dc % 2 == 0:
                                p.pool(TT(AT[:, dc, :], AT[:, dc, :], gT[:, dc, :], ALU.mult), reads=[ATB, gTB[dc]], writes=[ATB])
                            else:
                                p.dve(TT(AT[:, dc, :], AT[:, dc, :], gT[:, dc, :], ALU.mult), reads=[ATB, gTB[dc]], writes=[ATB])
                        for dc in range(8):
                            ps, psB_ = next_ps()
                            proj_fm(ps, wa, dc * 128, AT, 8, 0, 512, [waB, ATB], [psB_])
                            s = dc % 2
                            p.dve(TT(mATr[s][:], ps, sga[:, dc, :], ALU.mult), reads=[psB_, sgaB[dc]], writes=[mATrB[s]])
                            p.dma(DMA(mAT_st[j][:, dc * 512:(dc + 1) * 512], mATr[s][:]), reads=[mATrB[s]], writes=[mATstB])

                if not rev:
                    xload(0)
                drain(front(0))
                for it in range(NS):
                    genA = front(it + 1) if it + 1 < NS else None
                    genB = frontB(it) if rev else None
                    back(it, genB, genA)
                    drain(genA)
                p.flush()

        def sweep_c(l):
            x_src = x_in if l == 0 else x1
            with contextlib.ExitStack() as st:
                def T(name, shape, dt):
                    return st.enter_context(nc.sbuf_tensor(uniq(name), shape, dt))

                def PS(name, shape, dt):
                    return st.enter_context(nc.psum_tensor(uniq(name), shape, dt))

                gpre, gpreB = load_rowscale(st, "gpre", W["norm_mix_pre"][l], 8)
                wr = T("wr", [128, 8, 2048], BF16); wrB = p.buf("wr")
                wb = T("wb", [128, 4, 1024], BF16); wbB = p.buf("wb")
                wo = T("wo", [128, 8, 1024], BF16); woB = p.buf("wo")
                cU, cV, cGB = 0, 512, 1024
                with contextlib.ExitStack() as wst:
                    stage = make_stage(wst)
                    load_w(stage, W["w_in"][l], 8, OU, 512, wr, 0, wrB, gpre, gpreB)
                    load_w(stage, W["w_in"][l], 8, OV, 512, wr, 512, wrB, gpre, gpreB)
                    load_w(stage, W["w_in"][l], 8, OGB, 1024, wr, 1024, wrB, gpre, gpreB)
                    load_w(stage, W["w_b"][l], 4, 0, 1024, wb, 0, wbB)
                    load_w(stage, W["w_out"][l], 8, 0, 1024, wo, 0, woB)
                    p.flush()
                lng, lngB = load_bcast(st, "lng", W["sg_ln_g"][l], 512)
                lnb, lnbB = load_bcast(st, "lnb", W["sg_ln_b"][l], 512)
                gpo, gpoB = load_bcast(st, "gpo", W["norm_mix_post"][l], 1024)
                wsf = T("wsf", [128, 8, 128], F32); wsfB = p.buf("wsf")
                wsT = T("wsT", [128, 8, 128], BF16); wsTB = p.buf("wsT")
                bsb = T("bsb", [128, 4, 128], F32); bsbB = p.buf("bsb")
                psW = PS("psW", [128, 4, 128], F32); psWB = p.buf("psW")
                p.dma(DMA(wsf[:], W["sg_w"][l].rearrange("g t s -> t g s")), writes=[wsfB])
                for half in range(2):
                    for g in range(half * 4, half * 4 + 4):
                        p.pe(TR(psW[:, g % 4, :], wsf[:, g, :], identf[:]), reads=[wsfB, cB], writes=[psWB])
                    p.dve(CP(wsT[:, half * 4:half * 4 + 4, :], psW[:]), reads=[psWB], writes=[wsTB])
                for g in range(8):
                    p.dma(DMA(bsb[(g % 2) * 64:(g % 2) * 64 + 64, g // 2, :], W["sg_b"][l, g, :].partition_broadcast(64)),
                          writes=[bsbB])

                hT = T("hT", [128, 8, 512], BF16); hTB = p.buf("hT")
                mAT = T("mAT", [128, 8, 512], BF16); mATB = p.buf("mAT")
                xt = T("xt", [128, 4, D], F32); xtB = p.buf("xt")
                uT = T("uT", [128, 4, 512], BF16); uTB = p.bufs(4, "uT")
                gv = T("gv", [128, 512], F32); gvB = p.buf("gv")
                st6 = T("st6", [128, 6], F32); st6B = p.buf("st6")
                mv = T("mv", [128, 2], F32); mvB = p.buf("mv")
                rs = T("rs", [128, 1], F32); rsB = p.buf("rs")
                vh = T("vh", [128, 512], F32); vhB = p.buf("vh")
                vn = T("vn", [128, 512], BF16); vnB = p.buf("vn")
                tg = T("tg", [128, 4, 128], F32); tgB = p.buf("tg")
                BT = T("BT", [128, 4, 512], BF16); BTB = p.bufs(4, "BT")
                tE = [T(f"tE{i}", [128, 512], F32) for i in range(2)]; tEB = p.bufs(2, "tE")
                sgb = T("sgb", [128, 8, 512], BF16); sgbB = p.bufs(8, "sgb")
                tm = [T(f"tm{i}", [128, 512], F32) for i in range(2)]; tmB = p.bufs(2, "tm")
                mg = T("mg", [128, 8, 512], BF16); mgB = p.bufs(8, "mg")
                sq = T("sq", [128, 512], F32); sqB = p.buf("sq")
                ss = T("ss", [128, 2], F32); ssB = p.buf("ss")
                rstd = T("rstd", [128, 1], F32); rstdB = p.buf("rstd")
                tmp = T("tmp", [128, D], F32); tmpB = p.buf("tmp")
                yo = [T(f"yo{i}", [128, D], F32) for i in range(2)]; yoB = p.bufs(2, "yo")

                psP = [PS(f"psP{i}", [128, 512], F32) for i in range(3)]; psPB = p.bufs(3, "psP")
                psG = PS("psG", [128, 4, 128], F32); psGB = p.buf("psG")
                psX = [PS(f"psX{i}", [128, 512], F32) for i in range(2)]; psXB = p.bufs(2, "psX")
                pp = {"i": 0}

                def next_ps():
                    i = pp["i"] % 3
                    pp["i"] += 1
                    return psP[i][:], psPB[i]

                xmidstB = p.buf("xmidst")
                for j in range(NS):
                    p.dma(DMA(hT[:].rearrange("p k t -> p (k t)"), hT_st[j]), writes=[hTB])
                    p.dma(DMA(mAT[:].rearrange("p k t -> p (k t)"), mAT_st[j]), writes=[mATB])
                    p.dma(DMA(xt[:], x_src[j * 512:(j + 1) * 512, :].rearrange("(t p) d -> p t d", p=128)), writes=[xtB])
                    for c in range(4):
                        ps, psB_ = next_ps()
                        proj_fm(ps, wr, cU + c * 128, hT, 8, 0, 512, [wrB, hTB], [psB_])
                        p.act(ACT(uT[:, c, :], ps, AF.Gelu), reads=[psB_], writes=[uTB[c]])
                    for t in range(4):
                        tk = slice(t * 128, (t + 1) * 128)
                        ps, psB_ = next_ps()
                        proj_tm(ps, hT, t * 128, wr, cV, 512, 8, [wrB, hTB], [psB_])
                        p.act(ACT(gv[:], ps, AF.Gelu), reads=[psB_], writes=[gvB])
                        p.dve(lambda e: e.bn_stats(st6[:], gv[:]), reads=[gvB], writes=[st6B])
                        p.dve(lambda e: e.bn_aggr(mv[:], st6[:]), reads=[st6B], writes=[mvB])
                        p.dve(TS(rs[:], mv[:, 1:2], 1.0, EPS, ALU.mult, ALU.add), reads=[mvB], writes=[rsB])
                        p.pool(TT(rs[:], rs[:], mhalf[:, 0:1], ALU.pow), reads=[rsB, cB], writes=[rsB])
                        p.dve(TS(vh[:], gv[:], mv[:, 0:1], rs[:, 0:1], ALU.subtract, ALU.mult), reads=[gvB, mvB, rsB], writes=[vhB])
                        p.dve(TT(vh[:], vh[:], lng[:], ALU.mult), reads=[vhB, lngB], writes=[vhB])
                        p.dve(TT(vn[:], vh[:], lnb[:], ALU.add), reads=[vhB, lnbB], writes=[vnB])
                        for g in range(8):
                            p.pe(MM(psG[(g % 2) * 64:(g % 2) * 64 + 64, g // 2, :], vn[:, g * 64:(g + 1) * 64], wsT[:, g, :]),
                                 reads=[vnB, wsTB], writes=[psGB])
                        p.dve(TT(tg[:], psG[:], bsb[:], ALU.add), reads=[psGB, bsbB], writes=[tgB])
                        p.pool(TT(BT[:, :, tk], tg[:], uT[:, :, tk], ALU.mult), reads=[tgB, uTB], writes=[BTB[t]])
                    for dc in range(8):
                        ps, psB_ = next_ps()
                        proj_fm(ps, wr, cGB + dc * 128, hT, 8, 0, 512, [wrB, hTB], [psB_])
                        s = dc % 2
                        sigmoid_from_psum(ps, psB_, tE[s][:], tEB[s], sgb[:, dc, :], sgbB[dc])
                    for dc in range(8):
                        ps, psB_ = next_ps()
                        proj_fm(ps, wb, dc * 128, BT, 4, 0, 512, [wbB, BTB], [psB_])
                        s = dc % 2
                        p.dve(TT(tm[s][:], ps, sgb[:, dc, :], ALU.mult), reads=[psB_, sgbB[dc]], writes=[tmB[s]])
                        p.op("pool" if dc % 2 == 0 else "dve", TT(mg[:, dc, :], tm[s][:], mAT[:, dc, :], ALU.add),
                             reads=[tmB[s], mATB], writes=[mgB[dc]])
                    for t in range(4):
                        for hf in range(2):
                            proj_tm(psX[hf][:], mg, t * 128, wo, hf * 512, 512, 8, [woB, mgB], [psXB[hf]])
                        ob = t % 2
                        post_norm_residual([psX[0][:], psX[1][:]], psXB, sq, sqB, ss, ssB, rstd, rstdB, gpo, gpoB,
                                           xt[:, t, :], xtB, yo[ob], yoB[ob], tmp, tmpB)
                        p.dma(DMA(xmid[j * 512 + t * 128:j * 512 + (t + 1) * 128, :], yo[ob][:]), reads=[yoB[ob]],
                              writes=[xmidstB])
                p.flush()

        def sweep_d(l):
            TD = 256
            with contextlib.ExitStack() as st:
                def T(name, shape, dt):
                    return st.enter_context(nc.sbuf_tensor(uniq(name), shape, dt))

                def PS(name, shape, dt):
                    return st.enter_context(nc.psum_tensor(uniq(name), shape, dt))

                gpre, gpreB = load_rowscale(st, "gpre", W["norm_ffn_pre"][l], 8)
                wg = T("wg", [128, 8, FH], BF16); wgB = p.buf("wg")
                wu = T("wu", [128, 8, FH], BF16); wuB = p.buf("wu")
                wd = T("wd", [128, FC, D], BF16); wdB = p.buf("wd")
                with contextlib.ExitStack() as wst:
                    stage = make_stage(wst)
                    load_w(stage, W["w_gate"][l], 8, 0, FH, wg, 0, wgB, gpre, gpreB)
                    load_w(stage, W["w_up"][l], 8, 0, FH, wu, 0, wuB, gpre, gpreB)
                    load_w(stage, W["w_down"][l], FC, 0, D, wd, 0, wdB)
                    p.flush()
                gpo, gpoB = load_bcast(st, "gpo", W["norm_ffn_post"][l], 1024)

                xt = [T(f"xt{i}", [128, 2, D], F32) for i in range(2)]; xtB = p.bufs(2, "xt")
                junk = T("junk", [128, D], BF16); junkB = p.buf("junk")
                ss = T("ss", [128, 4], F32); ssB = p.buf("ss")
                rstd = T("rstd", [128, 4], F32); rstdB = p.buf("rstd")
                xs = T("xs", [128, 2, D], BF16); xsB = p.bufs(2, "xs")
                hT = T("hT", [128, 8, TD], BF16); hTB = p.buf("hT")
                sl = [T(f"sl{i}", [128, TD], F32) for i in range(2)]; slB = p.bufs(2, "sl")
                hid = T("hid", [128, FC, TD], BF16); hidB = p.bufs(FC, "hid")
                sq = T("sq", [128, 512], F32); sqB = p.buf("sq")
                ss2 = T("ss2", [128, 2], F32); ss2B = p.buf("ss2")
                rstd2 = T("rstd2", [128, 1], F32); rstd2B = p.buf("rstd2")
                tmp = T("tmp", [128, D], F32); tmpB = p.buf("tmp")
                yo = [T(f"yo{i}", [128, D], F32) for i in range(2)]; yoB = p.bufs(2, "yo")

                psTt = PS("psTt", [128, 2, 512], BF16); psT = [psTt[:, 0, :], psTt[:, 1, :]]; psTB = [p.buf("psT")] * 2
                psA = [PS(f"psA{i}", [128, 512], F32)[:, 0:TD] for i in range(2)]; psAB = p.bufs(2, "psA")
                psU = [PS(f"psU{i}", [128, 512], F32)[:, 0:TD] for i in range(2)]; psUB = p.bufs(2, "psU")
                psX = [PS(f"psX{i}", [128, 512], F32) for i in range(2)]; psXB = p.bufs(2, "psX")

                if l == 1:
                    djs = list(range(cfg.out_lo // TD, (cfg.out_lo + cfg.out_n) // TD))
                else:
                    djs = list(range(NT // TD))
                xastB = p.buf("xast")
                p.dma(DMA(xt[0][:], xmid[djs[0] * TD:(djs[0] + 1) * TD, :].rearrange("(t p) d -> p t d", p=128)), writes=[xtB[0]])
                for dit, j in enumerate(djs):
                    cur = dit % 2
                    if dit + 1 < len(djs):
                        jn = djs[dit + 1]
                        p.dma(DMA(xt[1 - cur][:], xmid[jn * TD:(jn + 1) * TD, :].rearrange("(t p) d -> p t d", p=128)),
                              writes=[xtB[1 - cur]])
                    norm_transpose((junk, junkB, ss, ssB, rstd, rstdB, xs, xsB), xt[cur], xtB[cur], 2, hT, hTB, psT, psTB)
                    for fc in range(FC):
                        s = fc % 2
                        proj_fm(psA[s][:], wg, fc * 128, hT, 8, 0, TD, [wgB, hTB], [psAB[s]])
                        proj_fm(psU[s][:], wu, fc * 128, hT, 8, 0, TD, [wuB, hTB], [psUB[s]])
                        p.act(ACT(sl[s][:], psA[s][:], AF.Silu), reads=[psAB[s]], writes=[slB[s]])
                        p.dve(TT(hid[:, fc, :], psU[s][:], sl[s][:], ALU.mult), reads=[psUB[s], slB[s]], writes=[hidB[fc]])
                    for t in range(2):
                        for hf in range(2):
                            for fc in range(FC):
                                p.pe(MM(psX[hf][:], hid[:, fc, t * 128:(t + 1) * 128], wd[:, fc, hf * 512:(hf + 1) * 512],
                                        fc == 0, fc == FC - 1), reads=[hidB[fc], wdB], writes=[psXB[hf]])
                        ob = t % 2
                        post_norm_residual([psX[0][:], psX[1][:]], psXB, sq, sqB, ss2, ss2B, rstd2, rstd2B, gpo, gpoB,
                                           xt[cur][:, t, :], xtB[cur], yo[ob], yoB[ob], tmp, tmpB)
                        p.dma(DMA(xa[j * TD + t * 128:j * TD + (t + 1) * 128, :], yo[ob][:]), reads=[yoB[ob]],
                              writes=[xastB])
                p.flush()

        def sweep_e(l, final):
            TD = 256
            with contextlib.ExitStack() as st:
                def T(name, shape, dt):
                    return st.enter_context(nc.sbuf_tensor(uniq(name), shape, dt))

                def PS(name, shape, dt):
                    return st.enter_context(nc.psum_tensor(uniq(name), shape, dt))

                wp = T("wp", [128, 2, D], BF16); wpB = p.buf("wp")
                wq = T("wq", [128, 8, D], BF16); wqB = p.buf("wq")
                with contextlib.ExitStack() as wst:
                    stage = make_stage(wst)
                    load_w(stage, W["w_ple"][l], 2, 0, D, wp, 0, wpB)
                    load_w(stage, W["w_ple_gate"][l], 8, 0, D, wq, 0, wqB)
                    p.flush()
                xt = [T(f"xt{i}", [128, 2, D], F32) for i in range(3)]; xtB = p.bufs(3, "xt")
                pt = [T(f"pt{i}", [128, 2, 256], F32) for i in range(3)]; ptB = p.bufs(3, "pt")
                xb = [T(f"xb{i}", [128, 2, D], BF16) for i in range(2)]; xbB = [p.bufs(2, f"xb{i}_") for i in range(2)]
                pb = [T(f"pb{i}", [128, 2, 256], BF16) for i in range(2)]; pbB = p.bufs(2, "pb")
                xT = [T(f"xT{i}", [128, 8, TD], BF16) for i in range(2)]; xTB = p.bufs(2, "xT")
                pT = [T(f"pT{i}", [128, 2, TD], BF16) for i in range(2)]; pTB = p.bufs(2, "pT")
                tE = [T(f"tE{i}", [128, 512], F32) for i in range(2)]; tEB = p.bufs(2, "tE")
                t2 = [T(f"t2{i}", [128, 512], F32) for i in range(2)]; t2B = p.bufs(2, "t2")
                yo = [T(f"yo{i}", [128, D], F32) for i in range(2)]; yoB = p.bufs(2, "yo")
                psT = [PS(f"psT{i}", [128, 1024], BF16)[:, 0:TD] for i in range(2)]; psTB = p.bufs(2, "psT")
                psG = [PS(f"psG{i}", [128, 512], F32) for i in range(2)]; psGB = p.bufs(2, "psG")
                psP = [PS(f"psP{i}", [128, 512], F32) for i in range(2)]; psPB = p.bufs(2, "psP")

                if final:
                    js = list(range(cfg.out_lo // TD, (cfg.out_lo + cfg.out_n) // TD))
                else:
                    js = list(range(NT // TD))
                nj = len(js)

                def ld(it):
                    if it >= nj:
                        return
                    j = js[it]
                    s = it % 3
                    p.dma(DMA(xt[s][:], xa[j * TD:(j + 1) * TD, :].rearrange("(t p) d -> p t d", p=128)), writes=[xtB[s]])
                    p.dma(DMA(pt[s][:], p_in[l, j * TD:(j + 1) * TD, :].rearrange("(t p) d -> p t d", p=128)), writes=[ptB[s]])

                def prep(it):
                    if it >= nj:
                        return
                    s3 = it % 3
                    b = it % 2
                    p.act(ACP(xb[b][:, 0, :], xt[s3][:, 0, :]), reads=[xtB[s3]], writes=[xbB[b][0]])
                    p.pool(CP(xb[b][:, 1, :], xt[s3][:, 1, :]), reads=[xtB[s3]], writes=[xbB[b][1]])
                    p.pool(CP(pb[b][:], pt[s3][:]), reads=[ptB[s3]], writes=[pbB[b]])
                    for kc in range(8):
                        s = kc % 2
                        for t in range(2):
                            p.pe(TR(psT[s][:, t * 128:(t + 1) * 128], xb[b][:, t, kc * 128:(kc + 1) * 128], identb[:]),
                                 reads=[xbB[b][t], cB], writes=[psTB[s]])
                        if s == 0:
                            p.dve(CP(xT[b][:, kc, :], psT[s][:, 0:TD]), reads=[psTB[s]], writes=[xTB[b]])
                        else:
                            p.act(ACP(xT[b][:, kc, :], psT[s][:, 0:TD]), reads=[psTB[s]], writes=[xTB[b]])
                    for kc in range(2):
                        s = kc % 2
                        for t in range(2):
                            p.pe(TR(psT[s][:, t * 128:(t + 1) * 128], pb[b][:, t, kc * 128:(kc + 1) * 128], identb[:]),
                                 reads=[pbB[b], cB], writes=[psTB[s]])
                        p.dve(CP(pT[b][:, kc, :], psT[s][:, 0:TD]), reads=[psTB[s]], writes=[pTB[b]])

                x1stB = p.buf("x1st")

                def compute(it):
                    j = js[it]
                    s3 = it % 3
                    b = it % 2
                    for t in range(2):
                        ob = t % 2
                        for hf in range(2):
                            sl = slice(hf * 512, (hf + 1) * 512)
                            proj_tm(psG[hf][:], xT[b], t * 128, wq, hf * 512, 512, 8, [wqB, xTB[b]], [psGB[hf]])
                            proj_tm(psP[hf][:], pT[b], t * 128, wp, hf * 512, 512, 2, [wpB, pTB[b]], [psPB[hf]])
                            sigmoid_from_psum(psG[hf][:], psGB[hf], tE[hf][:], tEB[hf], tE[hf][:], tEB[hf])
                            p.dve(TT(t2[hf][:], psP[hf][:], tE[hf][:], ALU.mult), reads=[psPB[hf], tEB[hf]], writes=[t2B[hf]])
                            p.pool(TT(yo[ob][:, sl], t2[hf][:], xt[s3][:, t, sl], ALU.add), reads=[t2B[hf], xtB[s3]],
                                   writes=[yoB[ob]])
                        r0 = j * TD + t * 128
                        if final:
                            dst = out[r0 - cfg.out_lo:r0 - cfg.out_lo + 128, :]
                        else:
                            dst = x1[r0:r0 + 128, :]
                        p.dma(DMA(dst, yo[ob][:]), reads=[yoB[ob]], writes=[x1stB])

                ld(0)
                ld(1)
                prep(0)
                for it in range(nj):
                    ld(it + 2)
                    prep(it + 1)
                    compute(it)
                p.flush()

        steps = []
        for l in range(2):
            steps += [lambda l=l: sweep_hgrn(l, False), lambda l=l: sweep_hgrn(l, True), lambda l=l: sweep_c(l),
                      lambda l=l: sweep_d(l), lambda l=l: sweep_e(l, final=(l == 1))]
        if cfg.stop_after is not None:
            steps = steps[:cfg.stop_after] if isinstance(cfg.stop_after, int) else [steps[i] for i in cfg.stop_after]
        for f in steps:
            f()
        ninst = p.ninst
    return nc, ninst


SEG = 4096
HALO = 256
_WNAMES = ["norm_mix_pre", "w_in", "lb_gamma_fwd", "lb_gamma_bwd", "hg_norm", "sg_w", "sg_b", "sg_ln_g",
           "sg_ln_b", "w_a", "w_b", "w_out", "norm_mix_post", "norm_ffn_pre", "w_gate", "w_up", "w_down",
           "norm_ffn_post", "w_ple", "w_ple_gate"]
_CACHE = {}


def kernel(**inputs):
    x = np.asarray(inputs["x"], dtype=np.float32)
    pp = np.asarray(inputs["p"], dtype=np.float32)
    Bn, S, _ = x.shape
    nseg = S // SEG
    ncores = Bn * nseg
    NT = SEG + 2 * HALO
    key = (NT,)
    if key not in _CACHE:
        _CACHE[key] = build(Cfg(NT, HALO, SEG))[0]
    nc = _CACHE[key]
    wts = {k: np.ascontiguousarray(np.asarray(inputs[k], dtype=np.float32)) for k in _WNAMES}
    in_maps = []
    for b in range(Bn):
        for sgm in range(nseg):
            lo = sgm * SEG - HALO
            hi = (sgm + 1) * SEG + HALO
            xs = np.zeros((NT, D), np.float32)
            ps = np.zeros((2, NT, 256), np.float32)
            a, bnd = max(lo, 0), min(hi, S)
            xs[a - lo:bnd - lo] = x[b, a:bnd]
            ps[:, a - lo:bnd - lo] = pp[:, b, a:bnd]
            m = {"x": xs, "p": ps}
            m.update(wts)
            in_maps.append(m)
    res = run_bass_kernel_spmd(nc, in_maps, core_ids=list(range(ncores)))
    out = np.empty((Bn, S, D), np.float32)
    i = 0
    for b in range(Bn):
        for sgm in range(nseg):
            out[b, sgm * SEG:(sgm + 1) * SEG] = res.results[i]["out"]
            i += 1
    return out
```
